# Optimizing a Trainium2 kernel written in Bass

```python
import jax, jax.numpy as jnp
from jax import lax
import numpy as np

D_MODEL = 1024
BATCH = 32
SEQ = 256
DEPTH = 4
DEC_BATCH = 4
DEC_SEQ = 2048
PAST_LEN = 256

GRID_W = 64
D_MIX = D_MODEL
WA = D_MIX // 2
HEAD_A = 64
HA = WA // HEAD_A
WB = D_MIX - WA
HB = 8
BW = WB // HB
R_W = 64
R_A = 64
R_G = 128
D_FF = 2816
CONV_B = 4
CONV_B_LEFT = 2
CONV_F = 3
CONV_F_LEFT = 1
LRU_C = 8.0
N_DIR = 2
EPS = 1e-6
LNX_EPS = 64e-5
KK_EPS = 1e-12
DECAY_SCALE = 0.6065306597126334
P_IN = 3 * WA + R_W + R_A + R_G + 2 * WB
SPLITS = (WA, 2 * WA, 3 * WA, 3 * WA + R_W, 3 * WA + R_W + R_A, 3 * WA + R_W + R_A + R_G,
          3 * WA + R_W + R_A + R_G + WB)

kernel_name = 'hybrid_rwkv7_rglru_prefix_diffusion_step'


def rmsnorm(x, g):
    xf = x.astype(jnp.float32)
    y = xf * lax.rsqrt(jnp.mean(xf * xf, axis=-1, keepdims=True) + EPS)
    return (y * g.astype(jnp.float32)).astype(x.dtype)


def dwconv(x, w, b, left, grid):
    bsz, L, C = x.shape
    K = w.shape[0]
    if grid:
        rows = L // GRID_W
        x = x.reshape(bsz * rows, GRID_W, C)
    n = x.shape[1]
    xp = jnp.pad(x, ((0, 0), (left, K - 1 - left), (0, 0)))
    y = b + xp[:, 0:n, :] * w[0]
    for j in range(1, K):
        y = y + xp[:, j:j + n, :] * w[j]
    return y.reshape(bsz, L, C)


def _heads(t):
    return t.reshape(t.shape[:-1] + (HA, HEAD_A))


def dir_time_major(x2):
    x2 = jnp.stack([x2[0], jnp.flip(x2[1], axis=1)])
    return jnp.moveaxis(x2, 2, 0)


def merge_dirs(y):
    y = jnp.moveaxis(y, 0, 2)
    return y[0] + jnp.flip(y[1], axis=1)


def _wkv7_step(S, inp):
    r_t, w_t, kk_t, b_t, k_t, v_t = inp
    sa = jnp.einsum('dbhij,dbhj->dbhi', S, kk_t)
    S = S * w_t[..., None, :] - sa[..., :, None] * b_t[..., None, :] + v_t[..., :, None] * k_t[..., None, :]
    y = jnp.einsum('dbhij,dbhj->dbhi', S, r_t)
    return S, y


def rwkv7_mix(r, k, v, xw, xa, xg, s0, w0, w_up, a0, a_up, g_up, k_k, k_a, r_k, lnx_g, lnx_b):
    bsz, T, _ = r.shape
    dt = r.dtype
    f32 = jnp.float32
    r, k, v = r.astype(f32), k.astype(f32), v.astype(f32)
    logit_w = w0[:, None, None, :] + jnp.einsum('btr,drc->dbtc', jnp.tanh(xw), w_up)
    decay = jnp.exp(-DECAY_SCALE * jax.nn.sigmoid(logit_w.astype(f32)))
    a = jax.nn.sigmoid((a0[:, None, None, :] + jnp.einsum('btr,drc->dbtc', xa, a_up)).astype(f32))
    g = jax.nn.sigmoid(xg) @ g_up
    kk = _heads(k * k_k)
    kk = kk * lax.rsqrt(jnp.sum(kk * kk, axis=-1, keepdims=True) + KK_EPS)
    k_dir = k[None] * (1.0 + (a - 1.0) * k_a)
    rh, vh, kdh = _heads(r), _heads(v), _heads(k_dir)
    both = lambda t: jnp.broadcast_to(t, (N_DIR,) + t.shape)
    xs = (dir_time_major(both(rh)), dir_time_major(_heads(decay)), dir_time_major(both(kk)),
          dir_time_major(both(kk) * _heads(a)), dir_time_major(kdh), dir_time_major(both(vh)))
    s_fin, y = lax.scan(_wkv7_step, s0.astype(f32), xs)
    o = merge_dirs(y)
    mu = jnp.mean(o, axis=-1, keepdims=True)
    var = jnp.mean(jnp.square(o - mu), axis=-1, keepdims=True)
    o = ((o - mu) * lax.rsqrt(var + LNX_EPS)).reshape(bsz, T, WA) * lnx_g + lnx_b
    bonus = jnp.sum(rh[None] * kdh * r_k, axis=(0, -1))[..., None] * vh
    out = (o + bonus.reshape(bsz, T, WA)) * g
    return out.astype(dt), s_fin


def _lin_combine(e1, e2):
    a1, b1 = e1
    a2, b2 = e2
    return a1 * a2, a2 * b1 + b2


def rglru_mix(xb, gb, h0, conv_w, conv_b, ga_w, ga_b, gx_w, gx_b, lam, grid):
    bsz, T, _ = xb.shape
    f32 = jnp.float32
    xc = dwconv(xb, conv_w, conv_b, CONV_B_LEFT, grid)
    xh = xc.reshape(bsz, T, HB, BW)
    rg = jax.nn.sigmoid((jnp.einsum('bthi,dhij->dbthj', xh, ga_w).reshape(N_DIR, bsz, T, WB)
                         + ga_b[:, None, None, :]).astype(f32))
    ig = jax.nn.sigmoid((jnp.einsum('bthi,dhij->dbthj', xh, gx_w).reshape(N_DIR, bsz, T, WB)
                         + gx_b[:, None, None, :]).astype(f32))
    log_a = -LRU_C * rg * jax.nn.softplus(-lam.astype(f32))[:, None, None, :]
    a = jnp.exp(log_a)
    u = jnp.sqrt(-jnp.expm1(2.0 * log_a)) * (ig * xc.astype(f32)[None])
    a_t = dir_time_major(a)
    u_t = dir_time_major(u)
    u_t = u_t.at[0].add(a_t[0] * h0.astype(f32))
    _, h = lax.associative_scan(_lin_combine, (a_t, u_t), axis=0)
    y = merge_dirs(h).astype(xb.dtype) * jax.nn.gelu(gb)
    return y, h[-1]


def block(x, cvec, s0, h0, grid, lp):
    mod = (jax.nn.silu(cvec) @ lp['w_mod'] + lp['b_mod'])[..., None, :]
    sh1, sc1, g1, sh2, sc2, g2 = jnp.split(mod, 6, axis=-1)
    h = rmsnorm(x, lp['ln1_g']) * (1.0 + sc1) + sh1
    proj = h @ lp['w_in']
    r, k, v, xw, xa, xg, xb, gb = jnp.split(proj, SPLITS, axis=-1)
    ya, s_fin = rwkv7_mix(r, k, v, xw, xa, xg, s0, lp['rw_w0'], lp['rw_w_up'], lp['rw_a0'], lp['rw_a_up'],
                          lp['rw_g_up'], lp['rw_k_k'], lp['rw_k_a'], lp['rw_r_k'], lp['rw_lnx_g'], lp['rw_lnx_b'])
    yb, h_fin = rglru_mix(xb, gb, h0, lp['lru_conv_w'], lp['lru_conv_b'], lp['lru_ga_w'], lp['lru_ga_b'],
                          lp['lru_gx_w'], lp['lru_gx_b'], lp['lru_lam'], grid)
    x = x + g1 * (jnp.concatenate([ya, yb], axis=-1) @ lp['w_out'])
    h = rmsnorm(x, lp['ln2_g']) * (1.0 + sc2) + sh2
    u = dwconv(h @ lp['w_up'], lp['ffn_conv_w'], lp['ffn_conv_b'], CONV_F_LEFT, grid)
    ua, ub = jnp.split(u, 2, axis=-1)
    x = x + g2 * ((jax.nn.gelu(ua) * ub) @ lp['w_down'])
    return x, s_fin, h_fin


def setup_inputs(seed: int = 0) -> dict:
    key = jax.random.key(seed)
    ks = iter(jax.random.split(key, 40))
    f32 = jnp.float32
    nrm = lambda shape, s: jax.random.normal(next(ks), shape, f32) * s
    L, D = DEPTH, D_MODEL
    x_prompt = nrm((BATCH, SEQ, D), 1.0)
    x_sample = nrm((DEC_BATCH, DEC_SEQ, D), 1.0)
    state_rwkv = nrm((DEC_BATCH, DEPTH, N_DIR, HA, HEAD_A, HEAD_A), 1.0)
    state_lru = nrm((DEC_BATCH, DEPTH, N_DIR, WB), 1.0)
    c = nrm((DEC_BATCH, D), 1.0)
    c_ctx = nrm((D,), 1.0)
    ln1_g = 1.0 + nrm((L, D), 0.02)
    w_mod = nrm((L, D, 6 * D), 0.5 * D ** -0.5)
    b_mod = nrm((L, 6 * D), 0.02)
    w_in = nrm((L, D, P_IN), D ** -0.5)
    rw_w0 = jax.random.uniform(next(ks), (L, N_DIR, WA), f32, -4.0, 2.0)
    rw_w_up = nrm((L, N_DIR, R_W, WA), 0.5 * R_W ** -0.5)
    rw_a0 = nrm((L, N_DIR, WA), 0.5)
    rw_a_up = nrm((L, N_DIR, R_A, WA), R_A ** -0.5)
    rw_g_up = nrm((L, R_G, WA), R_G ** -0.5)
    rw_k_k = 0.85 + nrm((L, WA), 0.1)
    rw_k_a = 1.0 + nrm((L, WA), 0.1)
    rw_r_k = nrm((L, HA, HEAD_A), 0.1)
    rw_lnx_g = 1.0 + nrm((L, WA), 0.02)
    rw_lnx_b = nrm((L, WA), 0.02)
    lru_conv_w = nrm((L, CONV_B, WB), CONV_B ** -0.5)
    lru_conv_b = nrm((L, WB), 0.02)
    lru_ga_w = nrm((L, N_DIR, HB, BW, BW), BW ** -0.5)
    lru_ga_b = nrm((L, N_DIR, WB), 0.02)
    lru_gx_w = nrm((L, N_DIR, HB, BW, BW), BW ** -0.5)
    lru_gx_b = nrm((L, N_DIR, WB), 0.02)
    a_pow = jax.random.uniform(next(ks), (L, N_DIR, WB), f32, 0.9, 0.999)
    p = a_pow ** (1.0 / LRU_C)
    lru_lam = jnp.log(p) - jnp.log1p(-p)
    w_out = nrm((L, D_MIX, D), D_MIX ** -0.5)
    ln2_g = 1.0 + nrm((L, D), 0.02)
    w_up = nrm((L, D, 2 * D_FF), D ** -0.5)
    ffn_conv_w = nrm((L, CONV_F, 2 * D_FF), CONV_F ** -0.5)
    ffn_conv_b = nrm((L, 2 * D_FF), 0.02)
    w_down = nrm((L, D_FF, D), D_FF ** -0.5)
    lnf_g = 1.0 + nrm((D,), 0.02)
    return {'x_prompt': x_prompt, 'x_sample': x_sample, 'state_rwkv': state_rwkv, 'state_lru': state_lru,
            'c': c, 'c_ctx': c_ctx, 'ln1_g': ln1_g, 'w_mod': w_mod, 'b_mod': b_mod, 'w_in': w_in,
            'rw_w0': rw_w0, 'rw_w_up': rw_w_up, 'rw_a0': rw_a0, 'rw_a_up': rw_a_up, 'rw_g_up': rw_g_up,
            'rw_k_k': rw_k_k, 'rw_k_a': rw_k_a, 'rw_r_k': rw_r_k, 'rw_lnx_g': rw_lnx_g, 'rw_lnx_b': rw_lnx_b,
            'lru_conv_w': lru_conv_w, 'lru_conv_b': lru_conv_b, 'lru_ga_w': lru_ga_w, 'lru_ga_b': lru_ga_b,
            'lru_gx_w': lru_gx_w, 'lru_gx_b': lru_gx_b, 'lru_lam': lru_lam, 'w_out': w_out, 'ln2_g': ln2_g,
            'w_up': w_up, 'ffn_conv_w': ffn_conv_w, 'ffn_conv_b': ffn_conv_b, 'w_down': w_down, 'lnf_g': lnf_g}


def reference(x_prompt, x_sample, state_rwkv, state_lru, c, c_ctx, ln1_g, w_mod, b_mod, w_in,
              rw_w0, rw_w_up, rw_a0, rw_a_up, rw_g_up, rw_k_k, rw_k_a, rw_r_k, rw_lnx_g, rw_lnx_b,
              lru_conv_w, lru_conv_b, lru_ga_w, lru_ga_b, lru_gx_w, lru_gx_b, lru_lam, w_out, ln2_g,
              w_up, ffn_conv_w, ffn_conv_b, w_down, lnf_g):
    bp = x_prompt.shape[0]
    yp = x_prompt
    ys = x_sample
    new_rwkv = []
    new_lru = []
    for l in range(DEPTH):
        lp = {'w_mod': w_mod[l], 'b_mod': b_mod[l], 'ln1_g': ln1_g[l], 'w_in': w_in[l],
              'rw_w0': rw_w0[l], 'rw_w_up': rw_w_up[l], 'rw_a0': rw_a0[l], 'rw_a_up': rw_a_up[l],
              'rw_g_up': rw_g_up[l], 'rw_k_k': rw_k_k[l], 'rw_k_a': rw_k_a[l], 'rw_r_k': rw_r_k[l],
              'rw_lnx_g': rw_lnx_g[l], 'rw_lnx_b': rw_lnx_b[l],
              'lru_conv_w': lru_conv_w[l], 'lru_conv_b': lru_conv_b[l], 'lru_ga_w': lru_ga_w[l],
              'lru_ga_b': lru_ga_b[l], 'lru_gx_w': lru_gx_w[l], 'lru_gx_b': lru_gx_b[l], 'lru_lam': lru_lam[l],
              'w_out': w_out[l], 'ln2_g': ln2_g[l], 'w_up': w_up[l], 'ffn_conv_w': ffn_conv_w[l],
              'ffn_conv_b': ffn_conv_b[l], 'w_down': w_down[l]}
        s0 = jnp.zeros((N_DIR, bp, HA, HEAD_A, HEAD_A), jnp.float32)
        h0 = jnp.zeros((N_DIR, bp, WB), jnp.float32)
        yp, s_fin, h_fin = block(yp, c_ctx, s0, h0, False, lp)
        new_rwkv.append(jnp.moveaxis(s_fin, 0, 1))
        new_lru.append(jnp.moveaxis(h_fin, 0, 1))
        ys, _, _ = block(ys, c, jnp.moveaxis(state_rwkv[:, l], 1, 0), jnp.moveaxis(state_lru[:, l], 1, 0), True, lp)
    y_prompt = rmsnorm(yp, lnf_g)
    y_sample = rmsnorm(ys, lnf_g)
    new_state_rwkv = jnp.stack(new_rwkv, axis=1).astype(x_prompt.dtype)
    new_state_lru = jnp.stack(new_lru, axis=1).astype(x_prompt.dtype)
    return (y_prompt, y_sample, new_state_rwkv, new_state_lru)
```

```python
import numpy as np
from contextlib import ExitStack
import concourse.bass as bass
import concourse.mybir as mybir
from concourse.bass_utils import run_bass_kernel_spmd

F32 = mybir.dt.float32
BF16 = mybir.dt.bfloat16
AF = mybir.ActivationFunctionType
ALU = mybir.AluOpType
ENGS = ["tensor", "vector", "scalar", "gpsimd", "sync"]

D = 1024; WA = 512; PIN = 2816; DFF = 2816; C = 64
EPS = 1e-6; LNX_EPS = 64e-5; KK_EPS = 1e-12; DSC = 0.6065306597126334

def _cols():
    o = {}; n = 0
    def add(name, k):
        nonlocal n
        o[name] = n; n += k
    add("ln1g", 8); add("ln2g", 8); add("bmod", 48)
    add("w0", 8); add("a0", 8); add("k_k", 4); add("k_a", 4); add("r_k", 4); add("lnxg", 4); add("lnxb", 4)
    add("cw", 16); add("cb", 4); add("gab", 8); add("gxb", 8); add("lam", 8)
    add("fw", 132); add("fb", 44); add("lnf", 8)
    return o, n
PC, NPC = _cols()


class Reg:
    __slots__ = ("w", "r", "parents")
    def __init__(self, parents=None):
        self.w = None; self.r = {}; self.parents = parents

    def resolve(self):
        if self.parents:
            ps = self.parents; self.parents = None
            for p in ps:
                p.resolve()
                if p.w is not None and self.r.get(p.w[0], 0) < p.w[1]: self.r[p.w[0]] = p.w[1]
                for k, v in p.r.items():
                    if self.r.get(k, 0) < v: self.r[k] = v


def _flat(regs):
    out = []
    for r in regs:
        if isinstance(r, (list, tuple)): out.extend(_flat(r))
        else: out.append(r)
    return out


class Sched:
    ROT = 12000
    def __init__(self, nc, es, n_dma_sems=16):
        self.nc = nc; self.es = es
        self.prog = {e: [] for e in ENGS}
        self.cnt = {e: 0 for e in ENGS}
        self.epoch = {e: 0 for e in ENGS}
        self.waited = {e: {} for e in ENGS}
        self.sems = {}
        self.n_dma = n_dma_sems
        self.dma_cnt = {}
        self.dma_next = {"sync": 0, "gpsimd": 0, "scalar": 0}
        for e in ENGS:
            self.sems[(e, 0)] = es.enter_context(nc.semaphore("s_%s0" % e))
        for q in ("sync", "gpsimd"):
            for i in range(n_dma_sems):
                self.sems[("dma", q, i)] = es.enter_context(nc.semaphore("s_dma_%s%d" % (q, i)))
                self.dma_cnt[("dma", q, i)] = 0
        self.out_tokens = []
        self.last_tok = {}
        self.rec = None
        self.t_eng = {e: 0.0 for e in ENGS}
        self.t_w = {}; self.t_r = {}
        self.marks = []; self.busy = {e: 0.0 for e in ENGS}

    def _deps(self, reads, writes):
        for r in reads: r.resolve()
        for w in writes: w.resolve()
        deps = {}
        def add(tok):
            if tok is None: return
            k, v = tok
            if deps.get(k, 0) < v: deps[k] = v
        for r in reads: add(r.w)
        for w in writes:
            add(w.w)
            for t in w.r.items(): add(t)
        return deps

    def _emit_waits(self, eng, deps):
        for k, v in deps.items():
            if self.waited[eng].get(k, 0) >= v: continue
            self.waited[eng][k] = v
            sem = self.sems[k]
            self.prog[eng].append(lambda e, sem=sem, v=v: e.wait_ge(sem, v))

    def _mark(self, tok, reads, writes):
        for r in reads:
            if r.r.get(tok[0], 0) < tok[1]: r.r[tok[0]] = tok[1]
        for w in writes:
            w.w = tok; w.r = {}

    def _next_tok(self, eng):
        if self.cnt[eng] >= self.ROT:
            self.epoch[eng] += 1; self.cnt[eng] = 0
            self.sems[(eng, self.epoch[eng])] = self.es.enter_context(self.nc.semaphore("s_%s%d" % (eng, self.epoch[eng])))
        self.cnt[eng] += 1
        key = (eng, self.epoch[eng])
        self.last_tok[eng] = (key, self.cnt[eng])
        return (key, self.cnt[eng])

    def mark(self, name):
        self.marks.append((name, max(self.t_eng.values()), dict(self.busy)))

    def record(self, body):
        assert self.rec is None
        self.rec = []
        body()
        r = self.rec; self.rec = None
        return r

    def _est(self, kind, args):
        if kind == "op":
            eng, fn, reads, writes, cost = args
            return eng, reads, writes, cost, cost
        if kind == "mm_group":
            fns, reads, writes, cost = args
            return "tensor", reads, writes, cost, cost
        eng, o, a, reads, writes, is_out, cost = args
        return eng, reads, writes, cost, 0.15

    def _start_time(self, est):
        eng, reads, writes, cost, occ = est
        t = 0.0
        for r in reads:
            t = max(t, self.t_w.get(id(r), 0.0))
        for w in writes:
            t = max(t, self.t_w.get(id(w), 0.0), self.t_r.get(id(w), 0.0))
        return max(t + 0.12, self.t_eng[eng]), t

    def _account(self, est):
        eng, reads, writes, cost, occ = est
        st, _ = self._start_time(est)
        end = st + cost
        self.busy[eng] += occ
        self.t_eng[eng] = st + occ
        for r in reads:
            if self.t_r.get(id(r), 0.0) < end: self.t_r[id(r)] = end
        for w in writes:
            self.t_w[id(w)] = end

    def emit_merged(self, recs, weights=None):
        import os
        eps = float(os.environ.get("KEPS", "0.2"))
        idx = [0] * len(recs)
        rem = [sum(self._est(k, a)[3] for k, a in rl) for rl in recs]
        while True:
            cands = []
            for si, rl in enumerate(recs):
                if idx[si] >= len(rl): continue
                kind, args = rl[idx[si]]
                st, rdy = self._start_time(self._est(kind, args))
                cands.append((st, rdy, si))
            if not cands: break
            mn = min(c[0] for c in cands)
            if eps > 0:
                pool = [c for c in cands if c[0] <= mn + eps]
                si = max(pool, key=lambda c: (rem[c[2]], -c[2]))[2]
            else:
                si = min(cands)[2]
            kind, args = recs[si][idx[si]]; idx[si] += 1
            rem[si] -= self._est(kind, args)[3]
            getattr(self, kind)(*args)

    def op(self, eng, fn, reads=(), writes=(), cost=0.5):
        reads = _flat(reads); writes = _flat(writes)
        if self.rec is not None:
            self.rec.append(("op", (eng, fn, reads, writes, cost))); return
        self._account((eng, reads, writes, cost, cost))
        deps = self._deps(reads, writes)
        self._emit_waits(eng, deps)
        tok = self._next_tok(eng)
        sem = self.sems[tok[0]]
        self.prog[eng].append(lambda e, fn=fn, sem=sem: fn(e).then_inc(sem, 1))
        self._mark(tok, reads, writes)

    def mm_group(self, fns, reads=(), writes=(), cost=0.5):
        reads = _flat(reads); writes = _flat(writes)
        if self.rec is not None:
            self.rec.append(("mm_group", (fns, reads, writes, cost))); return
        self._account(("tensor", reads, writes, cost, cost))
        deps = self._deps(reads, writes)
        self._emit_waits("tensor", deps)
        tok = self._next_tok("tensor")
        sem = self.sems[tok[0]]
        for fn in fns[:-1]:
            self.prog["tensor"].append(lambda e, fn=fn: fn(e))
        fn = fns[-1]
        self.prog["tensor"].append(lambda e, fn=fn, sem=sem: fn(e).then_inc(sem, 1))
        self._mark(tok, reads, writes)

    def dma(self, eng, out_ap, in_ap, reads=(), writes=(), is_output=False, cost=3.0):
        reads = _flat(reads); writes = _flat(writes)
        if self.rec is not None:
            self.rec.append(("dma", (eng, out_ap, in_ap, reads, writes, is_output, cost))); return
        self._account((eng, reads, writes, cost, 0.15))
        i = self.dma_next[eng]
        self.dma_next[eng] = (i + 1) % self.n_dma
        key = ("dma", eng, i)
        deps = self._deps(reads, writes)
        if self.dma_cnt[key] > 0 and deps.get(key, 0) < self.dma_cnt[key]:
            deps[key] = self.dma_cnt[key]
        self._emit_waits(eng, deps)
        self.dma_cnt[key] += 16
        tok = (key, self.dma_cnt[key])
        sem = self.sems[key]
        self.prog[eng].append(lambda e, o=out_ap, a=in_ap, sem=sem: e.dma_start(out=o, in_=a).then_inc(sem, 16))
        self._mark(tok, reads, writes)
        if is_output: self.out_tokens.append(tok)

    def barrier(self):
        deps = {}
        for e in ENGS:
            if e in self.last_tok:
                k, v = self.last_tok[e]; deps[k] = v
        for key, v in self.dma_cnt.items():
            if v > 0: deps[key] = v
        for e in ENGS:
            self._emit_waits(e, dict(deps))

    def finish(self):
        deps = {}
        for k, v in self.out_tokens:
            deps[k] = max(deps.get(k, 0), v)
        for e in ENGS:
            if e in self.last_tok:
                k, v = self.last_tok[e]; deps[k] = max(deps.get(k, 0), v)
        self._emit_waits("sync", deps)
        with self.nc.Block() as block:
            @block.tensor
            def _(e):
                for f in self.prog["tensor"]: f(e)
            @block.vector
            def _(e):
                for f in self.prog["vector"]: f(e)
            @block.scalar
            def _(e):
                for f in self.prog["scalar"]: f(e)
            @block.gpsimd
            def _(e):
                for f in self.prog["gpsimd"]: f(e)
            @block.sync
            def _(e):
                for f in self.prog["sync"]: f(e)


class Buf:
    __slots__ = ("ap", "reg")
    def __init__(self, ap, reg=None):
        self.ap = ap; self.reg = reg if reg is not None else Reg()
    def __getitem__(self, idx):
        return Buf(self.ap[idx], self.reg)
    def v(self, ap):
        return Buf(ap, self.reg)


class Arena:
    def __init__(self, tf, tb, cap_f32):
        self.tf = tf; self.tb = tb; self.cap = cap_f32; self.off = 0; self.hw = 0
        self.ents = []
    def _reg(self, o, e):
        par = [r for (s_, e_, r) in self.ents if s_ < e and o < e_]
        self.ents = [(s_, e_, r) for (s_, e_, r) in self.ents if not (o <= s_ and e_ <= e)]
        reg = Reg(parents=par if par else None)
        self.ents.append((o, e, reg))
        return reg
    def f(self, n):
        o = self.off; self.off += n
        assert self.off <= self.cap, ("arena overflow", self.off, self.cap)
        self.hw = max(self.hw, self.off)
        return Buf(self.tf[:, o:o + n], self._reg(o, o + n))
    def b(self, n):
        o = self.off; self.off += (n + 1) // 2
        assert self.off <= self.cap, ("arena overflow", self.off, self.cap)
        self.hw = max(self.hw, self.off)
        return Buf(self.tb[:, 2 * o:2 * o + n], self._reg(o, self.off))
    def mark(self):
        return self.off
    def release(self, m):
        self.off = m


import os
OPT = set(os.environ.get("KOPT", "qact").split(","))
def build(T, NL, SEG=256, ARENA_F32=17408):
    NT = T // 512; NCH = T // C; NSEG = T // SEG; CPS = SEG // C; R64 = T // 64; NSQ = T // 256
    assert SEG == 256
    nc = bass.Bass("TRN2", target_bir_lowering=False)
    dt_in = lambda n, s: nc.dram_tensor(n, s, F32, kind="ExternalInput").ap()
    dt_out = lambda n, s: nc.dram_tensor(n, s, F32, kind="ExternalOutput").ap()
    xT = dt_in("xT", [8, 128, T]); cv = dt_in("cv", [128, 8]); pp_d = dt_in("pp", [NL, 128, NPC])
    A0_d = dt_in("A0", [NL, 4, 128, 128]); h0_d = dt_in("h0", [NL, 128, 8]); fl_d = dt_in("fl", [128, 2 + 2 * R64])
    cst_d = dt_in("cst", [128, 7 * 512]); cstf_d = dt_in("cstf", [128, 256])
    w_mod = dt_in("w_mod", [NL, D, 6 * D]); w_in = dt_in("w_in", [NL, D, PIN]); w_out = dt_in("w_out", [NL, D, D])
    w_up = dt_in("w_up", [NL, D, 2 * DFF]); w_down = dt_in("w_down", [NL, DFF, D])
    rw_wup = dt_in("rw_w_up", [NL, 2, 64, WA]); rw_aup = dt_in("rw_a_up", [NL, 2, 64, WA]); rw_gup = dt_in("rw_g_up", [NL, 128, WA])
    ga_w = dt_in("lru_ga_w", [NL, 2, 8, 64, 64]); gx_w = dt_in("lru_gx_w", [NL, 2, 8, 64, 64])
    yT = dt_out("yT", [8, 128, T]); So_d = dt_out("So", [NL, 4, NSQ, 128, 128]); ho_d = dt_out("ho", [NL, 4, 128, 2 * NSQ])

    es = ExitStack()
    S = Sched(nc, es)
    sbt = lambda n, s, d: es.enter_context(nc.sbuf_tensor("sb_" + n, s, d))
    xres_t = sbt("xres", [128, 8 * T], F32); xres = [Buf(xres_t[:, c * T:(c + 1) * T]) for c in range(8)]
    hbf_t = sbt("hbf", [128, 8 * T], BF16); hbf = [Buf(hbf_t[:, c * T:(c + 1) * T]) for c in range(8)]
    ar_t = sbt("arena", [128, ARENA_F32], F32)
    AR = Arena(ar_t, ar_t.bitcast(BF16), ARENA_F32)
    cst_t = sbt("cst", [128, 7 * 512], BF16); cstB = Buf(cst_t[:, :])
    ML, MU, MLI, MUI, IDT, MRF, MRB = [cstB[:, i * 512:(i + 1) * 512] for i in range(7)]
    cstf_t = sbt("cstf", [128, 256], F32); cstF = Buf(cstf_t[:, :]); BD64 = cstF[:, 0:128]; BD1 = cstF[:, 128:256]
    onesm = Buf(sbt("onesm", [128, 128], BF16)[:, :])
    ppB = Buf(sbt("ppt", [128, NL * NPC], F32)[:, :])
    der_t = sbt("der", [128, NL * 48], F32); derL = [Buf(der_t[:, l * 48:(l + 1) * 48]) for l in range(NL)]
    mod_t = sbt("modt", [128, NL * 48], F32); modL = [Buf(mod_t[:, l * 48:(l + 1) * 48]) for l in range(NL)]
    flB = Buf(sbt("flt", [128, 2 + 2 * R64], F32)[:, :])
    cvB = Buf(sbt("cvt", [128, 8], F32)[:, :]); cvbf = Buf(sbt("cvbf", [128, 8], BF16)[:, :])
    wsm_t = sbt("wsm", [128, 5 * 512], BF16); wsm = Buf(wsm_t[:, :])
    bdw = Buf(sbt("bdw", [128, 16 * 128], BF16)[:, :])
    wst = [Buf(sbt("wst%d" % i, [128, 8 * 128], BF16)[:, :]) for i in range(4)]
    woutb = [Buf(sbt("wout%d" % i, [128, 1024], BF16)[:, :]) for i in range(2)]
    wst = wst + woutb
    A32 = Buf(sbt("A32", [128, 128], F32)[:, :]); Abf = Buf(sbt("Abf", [128, 128], BF16)[:, :])
    tmpA = Buf(sbt("tmpA", [128, 128], F32)[:, :])
    Xbf = Buf(sbt("Xbf", [128, 128], BF16)[:, :]); Ubf = Buf(sbt("Ubf", [128, 128], BF16)[:, :])
    stg = [Buf(sbt("stg%d" % i, [128, 128], F32)[:, :]) for i in range(2)]
    h0t = Buf(sbt("h0t", [128, 8], F32)[:, :])
    hfin_d = [Buf(sbt("hfin%d" % d, [128, NSQ], F32)[:, :]) for d in range(2)]
    hini_d = [Buf(sbt("hini%d" % d, [128, NSQ], F32)[:, :]) for d in range(2)]
    tbs = [Buf(sbt("tb%d" % i, [128, R64], F32)[:, :]) for i in range(4)]
    wc_t = sbt("wct", [128, 6 * 4], F32)
    WC = [[Buf(wc_t[:, (s3 * 2 + d) * 4:(s3 * 2 + d) * 4 + 4]) for d in range(2)] for s3 in range(3)]
    ps_t = es.enter_context(nc.psum_tensor("ps", [128, 4096], F32))
    bank = [Buf(ps_t[:, b * 512:(b + 1) * 512]) for b in range(8)]

    V, G, A_ = "vector", "gpsimd", "scalar"
    def fd(buf):
        n = 1
        for st_, c_ in list(buf.ap.ap)[1:]:
            n *= int(c_)
        return n
    def ecost(eng, out):
        n = fd(out)
        if eng == V: return 0.25 + n * 0.00105
        if eng == G: return 0.35 + n * 0.0017
        return 0.25 + n * 0.00085
    def tt(eng, out, a, b, op):
        S.op(eng, lambda e: e.tensor_tensor(out=out.ap, in0=a.ap, in1=b.ap, op=op), reads=[a.reg, b.reg], writes=[out.reg], cost=ecost(eng, out))
    def ts(eng, out, a, s1, s2=None, op0=ALU.mult, op1=None):
        rd = [a.reg] + [x.reg for x in (s1, s2) if isinstance(x, Buf)]
        a1 = s1.ap if isinstance(s1, Buf) else s1
        a2 = s2.ap if isinstance(s2, Buf) else s2
        if op1 is None and eng == G and op0 == ALU.mult:
            S.op(eng, lambda e: e.tensor_scalar(out=out.ap, in0=a.ap, scalar1=a1, scalar2=0.0, op0=ALU.mult, op1=ALU.add), reads=rd, writes=[out.reg], cost=ecost(eng, out))
        elif op1 is None:
            S.op(eng, lambda e: e.tensor_scalar(out=out.ap, in0=a.ap, scalar1=a1, scalar2=None, op0=op0), reads=rd, writes=[out.reg], cost=ecost(eng, out))
        else:
            S.op(eng, lambda e: e.tensor_scalar(out=out.ap, in0=a.ap, scalar1=a1, scalar2=a2, op0=op0, op1=op1), reads=rd, writes=[out.reg], cost=ecost(eng, out))
    def stt(out, a, s, b, op0, op1):
        rd = [a.reg, b.reg] + ([s.reg] if isinstance(s, Buf) else [])
        sa = s.ap if isinstance(s, Buf) else s
        S.op(V, lambda e: e.scalar_tensor_tensor(out=out.ap, in0=a.ap, scalar=sa, in1=b.ap, op0=op0, op1=op1), reads=rd, writes=[out.reg], cost=ecost(V, out))
    def act(out, a, func, scale=1.0, bias=0.0):
        rd = [a.reg] + [x.reg for x in (scale, bias) if isinstance(x, Buf)]
        sc = scale.ap if isinstance(scale, Buf) else scale
        bi = bias.ap if isinstance(bias, Buf) else bias
        S.op(A_, lambda e: e.activation(out=out.ap, in_=a.ap, func=func, bias=bi, scale=sc), reads=rd, writes=[out.reg], cost=ecost(A_, out))
    def cp(eng, out, a):
        if eng == A_:
            act(out, a, AF.Copy)
        else:
            S.op(eng, lambda e: e.tensor_copy(out=out.ap, in_=a.ap), reads=[a.reg], writes=[out.reg], cost=ecost(eng, out))
    def recip(out, a):
        S.op(V, lambda e: e.reciprocal(out=out.ap, in_=a.ap), reads=[a.reg], writes=[out.reg])
    def scan(out, d0, d1, init):
        ia = init.ap if isinstance(init, Buf) else init
        rd = [d0.reg, d1.reg] + ([init.reg] if isinstance(init, Buf) else [])
        S.op(V, lambda e: e.tensor_tensor_scan(out=out.ap, data0=d0.ap, data1=d1.ap, initial=ia, op0=ALU.mult, op1=ALU.add), reads=rd, writes=[out.reg], cost=0.3 + fd(out) * 0.0021)
    def mms(items, out_reg):
        rd = []
        fns = []
        cost = 0.0
        for (o, l_, r_, st, sp) in items:
            n_ = fd(r_)
            cost += max(0.045, n_ * 0.00052) * (4.0 if r_.ap.tensor.dtype == F32 else 1.0)
            rd.append(l_.reg); rd.append(r_.reg)
            fns.append(lambda e, o=o, l_=l_, r_=r_, st=st, sp=sp: e.matmul(o, lhsT=l_.ap, rhs=r_.ap, start=st, stop=sp))
        rdd = {}
        for r in _flat(rd): rdd[id(r)] = r
        S.mm_group(fns, reads=list(rdd.values()), writes=[out_reg], cost=cost)
    def ld(out, src, eng="sync"):
        S.dma(eng, out.ap, src, writes=[out.reg])
    def memset(eng, out, val):
        S.op(eng, lambda e: e.memset(out.ap, val), writes=[out.reg])
    def rev(b):
        return Buf(b.ap[:, ::-1], b.reg)
    def v3(b, w=64):
        return b.ap.rearrange("p (r w) -> p r w", w=w)

    ld(cstB, cst_d[:, :], "gpsimd"); ld(cstF, cstf_d[:, :]); ld(flB, fl_d[:, :]); ld(cvB, cv[:, :])
    for l in range(NL):
        ld(ppB[:, l * NPC:(l + 1) * NPC], pp_d[l])
    for c in range(8):
        ld(xres[c], xT[c])
    memset(V, onesm, 1.0 / D)
    keepf = flB[:, 0:1]
    cfl_prev = flB[:, 2:2 + R64]
    cfl_next = flB[:, 2 + R64:2 + 2 * R64]
    def P(l, name, j=0, n=1):
        o = l * NPC + PC[name] + j
        return ppB[:, o:o + n]
    def DR(l, j, n=1):
        return derL[l][:, j:j + n]
    def MOD(l, j, n=1):
        return modL[l][:, j:j + n]
    act(cvbf, cvB, AF.Silu)

    wst_i = {}
    def load_w_cols(wd, l, col0, row0=0, nk=8, pool=(0, 1, 2, 3)):
        i = wst_i.get(pool, 0); wst_i[pool] = i + 1
        wb = wst[pool[i % len(pool)]]
        S.dma("gpsimd", wb.ap[:, 0:nk * 128].rearrange("p (k n) -> p k n", k=nk),
              wd[l][row0:row0 + nk * 128, :].rearrange("(k p) n -> p k n", p=128)[:, :, col0:col0 + 128], writes=[wb.reg])
        return wb

    modps = bank[7][:, 256:512]
    mod_pool = [(0, 1, 2, 3)]
    def mod_cols(l, c0, c1):
        for col in range(c0, c1):
            wb = load_w_cols(w_mod, l, col * 128, pool=mod_pool[0])
            mms([(modps.ap[:, col:col + 1], wb[:, k * 128:(k + 1) * 128], cvbf[:, k:k + 1], k == 0, k == 7) for k in range(8)], modps.reg)
    def mod_fin(l):
        tt(V, MOD(l, 0, 48), modps[:, 0:48], P(l, "bmod", 0, 48), ALU.add)
        stt(DR(l, 0, 8), MOD(l, 8, 8), 1.0, P(l, "ln1g", 0, 8), ALU.add, ALU.mult)
        stt(DR(l, 8, 8), MOD(l, 32, 8), 1.0, P(l, "ln2g", 0, 8), ALU.add, ALU.mult)
        ts(V, DR(l, 16, 4), P(l, "k_a", 0, 4), -1.0, 1.0, ALU.mult, ALU.add)
        tt(V, DR(l, 20, 4), P(l, "k_a", 0, 4), P(l, "r_k", 0, 4), ALU.mult)
        ts(V, DR(l, 24, 4), P(l, "k_a", 0, 4), -2.0, 2.0, ALU.mult, ALU.add)
        tt(V, DR(l, 24, 4), DR(l, 24, 4), P(l, "r_k", 0, 4), ALU.mult)
        act(DR(l, 36, 8), P(l, "lam", 0, 8), AF.Exp, scale=-1.0)
        act(DR(l, 36, 8), DR(l, 36, 8), AF.Ln, bias=1.0)
        ts(V, DR(l, 28, 8), DR(l, 36, 8), -8.0, None, ALU.mult)
    mod_cols(0, 0, 48); mod_fin(0)

    def rms_rstd(tq, sqb, rs):
        tsl = slice(tq * 512, (tq + 1) * 512)
        for c in range(8):
            act(sqb[:, c * 512:(c + 1) * 512], xres[c][:, tsl], AF.Square)
        mms([(bank[0].ap, onesm, sqb[:, c * 512:(c + 1) * 512], c == 0, c == 7) for c in range(8)], bank[0].reg)
        act(rs, bank[0], AF.Ln, bias=EPS)
        act(rs, rs, AF.Exp, scale=-0.5)

    def rmsnorm_to_hbf(l, ggoff, shoff):
        m = AR.mark()
        sqw = AR.b(8 * T); rsw = AR.f(T); tmp = [AR.f(T) for _ in range(3)]
        for c in range(8):
            act(sqw[:, c * T:(c + 1) * T], xres[c], AF.Square)
        for tq in range(NT):
            tsl = slice(tq * 512, (tq + 1) * 512)
            bk = bank[tq % 4]
            mms([(bk.ap, onesm, sqw[:, c * T + tq * 512:c * T + (tq + 1) * 512], c == 0, c == 7) for c in range(8)], bk.reg)
            act(rsw[:, tsl], bk, AF.Ln, bias=EPS)
        act(rsw, rsw, AF.Exp, scale=-0.5)
        for c in range(8):
            tb = tmp[c % 3]
            stt(tb, xres[c], DR(l, ggoff + c), rsw, ALU.mult, ALU.mult)
            if c % 4 != 3:
                act(hbf[c], tb, AF.Identity, bias=MOD(l, shoff + c))
            else:
                ts(V, hbf[c], tb, MOD(l, shoff + c), None, ALU.add)
        AR.release(m)

    def proj_fm(wb, bk, tq):
        mms([(bk.ap, wb[:, k * 128:(k + 1) * 128], hbf[k][:, tq * 512:(tq + 1) * 512], k == 0, k == 7) for k in range(8)], bk.reg)

    wout_i = [0]
    def xres_evac(l, fo, tq, bk, bi, tmps):
        xs = xres[fo][:, tq * 512:(tq + 1) * 512]
        if tmps is not None and bi % 3 == 2:
            tm = tmps[(bi // 3) % len(tmps)]
            act(tm, bk, AF.Identity, scale=MOD(l, 16 + fo))
            tt(G, xs, xs, tm, ALU.add)
        else:
            stt(xs, bk, MOD(l, 16 + fo), xs, ALU.mult, ALU.add)

    def wout_accum(l, kc, ycur, banks, tmps=None):
        wb = woutb[wout_i[0] % 2]; wout_i[0] += 1
        S.dma("gpsimd", wb.ap, w_out[l][kc * 128:(kc + 1) * 128, :], writes=[wb.reg])
        bi = 0
        for fo in range(8):
            for tq in range(NT):
                bk = banks[bi % len(banks)]
                mms([(bk.ap, wb[:, fo * 128:(fo + 1) * 128], ycur[:, tq * 512:(tq + 1) * 512], True, True)], bk.reg)
                xres_evac(l, fo, tq, bk, bi, tmps)
                bi += 1

    def hblk(hh):
        return slice(hh * 64, (hh + 1) * 64)

    for l in range(NL):
        S.mark("L%d start" % l)
        rmsnorm_to_hbf(l, 0, 0)
        S.mark("L%d norm1" % l)
        for d in range(2):
            S.dma("gpsimd", wsm.ap[0:64, d * 512:(d + 1) * 512], rw_wup[l, d], writes=[wsm.reg])
            S.dma("gpsimd", wsm.ap[64:128, (2 + d) * 512:(3 + d) * 512], rw_aup[l, d], writes=[wsm.reg])
        S.dma("gpsimd", wsm.ap[:, 4 * 512:5 * 512], rw_gup[l], writes=[wsm.reg])
        memset(G, bdw, 0.0)
        for gt, gw in enumerate((ga_w, gx_w)):
            for d in range(2):
                for c in range(4):
                    o = ((gt * 2 + d) * 4 + c) * 128
                    for hh in range(2):
                        S.dma("gpsimd", bdw.ap[hblk(hh), o + hh * 64:o + (hh + 1) * 64], gw[l, d, 2 * c + hh], writes=[bdw.reg])
        WUP = lambda d, c: wsm[0:64, d * 512 + c * 128:d * 512 + (c + 1) * 128]
        AUP = lambda d, c: wsm[64:128, (2 + d) * 512 + c * 128:(2 + d) * 512 + (c + 1) * 128]
        GUP = lambda c: wsm[:, 4 * 512 + c * 128:4 * 512 + (c + 1) * 128]

        m_layer = AR.mark()
        txw = AR.b(T)
        wb = load_w_cols(w_in, l, 3 * WA)
        for tq in range(NT):
            bk = bank[6 + tq % 2]; tsl = slice(tq * 512, (tq + 1) * 512)
            proj_fm(wb, bk, tq)
            act(txw[0:64, tsl], bk[0:64, :], AF.Tanh)
            act(txw[64:128, tsl], bk[64:128, :], AF.Copy)

        m_wkv = AR.mark()
        pending = []
        for c in range(4):
            AR.release(m_wkv)
            rbf = AR.b(T); kbf = AR.b(T); oacc = AR.f(T); Vtm = AR.b(T)
            def raws():
                for (dst, col0) in ((rbf, c * 128), (kbf, WA + c * 128)):
                    wb = load_w_cols(w_in, l, col0, pool=(0, 1))
                    for tq in range(NT):
                        bk = bank[tq % 2]
                        proj_fm(wb, bk, tq)
                        cp(A_ if tq % 2 == 0 else V, dst[:, tq * 512:(tq + 1) * 512], bk)
                wv = load_w_cols(w_in, l, 2 * WA + c * 128, pool=(0, 1))
                for n0 in range(0, NCH, 8):
                    bk = bank[2 + (n0 // 8) % 2]
                    items = []
                    for n in range(n0, n0 + 8):
                        for hh in range(2):
                            for k in range(8):
                                items.append((bk.ap[hblk(hh), (n - n0) * 64:(n - n0 + 1) * 64],
                                              hbf[k][:, n * 64:(n + 1) * 64], wv[:, k * 128 + hh * 64:k * 128 + (hh + 1) * 64], k == 0, k == 7))
                    mms(items, bk.reg)
                    cp(A_, Vtm[:, n0 * 64:(n0 + 8) * 64], bk)
                ld(A32, A0_d[l, c])
                cp(A_, Abf, A32)
            S.emit_merged(pending + [S.record(raws)])
            pending = []
            m_seg = AR.mark()
            OPD3 = [[[AR.b(SEG) for _ in range(7)] for _ in range(2)] for _ in range(3)]
            TTD = [[AR.b(SEG) for _ in range(2)] for _ in range(2)]
            PQA = [[[AR.b(SEG) for _ in range(2)] for _ in range(2)] for _ in range(2)]
            PQB = [[AR.b(SEG) for _ in range(3)] for _ in range(2)]
            KB = [[AR.b(SEG) for _ in range(2)] for _ in range(2)]
            TMPS = [[AR.f(SEG) for _ in range(6)] for _ in range(2)]

            def blockmm(lt, rh, bk, rhs_fixed=None):
                items = []
                for n in range(CPS):
                    for hh in range(2):
                        hs = hblk(hh); ns = slice(n * 64, (n + 1) * 64)
                        r_ = rh[hs, ns] if rhs_fixed is None else rhs_fixed[hs, 0:64]
                        items.append((bk.ap[hs, ns], lt[hs, ns], r_, True, True))
                mms(items, bk.reg)

            def prepA(d, sg):
                seg = sg if d == 0 else NSEG - 1 - sg
                t0 = seg * SEG
                tsl = slice(t0, t0 + SEG)
                RT, QT, Ktm, Btm, LkT, MkT, MbT = OPD3[sg % 3][d]
                KT, BT = KB[d]
                Pa, Qa = PQA[sg % 2][d]
                sgm, cum, aa, t3, t4, Ep = TMPS[d]
                pk = bank[d][:, 0:SEG]
                mms([(pk.ap, WUP(d, c), txw[0:64, tsl], True, True)], pk.reg)
                act(sgm, pk, AF.Sigmoid, bias=P(l, "w0", d * 4 + c))
                mms([(pk.ap, AUP(d, c), txw[64:128, tsl], True, True)], pk.reg)
                act(aa, pk, AF.Sigmoid, bias=P(l, "a0", d * 4 + c))
                if d == 0:
                    scan(cum, MRF[:, 0:SEG], sgm, 0.0)
                else:
                    scan(rev(cum), rev(MRB[:, 0:SEG]), rev(sgm), 0.0)
                tt(G, sgm, cum, sgm, ALU.subtract)
                act(Ep, cum, AF.Exp, scale=-DSC)
                act(cum, cum, AF.Exp, scale=DSC)
                act(sgm, sgm, AF.Exp, scale=-DSC)
                Em = cum; Ex = sgm
                c0 = 63 if d == 0 else 0
                cp(G, WC[sg % 3][d], Buf(v3(Ep)[:, :, c0], Ep.reg))
                tt(G if "rtp" in OPT else V, RT, rbf[:, tsl], Ep, ALU.mult)
                ts(V, t3, aa, P(l, "k_a", c), DR(l, 16 + c), ALU.mult, ALU.add)
                tt(G, t3, t3, kbf[:, tsl], ALU.mult)
                tt(V, KT, t3, Em, ALU.mult)
                ts(G if "t4p" in OPT else V, t4, kbf[:, tsl], P(l, "k_k", c), None, ALU.mult)
                act(t3, t4, AF.Square)
                mms([(pk.ap, BD1, t3, True, True)], pk.reg)
                act(t3, pk, AF.Ln, bias=KK_EPS)
                act(t3, t3, AF.Exp, scale=-0.5)
                tt(V, t4, t4, t3, ALU.mult)
                tt(G, QT, t4, Ex, ALU.mult)
                tt(G if "bp" in OPT else V, t4, t4, aa, ALU.mult)
                stt(BT, t4, -1.0, Em, ALU.mult, ALU.mult)
                mN, mNT, mMT = (ML, MU, MUI) if d == 0 else (MU, ML, MLI)
                def score(lt, rh, mask, dst, alt=False):
                    blockmm(lt, rh, pk)
                    if alt:
                        cp(A_, dst, pk)
                        tt(G, dst, dst, mask[:, 0:SEG], ALU.mult)
                    else:
                        tt(V, dst, pk, mask[:, 0:SEG], ALU.mult)
                score(QT, BT, mN, Pa, "mpq" in OPT)
                score(BT, QT, mNT, Qa, "mpq" in OPT)
                score(KT, QT, mNT, LkT, "mlk" in OPT)
                score(KT, RT, mMT, MkT, "mmk" in OPT)
                score(BT, RT, mMT, MbT, "mmb" in OPT)
                wc = WC[sg % 3][d]
                wbc = Buf(bass.AP(tensor=wc.ap.tensor, offset=wc.ap.offset, ap=[list(wc.ap.ap[0]), [1, CPS], [0, 64]]), wc.reg)
                for src in (KT, BT):
                    s3 = Buf(v3(src), src.reg)
                    tt(V, s3, s3, wbc, ALU.mult)
                blockmm(KT, None, pk, rhs_fixed=IDT); cp(A_, Ktm, pk)
                blockmm(BT, None, pk, rhs_fixed=IDT); cp(A_, Btm, pk)

            def prepB(d, sg):
                Pa, Qa = PQA[sg % 2][d]
                Pb, Qb, Talt = PQB[d]
                TT = TTD[sg % 2][d]
                bb = [bank[2][:, 0:SEG], bank[3][:, 0:SEG]] if d == 0 else [bank[6][:, 0:SEG], bank[7][:, 0:SEG]]
                Tc, Tn = Talt, TT
                tt(V, Tc, Qa, IDT[:, 0:SEG], ALU.add)
                Pc, Pn, Qc, Qn = Pa, Pb, Qa, Qb
                for rd_ in range(1, 6):
                    b0 = bb[rd_ % 2]; b1 = bb[(rd_ + 1) % 2]
                    blockmm(Qc, Pc, b0); cp(A_, Pn, b0)
                    if rd_ < 5:
                        blockmm(Pc, Qc, b1); cp(A_ if "qact" in OPT else V, Qn, b1)
                    blockmm(Pn, Tc, b0); tt(V, Tn, b0, Tc, ALU.add)
                    Pc, Pn = Pn, Pc; Qc, Qn = Qn, Qc; Tc, Tn = Tn, Tc
                assert Tc is TT

            XS = bank[4][:, 0:256]; UU = bank[4][:, 256:512]; Yb = [bank[5][:, 0:256], bank[5][:, 256:512]]
            def seq(sg):
                segs = [sg, NSEG - 1 - sg]
                OPS = [OPD3[sg % 3][d] + [TTD[sg % 2][d]] for d in range(2)]
                for st in range(CPS):
                    nloc = [st, CPS - 1 - st]
                    for d in range(2):
                        ds = slice(d * 64, (d + 1) * 64)
                        ts(G, tmpA[:, ds], A32[:, ds], WC[sg % 3][d][:, nloc[d]:nloc[d] + 1], None, ALU.mult)
                    items = []
                    for d in range(2):
                        RT, QT, Ktm, Btm, LkT, MkT, MbT, TT = OPS[d]
                        nl = nloc[d]; ng = segs[d] * CPS + nl
                        for hh in range(2):
                            hs = hblk(hh); ns = slice(nl * 64, (nl + 1) * 64)
                            o = XS.ap[hs, d * 64:(d + 1) * 64]
                            items.append((o, QT[hs, ns], Abf[hs, d * 64:(d + 1) * 64], True, False))
                            items.append((o, LkT[hs, ns], Vtm[hs, ng * 64:(ng + 1) * 64], False, True))
                    mms(items, XS.reg)
                    cp(A_ if "xact" in OPT else V, Xbf, XS[:, 0:128])
                    items = []
                    for d in range(2):
                        TT = OPS[d][7]; nl = nloc[d]
                        for hh in range(2):
                            hs = hblk(hh); ns = slice(nl * 64, (nl + 1) * 64)
                            items.append((UU.ap[hs, d * 64:(d + 1) * 64], TT[hs, ns], Xbf[hs, d * 64:(d + 1) * 64], True, True))
                    mms(items, UU.reg)
                    cp(A_ if "uact" in OPT else V, Ubf, UU[:, 0:128])
                    items = []
                    for d in range(2):
                        RT, QT, Ktm, Btm, LkT, MkT, MbT, TT = OPS[d]
                        nl = nloc[d]; ng = segs[d] * CPS + nl
                        for hh in range(2):
                            hs = hblk(hh); ns = slice(nl * 64, (nl + 1) * 64)
                            o = XS.ap[hs, 128 + d * 64:128 + (d + 1) * 64]
                            items.append((o, Ktm[hs, ns], Vtm[hs, ng * 64:(ng + 1) * 64], True, False))
                            items.append((o, Btm[hs, ns], Ubf[hs, d * 64:(d + 1) * 64], False, True))
                    mms(items, XS.reg)
                    for d in range(2):
                        RT, QT, Ktm, Btm, LkT, MkT, MbT, TT = OPS[d]
                        nl = nloc[d]; ng = segs[d] * CPS + nl
                        items = []
                        for hh in range(2):
                            hs = hblk(hh); ns = slice(nl * 64, (nl + 1) * 64)
                            o = Yb[d].ap[hs, ns]
                            items.append((o, Abf[hs, d * 64:(d + 1) * 64], RT[hs, ns], True, False))
                            items.append((o, Vtm[hs, ng * 64:(ng + 1) * 64], MkT[hs, ns], False, False))
                            items.append((o, Ubf[hs, d * 64:(d + 1) * 64], MbT[hs, ns], False, True))
                        mms(items, Yb[d].reg)
                    gstep = sg * CPS + st
                    bnd = (gstep + 1) % 4 == 0
                    if not bnd:
                        tt(V, Abf, XS[:, 128:256], tmpA, ALU.add)
                    tt(V, A32, XS[:, 128:256], tmpA, ALU.add)
                    if bnd:
                        q = (gstep + 1) // 4 - 1
                        sb_ = stg[q % 2]
                        cp(V, sb_, A32)
                        S.dma("sync", So_d[l, c, q], sb_.ap, reads=[sb_.reg], is_output=True)
                        ts(V, A32, A32, keepf, None, ALU.mult)
                        cp(A_, Abf, A32)
                for d in range(2):
                    t0 = segs[d] * SEG
                    dst = oacc[:, t0:t0 + SEG]
                    first = (segs[d] < NSEG - 1 - segs[d]) if d == 0 else (segs[d] > NSEG - 1 - segs[d])
                    if d == 0 and segs[d] == NSEG - 1 - segs[d]:
                        first = True
                    if first:
                        cp(A_, dst, Yb[d])
                    else:
                        tt(V, dst, Yb[d], dst, ALU.add)

            S.emit_merged([S.record(lambda: prepA(0, 0)), S.record(lambda: prepA(1, 0))])
            recs = [S.record(lambda: prepB(0, 0)), S.record(lambda: prepB(1, 0))]
            if NSEG > 1:
                recs += [S.record(lambda: prepA(0, 1)), S.record(lambda: prepA(1, 1))]
            S.emit_merged(recs)
            for sg in range(NSEG):
                recs = [S.record(lambda: seq(sg))]
                if sg + 1 < NSEG:
                    recs += [S.record(lambda: prepB(0, sg + 1)), S.record(lambda: prepB(1, sg + 1))]
                if sg + 2 < NSEG:
                    recs += [S.record(lambda: prepA(0, sg + 2)), S.record(lambda: prepA(1, sg + 2))]
                S.emit_merged(recs)
            AR.release(m_seg)
            ycur = AR.b(T)
            NPS = min(NT, 4)
            f = [[AR.f(512) for _ in range(4)] for _ in range(NPS)]
            sxgt = [AR.b(512) for _ in range(NPS)]
            wtmp = [AR.f(512), AR.f(512)]
            wv = load_w_cols(w_in, l, 2 * WA + c * 128, pool=(2, 3))
            wg = load_w_cols(w_in, l, 3 * WA + 128, pool=(2, 3))
            def post(tq):
                tsl = slice(tq * 512, (tq + 1) * 512)
                ff = f[tq % NPS]; pbk = [bank[(tq % NPS) * 2], bank[(tq % NPS) * 2 + 1]]
                o = oacc[:, tsl]
                mms([(pbk[0].ap, BD64, o, True, True)], pbk[0].reg)
                tt(V, ff[0], o, pbk[0], ALU.subtract)
                act(ff[1], ff[0], AF.Square)
                mms([(pbk[1].ap, BD64, ff[1], True, True)], pbk[1].reg)
                act(ff[1], pbk[1], AF.Ln, bias=LNX_EPS)
                act(ff[1], ff[1], AF.Exp, scale=-0.5)
                tt(V, ff[0], ff[0], ff[1], ALU.mult)
                ts(V, ff[0], ff[0], P(l, "lnxg", c), P(l, "lnxb", c), ALU.mult, ALU.add)
                for d in range(2):
                    mms([(pbk[d].ap, AUP(d, c), txw[64:128, tsl], True, True)], pbk[d].reg)
                    act(ff[1 + d], pbk[d], AF.Sigmoid, bias=P(l, "a0", d * 4 + c))
                tt(G, ff[1], ff[1], ff[2], ALU.add)
                ts(V, ff[1], ff[1], DR(l, 20 + c), DR(l, 24 + c), ALU.mult, ALU.add)
                tt(G, ff[2], rbf[:, tsl], kbf[:, tsl], ALU.mult)
                tt(V, ff[1], ff[1], ff[2], ALU.mult)
                mms([(pbk[0].ap, BD1, ff[1], True, True)], pbk[0].reg)
                cp(A_, ff[2], pbk[0])
                proj_fm(wv, pbk[1], tq)
                tt(V, ff[1], pbk[1], ff[2], ALU.mult)
                tt(G, ff[0], ff[0], ff[1], ALU.add)
                proj_fm(wg, pbk[0], tq)
                act(sxgt[tq % NPS], pbk[0], AF.Sigmoid)
                mms([(pbk[1].ap, GUP(c), sxgt[tq % NPS], True, True)], pbk[1].reg)
                tt(V, ycur[:, tsl], pbk[1], ff[0], ALU.mult)
            recs = [S.record(lambda tq=tq: post(tq)) for tq in range(NT)]
            for i in range(0, NT, NPS):
                S.emit_merged(recs[i:i + NPS])
            pending = [S.record(lambda c=c, ycur=ycur, wtmp=wtmp: wout_accum(l, c, ycur, [bank[4], bank[5], bank[6], bank[7]], wtmp))]
        S.emit_merged(pending)
        S.mark("L%d wkv" % l)

        AR.release(m_layer)
        lru_pending = []
        ycs = [AR.b(T), AR.b(T)]
        m_lru = AR.mark()
        def wout_pair(kcA, kcB, ycA, ycB, banks, tmps=None):
            wA = woutb[0]; wB = woutb[1]
            S.dma("gpsimd", wA.ap, w_out[l][kcA * 128:(kcA + 1) * 128, :], writes=[wA.reg])
            S.dma("gpsimd", wB.ap, w_out[l][kcB * 128:(kcB + 1) * 128, :], writes=[wB.reg])
            bi = 0
            for fo in range(8):
                for tq in range(NT):
                    bk = banks[bi % len(banks)]; bi += 1
                    tsl = slice(tq * 512, (tq + 1) * 512)
                    mms([(bk.ap, wA[:, fo * 128:(fo + 1) * 128], ycA[:, tsl], True, False),
                         (bk.ap, wB[:, fo * 128:(fo + 1) * 128], ycB[:, tsl], False, True)], bk.reg)
                    xres_evac(l, fo, tq, bk, bi - 1, tmps)
        for c in range(4):
            AR.release(m_lru)
            xb32 = AR.f(T); xc = AR.f(T); gA0 = AR.f(T); gI0 = AR.f(T); gI1 = AR.f(T); hF = AR.f(T); hB = AR.f(T)
            xcb = AR.b(T); ycur = ycs[c % 2]
            gAs = [gA0, xb32]; gIs = [gI0, gI1]
            def lru_common():
                wb = load_w_cols(w_in, l, 3 * WA + 256 + c * 128, pool=(0, 1))
                for tq in range(NT):
                    bk = bank[tq % 4]
                    proj_fm(wb, bk, tq)
                    cp(A_, xb32[:, tq * 512:(tq + 1) * 512], bk)
                cw = lambda j: P(l, "cw", j * 4 + c)
                act(xc, xb32, AF.Identity, scale=cw(2), bias=P(l, "cb", c))
                x3 = v3(xb32); y3 = v3(xc)
                def shifted(j, off):
                    if off < 0:
                        o_ = Buf(y3[:, :, -off:64], xc.reg); i_ = Buf(x3[:, :, 0:64 + off], xb32.reg)
                    else:
                        o_ = Buf(y3[:, :, 0:64 - off], xc.reg); i_ = Buf(x3[:, :, off:64], xb32.reg)
                    stt(o_, i_, cw(j), o_, ALU.mult, ALU.add)
                shifted(1, -1); shifted(0, -2); shifted(3, 1)
                if R64 > 1:
                    def fix(j, ocol, icol, nxt, tb):
                        if nxt:
                            o_ = Buf(y3[:, 0:R64 - 1, ocol], xc.reg); i_ = Buf(x3[:, 1:R64, icol], xb32.reg); fl_ = cfl_next[:, 0:R64 - 1]
                        else:
                            o_ = Buf(y3[:, 1:R64, ocol], xc.reg); i_ = Buf(x3[:, 0:R64 - 1, icol], xb32.reg); fl_ = cfl_prev[:, 0:R64 - 1]
                        tt(G, tb[:, 0:R64 - 1], i_, fl_, ALU.mult)
                        stt(o_, tb[:, 0:R64 - 1], cw(j), o_, ALU.mult, ALU.add)
                    fix(1, 0, 63, False, tbs[0]); fix(0, 0, 62, False, tbs[1]); fix(0, 1, 63, False, tbs[2]); fix(3, 63, 0, True, tbs[3])
                cp(G, xcb, xc)
                ld(h0t, h0_d[l])
            def lru_dir(d):
                gA = gAs[d]; gI = gIs[d]
                for gt, dst in ((0, gA), (1, gI)):
                    o = ((gt * 2 + d) * 4 + c) * 128
                    bnm = "gab" if gt == 0 else "gxb"
                    for tq in range(NT):
                        bk = bank[2 * d + tq % 2]
                        mms([(bk.ap, bdw[:, o:o + 128], xcb[:, tq * 512:(tq + 1) * 512], True, True)], bk.reg)
                        act(dst[:, tq * 512:(tq + 1) * 512], bk, AF.Sigmoid, bias=P(l, bnm, d * 4 + c))
                hD = hF if d == 0 else hB
                act(gA, gA, AF.Exp, scale=DR(l, 28 + d * 4 + c))
                tt(G, gI, gI, xc, ALU.mult)
                tt(V, hD, gA, gA, ALU.mult)
                act(hD, hD, AF.Sqrt, scale=-1.0, bias=1.0)
                tt(V if d == 0 else G, gI, gI, hD, ALU.mult)
                ini = h0t[:, d * 4 + c:d * 4 + c + 1]
                for q in (range(NSQ) if d == 0 else range(NSQ - 1, -1, -1)):
                    qs = slice(q * 256, (q + 1) * 256)
                    if d == 0:
                        scan(hD[:, qs], gA[:, qs], gI[:, qs], ini)
                        last = hD[:, q * 256 + 255:q * 256 + 256]
                    else:
                        scan(rev(hD[:, qs]), rev(gA[:, qs]), rev(gI[:, qs]), ini)
                        last = hD[:, q * 256:q * 256 + 1]
                    cp(G, hfin_d[d][:, q:q + 1], last)
                    ini = hini_d[d][:, q:q + 1]
                    ts(G, ini, last, keepf, None, ALU.mult)
                S.dma("sync", ho_d[l, c][:, d * NSQ:(d + 1) * NSQ], hfin_d[d].ap, reads=[hfin_d[d].reg], is_output=True)
            def lru_final():
                tt(V, hF, hF, hB, ALU.add)
                wb2 = load_w_cols(w_in, l, 3 * WA + 256 + WA + c * 128, pool=(0, 1))
                for tq in range(NT):
                    bk = bank[4 + tq % 3]
                    proj_fm(wb2, bk, tq)
                    act(gA0[:, tq * 512:(tq + 1) * 512], bk, AF.Gelu_apprx_tanh)
                tt(V, ycur, hF, gA0, ALU.mult)
                if c % 2 == 1:
                    wout_pair(4 + c - 1, 4 + c, ycs[0], ycs[1], [bank[4], bank[5], bank[6]], [hB[:, 0:512]] + ([hB[:, 512:1024]] if T >= 1024 else []))
            mod_pool[0] = (2, 3)
            S.emit_merged(lru_pending + [S.record(lru_common)])
            recs = [S.record(lambda: lru_dir(0)), S.record(lambda: lru_dir(1))]
            wts = [3, 3]
            if l + 1 < NL:
                recs.append(S.record(lambda: mod_cols(l + 1, c * 12, (c + 1) * 12)))
                wts.append(1)
            S.emit_merged(recs, weights=wts)
            lru_pending = [S.record(lru_final)]
        S.emit_merged(lru_pending)
        if l + 1 < NL:
            mod_fin(l + 1)

        AR.release(m_layer)
        S.mark("L%d lru" % l)
        rmsnorm_to_hbf(l, 8, 24)
        S.mark("L%d norm2" % l)
        TH = T // 2 if T >= 1024 else T
        NH = T // TH; NTH = TH // 512
        mj = [AR.b(TH) for _ in range(22)]
        accs = [[AR.f(TH), AR.f(TH)] for _ in range(2)]
        ftmp = [AR.f(TH) for _ in range(2)]
        for hf in range(NH):
            tb0 = hf * TH
            def ffn_j(j, s, wbs, nxt):
                accA, accB = accs[s]
                tmpc = ftmp[s]
                R = TH // 64
                r0 = tb0 // 64
                for ab, acc in ((0, accA), (1, accB)):
                    colc = ab * 22 + j
                    wb = wbs.pop(0)
                    if nxt:
                        wbs.append(load_w_cols(w_up, l, nxt.pop(0) * 128, pool=(2 * s, 2 * s + 1, 4 + s)))
                    fw = lambda jj, colc=colc: P(l, "fw", jj * 44 + colc)
                    b0 = s * 4 + ab * 2
                    for i in range(NTH):
                        proj_fm(wb, bank[b0 + i], tb0 // 512 + i)
                    pu2 = Buf(ps_t[:, b0 * 512:(b0 + NTH) * 512], [bank[b0 + i].reg for i in range(NTH)])
                    act(acc, pu2, AF.Identity, scale=fw(1), bias=P(l, "fb", colc))
                    p3 = v3(pu2); a3 = v3(acc); t3_ = v3(tmpc)
                    act(Buf(t3_[:, :, 0:63], tmpc.reg), Buf(p3[:, :, 0:63], pu2.reg), AF.Identity, scale=fw(0))
                    o_ = Buf(a3[:, :, 1:64], acc.reg)
                    tt(G, o_, o_, Buf(t3_[:, :, 0:63], tmpc.reg), ALU.add)
                    o_ = Buf(a3[:, :, 0:63], acc.reg); i_ = Buf(p3[:, :, 1:64], pu2.reg)
                    stt(o_, i_, fw(2), o_, ALU.mult, ALU.add)
                    tA = tbs[s * 2]; tB = tbs[s * 2 + 1]
                    tt(V, tA[:, 0:R - 1], Buf(p3[:, 0:R - 1, 63], pu2.reg), cfl_prev[:, r0:r0 + R - 1], ALU.mult)
                    o_ = Buf(a3[:, 1:R, 0], acc.reg)
                    stt(o_, tA[:, 0:R - 1], fw(0), o_, ALU.mult, ALU.add)
                    tt(V, tB[:, 0:R - 1], Buf(p3[:, 1:R, 0], pu2.reg), cfl_next[:, r0:r0 + R - 1], ALU.mult)
                    o_ = Buf(a3[:, 0:R - 1, 63], acc.reg)
                    stt(o_, tB[:, 0:R - 1], fw(2), o_, ALU.mult, ALU.add)
                act(accA, accA, AF.Gelu_apprx_tanh)
                tt(V, mj[j], accA, accB, ALU.mult)
            def ffn_stream(s):
                cols = [ab * 22 + j for j in range(s, 22, 2) for ab in (0, 1)]
                wbs = [load_w_cols(w_up, l, cols.pop(0) * 128, pool=(2 * s, 2 * s + 1, 4 + s))]
                wbs.append(load_w_cols(w_up, l, cols.pop(0) * 128, pool=(2 * s, 2 * s + 1, 4 + s)))
                for j in range(s, 22, 2):
                    ffn_j(j, s, wbs, cols)
            S.emit_merged([S.record(lambda: ffn_stream(0)), S.record(lambda: ffn_stream(1))])
            def down(fo, s):
                wks = []
                for k0 in range(0, 22, 8):
                    kn = min(8, 22 - k0)
                    wks.append((k0, kn, load_w_cols(w_down, l, fo * 128, row0=k0 * 128, nk=kn, pool=(2 * s, 2 * s + 1, 4 + s))))
                for tq in range(NTH):
                    bk = bank[s * 4 + tq % 4]
                    items = []
                    for (k0, kn, wbk) in wks:
                        for kk_ in range(kn):
                            j = k0 + kk_
                            items.append((bk.ap, wbk[:, kk_ * 128:(kk_ + 1) * 128], mj[j][:, tq * 512:(tq + 1) * 512], j == 0, j == 21))
                    mms(items, bk.reg)
                    xs = xres[fo][:, tb0 + tq * 512:tb0 + (tq + 1) * 512]
                    stt(xs, bk, MOD(l, 40 + fo), xs, ALU.mult, ALU.add)
            def down_stream(s):
                for fo in range(s, 8, 2):
                    down(fo, s)
            S.emit_merged([S.record(lambda: down_stream(0)), S.record(lambda: down_stream(1))])
        AR.release(m_layer)

    S.mark("end layers")
    LNF = ppB[:, PC["lnf"]:PC["lnf"] + 8]
    sqb = [AR.b(8 * 512) for _ in range(2)]; rs = [AR.f(512) for _ in range(2)]; obs = [AR.f(512) for _ in range(8)]
    for tq in range(NT):
        tsl = slice(tq * 512, (tq + 1) * 512)
        rms_rstd(tq, sqb[tq % 2], rs[tq % 2])
        for c in range(8):
            ob = obs[c]
            stt(ob, xres[c][:, tsl], LNF[:, c:c + 1], rs[tq % 2], ALU.mult, ALU.mult)
            S.dma("sync", yT[c][:, tsl], ob.ap, reads=[ob.reg], is_output=True)
    S.mark("final")
    build.last_sched = S
    S.finish()
    es.close()
    return nc, AR.hw


def _colmat(v, n):
    return np.ascontiguousarray(np.swapaxes(v.reshape(v.shape[:-1] + (n, 128)), -1, -2))


def _consts():
    p = np.arange(128)[:, None] % 64; f = np.arange(512)[None, :] % 64
    ml = (f < p); mu = (f > p); mli = (f <= p); mui = (f >= p); idt = (f == p)
    mrf = np.broadcast_to(f != 0, (128, 512)); mrb = np.broadcast_to(f != 63, (128, 512))
    cst = np.concatenate([ml, mu, mli, mui, idt, mrf, mrb], axis=1).astype(np.float32)
    blk = (np.arange(128)[:, None] // 64) == (np.arange(128)[None, :] // 64)
    cstf = np.concatenate([blk / 64.0, blk * 1.0], axis=1).astype(np.float32)
    return cst, cstf


def make_in_maps(inp, T, NL, roles):
    R64 = T // 64
    cst, cstf = _consts()
    L = NL
    pp = np.zeros((L, 128, NPC), np.float32)
    def put(name, arr):
        pp[:, :, PC[name]:PC[name] + arr.shape[-1]] = arr
    put("ln1g", _colmat(inp["ln1_g"][:L], 8)); put("ln2g", _colmat(inp["ln2_g"][:L], 8)); put("bmod", _colmat(inp["b_mod"][:L], 48))
    dcat = lambda a: np.concatenate([_colmat(a[:L, 0], 4), _colmat(a[:L, 1], 4)], axis=-1)
    put("w0", dcat(inp["rw_w0"])); put("a0", dcat(inp["rw_a0"]))
    put("k_k", _colmat(inp["rw_k_k"][:L], 4)); put("k_a", _colmat(inp["rw_k_a"][:L], 4))
    put("r_k", _colmat(inp["rw_r_k"][:L].reshape(L, 512), 4))
    put("lnxg", _colmat(inp["rw_lnx_g"][:L], 4)); put("lnxb", _colmat(inp["rw_lnx_b"][:L], 4))
    put("cw", np.concatenate([_colmat(inp["lru_conv_w"][:L, j], 4) for j in range(4)], axis=-1))
    put("cb", _colmat(inp["lru_conv_b"][:L], 4))
    put("gab", dcat(inp["lru_ga_b"])); put("gxb", dcat(inp["lru_gx_b"])); put("lam", dcat(inp["lru_lam"]))
    put("fw", np.concatenate([_colmat(inp["ffn_conv_w"][:L, j], 44) for j in range(3)], axis=-1))
    put("fb", _colmat(inp["ffn_conv_b"][:L], 44))
    put("lnf", np.broadcast_to(_colmat(inp["lnf_g"], 8), (L, 128, 8)))
    shared = {"pp": pp, "cst": cst, "cstf": cstf}
    for k in ("w_mod", "w_in", "w_out", "w_up", "w_down", "rw_w_up", "rw_a_up", "rw_g_up", "lru_ga_w", "lru_gx_w"):
        shared[k] = np.ascontiguousarray(inp[k][:L])
    maps = []
    for role in roles:
        m = dict(shared)
        fl = np.zeros((128, 2 + 2 * R64), np.float32)
        if role[0] == "P":
            x = np.concatenate([inp["x_prompt"][s] for s in role[1]], axis=0)
            cvec = inp["c_ctx"]
            A0 = np.zeros((L, 4, 128, 128), np.float32); h0 = np.zeros((L, 128, 8), np.float32)
            r = np.arange(R64)
            fl[:, 2:2 + R64 - 1] = ((r[1:] % 4) != 0).astype(np.float32)[None, :]
            fl[:, 2 + R64:2 + 2 * R64 - 1] = (((r[:-1] + 1) % 4) != 0).astype(np.float32)[None, :]
        else:
            b = role[1]
            x = inp["x_sample"][b]
            cvec = inp["c"][b]
            st = inp["state_rwkv"][b][:L]
            A0 = np.ascontiguousarray(st.reshape(L, 2, 4, 2, 64, 64).transpose(0, 2, 3, 5, 1, 4).reshape(L, 4, 128, 128))
            h0 = dcat(inp["state_lru"][b][None].transpose(1, 0, 2, 3).reshape(L, 1, 2, 512)[:, 0][:, None].repeat(1, 1).reshape(L, 2, 512)[:, :, :].reshape(L, 2, 512)) if False else \
                np.concatenate([_colmat(inp["state_lru"][b][:L, 0], 4), _colmat(inp["state_lru"][b][:L, 1], 4)], axis=-1)
            fl[:, 0] = 1.0
        m["xT"] = np.ascontiguousarray(x.T.reshape(8, 128, T))
        m["cv"] = _colmat(cvec, 8)
        m["A0"] = A0.astype(np.float32); m["h0"] = np.ascontiguousarray(h0.astype(np.float32)); m["fl"] = fl
        maps.append(m)
    return maps


def assemble(results, roles, T, NL, n_prompt, n_sample):
    NSQ = T // 256
    yp = [None] * n_prompt; ys = [None] * n_sample
    nr = np.zeros((n_prompt, NL, 2, 8, 64, 64), np.float32); nl_ = np.zeros((n_prompt, NL, 2, 512), np.float32)
    for res, role in zip(results, roles):
        y = np.asarray(res["yT"]).reshape(1024, T).T
        if role[0] == "P":
            So = np.asarray(res["So"]).reshape(NL, 4, NSQ, 2, 64, 2, 64)
            ho = np.asarray(res["ho"]).reshape(NL, 4, 128, 2, NSQ)
            for qi, s in enumerate(role[1]):
                yp[s] = y[qi * 256:(qi + 1) * 256]
                for d in range(2):
                    qb = qi if d == 0 else NSQ - 1 - qi
                    blk = So[:, :, qb, :, :, d, :]
                    nr[s, :, d] = blk.transpose(0, 1, 2, 4, 3).reshape(NL, 8, 64, 64)
                    nl_[s, :, d] = ho[:, :, :, d, qi].reshape(NL, 512)
        else:
            ys[role[1]] = y
    return np.stack(yp), np.stack(ys), nr, nl_


def kernel(**inputs):
    inp = {k: np.asarray(v, dtype=np.float32) for k, v in inputs.items()}
    T, NL = 2048, 4
    roles = [("P", list(range(8 * i, 8 * i + 8))) for i in range(4)] + [("S", b) for b in range(4)]
    nc, _ = build(T, NL)
    maps = make_in_maps(inp, T, NL, roles)
    res = run_bass_kernel_spmd(nc, maps, core_ids=list(range(8)))
    yp, ys, nr, nl_ = assemble(res.results, roles, T, NL, 32, 4)
    return (yp.astype(np.float32), ys.astype(np.float32), nr.astype(np.float32), nl_.astype(np.float32))
```

```python
import numpy as np
from contextlib import ExitStack
import concourse.bass as bass
import concourse.mybir as mybir
from concourse.bass_utils import run_bass_kernel_spmd

F32 = mybir.dt.float32
BF16 = mybir.dt.bfloat16
AF = mybir.ActivationFunctionType
ALU = mybir.AluOpType
ENGS = ["tensor", "vector", "scalar", "gpsimd", "sync"]

D = 1024; WA = 512; PIN = 2816; DFF = 2816; C = 64
EPS = 1e-6; LNX_EPS = 64e-5; KK_EPS = 1e-12; DSC = 0.6065306597126334

def _cols():
    o = {}; n = 0
    def add(name, k):
        nonlocal n
        o[name] = n; n += k
    add("ln1g", 8); add("ln2g", 8); add("bmod", 48)
    add("w0", 8); add("a0", 8); add("k_k", 4); add("k_a", 4); add("r_k", 4); add("lnxg", 4); add("lnxb", 4)
    add("cw", 16); add("cb", 4); add("gab", 8); add("gxb", 8); add("lam", 8)
    add("fw", 132); add("fb", 44); add("lnf", 8)
    return o, n
PC, NPC = _cols()


class Reg:
    __slots__ = ("w", "r", "parents")
    def __init__(self, parents=None):
        self.w = None; self.r = {}; self.parents = parents

    def resolve(self):
        if self.parents:
            ps = self.parents; self.parents = None
            for p in ps:
                p.resolve()
                if p.w is not None and self.r.get(p.w[0], 0) < p.w[1]: self.r[p.w[0]] = p.w[1]
                for k, v in p.r.items():
                    if self.r.get(k, 0) < v: self.r[k] = v


def _flat(regs):
    out = []
    for r in regs:
        if isinstance(r, (list, tuple)): out.extend(_flat(r))
        else: out.append(r)
    return out


class Sched:
    ROT = 12000
    def __init__(self, nc, es, n_dma_sems=16):
        self.nc = nc; self.es = es
        self.prog = {e: [] for e in ENGS}
        self.cnt = {e: 0 for e in ENGS}
        self.epoch = {e: 0 for e in ENGS}
        self.waited = {e: {} for e in ENGS}
        self.sems = {}
        self.n_dma = n_dma_sems
        self.dma_cnt = {}
        self.dma_next = {"sync": 0, "gpsimd": 0, "scalar": 0}
        for e in ENGS:
            self.sems[(e, 0)] = es.enter_context(nc.semaphore("s_%s0" % e))
        for q in ("sync", "gpsimd"):
            for i in range(n_dma_sems):
                self.sems[("dma", q, i)] = es.enter_context(nc.semaphore("s_dma_%s%d" % (q, i)))
                self.dma_cnt[("dma", q, i)] = 0
        self.out_tokens = []
        self.last_tok = {}
        self.rec = None
        self.t_eng = {e: 0.0 for e in ENGS}
        self.t_w = {}; self.t_r = {}
        self.marks = []; self.busy = {e: 0.0 for e in ENGS}

    def _deps(self, reads, writes):
        for r in reads: r.resolve()
        for w in writes: w.resolve()
        deps = {}
        def add(tok):
            if tok is None: return
            k, v = tok
            if deps.get(k, 0) < v: deps[k] = v
        for r in reads: add(r.w)
        for w in writes:
            add(w.w)
            for t in w.r.items(): add(t)
        return deps

    def _emit_waits(self, eng, deps):
        for k, v in deps.items():
            if self.waited[eng].get(k, 0) >= v: continue
            self.waited[eng][k] = v
            sem = self.sems[k]
            self.prog[eng].append(lambda e, sem=sem, v=v: e.wait_ge(sem, v))

    def _mark(self, tok, reads, writes):
        for r in reads:
            if r.r.get(tok[0], 0) < tok[1]: r.r[tok[0]] = tok[1]
        for w in writes:
            w.w = tok; w.r = {}

    def _next_tok(self, eng):
        if self.cnt[eng] >= self.ROT:
            self.epoch[eng] += 1; self.cnt[eng] = 0
            self.sems[(eng, self.epoch[eng])] = self.es.enter_context(self.nc.semaphore("s_%s%d" % (eng, self.epoch[eng])))
        self.cnt[eng] += 1
        key = (eng, self.epoch[eng])
        self.last_tok[eng] = (key, self.cnt[eng])
        return (key, self.cnt[eng])

    def mark(self, name):
        self.marks.append((name, max(self.t_eng.values()), dict(self.busy)))

    def record(self, body):
        assert self.rec is None
        self.rec = []
        body()
        r = self.rec; self.rec = None
        return r

    def _est(self, kind, args):
        if kind == "op":
            eng, fn, reads, writes, cost = args
            return eng, reads, writes, cost, cost
        if kind == "mm_group":
            fns, reads, writes, cost = args
            return "tensor", reads, writes, cost, cost
        eng, o, a, reads, writes, is_out, cost = args
        return eng, reads, writes, cost, 0.15

    def _start_time(self, est):
        eng, reads, writes, cost, occ = est
        t = 0.0
        for r in reads:
            t = max(t, self.t_w.get(id(r), 0.0))
        for w in writes:
            t = max(t, self.t_w.get(id(w), 0.0), self.t_r.get(id(w), 0.0))
        return max(t + 0.12, self.t_eng[eng]), t

    def _account(self, est):
        eng, reads, writes, cost, occ = est
        st, _ = self._start_time(est)
        end = st + cost
        self.busy[eng] += occ
        self.t_eng[eng] = st + occ
        for r in reads:
            if self.t_r.get(id(r), 0.0) < end: self.t_r[id(r)] = end
        for w in writes:
            self.t_w[id(w)] = end

    def emit_merged(self, recs, weights=None):
        import os
        eps = float(os.environ.get("KEPS", "0.2"))
        idx = [0] * len(recs)
        rem = [sum(self._est(k, a)[3] for k, a in rl) for rl in recs]
        while True:
            cands = []
            for si, rl in enumerate(recs):
                if idx[si] >= len(rl): continue
                kind, args = rl[idx[si]]
                st, rdy = self._start_time(self._est(kind, args))
                cands.append((st, rdy, si))
            if not cands: break
            mn = min(c[0] for c in cands)
            if eps > 0:
                pool = [c for c in cands if c[0] <= mn + eps]
                si = max(pool, key=lambda c: (rem[c[2]], -c[2]))[2]
            else:
                si = min(cands)[2]
            kind, args = recs[si][idx[si]]; idx[si] += 1
            rem[si] -= self._est(kind, args)[3]
            getattr(self, kind)(*args)

    def op(self, eng, fn, reads=(), writes=(), cost=0.5):
        reads = _flat(reads); writes = _flat(writes)
        if self.rec is not None:
            self.rec.append(("op", (eng, fn, reads, writes, cost))); return
        self._account((eng, reads, writes, cost, cost))
        deps = self._deps(reads, writes)
        self._emit_waits(eng, deps)
        tok = self._next_tok(eng)
        sem = self.sems[tok[0]]
        self.prog[eng].append(lambda e, fn=fn, sem=sem: fn(e).then_inc(sem, 1))
        self._mark(tok, reads, writes)

    def mm_group(self, fns, reads=(), writes=(), cost=0.5):
        reads = _flat(reads); writes = _flat(writes)
        if self.rec is not None:
            self.rec.append(("mm_group", (fns, reads, writes, cost))); return
        self._account(("tensor", reads, writes, cost, cost))
        deps = self._deps(reads, writes)
        self._emit_waits("tensor", deps)
        tok = self._next_tok("tensor")
        sem = self.sems[tok[0]]
        for fn in fns[:-1]:
            self.prog["tensor"].append(lambda e, fn=fn: fn(e))
        fn = fns[-1]
        self.prog["tensor"].append(lambda e, fn=fn, sem=sem: fn(e).then_inc(sem, 1))
        self._mark(tok, reads, writes)

    def dma(self, eng, out_ap, in_ap, reads=(), writes=(), is_output=False, cost=3.0):
        reads = _flat(reads); writes = _flat(writes)
        if self.rec is not None:
            self.rec.append(("dma", (eng, out_ap, in_ap, reads, writes, is_output, cost))); return
        self._account((eng, reads, writes, cost, 0.15))
        i = self.dma_next[eng]
        self.dma_next[eng] = (i + 1) % self.n_dma
        key = ("dma", eng, i)
        deps = self._deps(reads, writes)
        if self.dma_cnt[key] > 0 and deps.get(key, 0) < self.dma_cnt[key]:
            deps[key] = self.dma_cnt[key]
        self._emit_waits(eng, deps)
        self.dma_cnt[key] += 16
        tok = (key, self.dma_cnt[key])
        sem = self.sems[key]
        self.prog[eng].append(lambda e, o=out_ap, a=in_ap, sem=sem: e.dma_start(out=o, in_=a).then_inc(sem, 16))
        self._mark(tok, reads, writes)
        if is_output: self.out_tokens.append(tok)

    def barrier(self):
        deps = {}
        for e in ENGS:
            if e in self.last_tok:
                k, v = self.last_tok[e]; deps[k] = v
        for key, v in self.dma_cnt.items():
            if v > 0: deps[key] = v
        for e in ENGS:
            self._emit_waits(e, dict(deps))

    def finish(self):
        deps = {}
        for k, v in self.out_tokens:
            deps[k] = max(deps.get(k, 0), v)
        for e in ENGS:
            if e in self.last_tok:
                k, v = self.last_tok[e]; deps[k] = max(deps.get(k, 0), v)
        self._emit_waits("sync", deps)
        with self.nc.Block() as block:
            @block.tensor
            def _(e):
                for f in self.prog["tensor"]: f(e)
            @block.vector
            def _(e):
                for f in self.prog["vector"]: f(e)
            @block.scalar
            def _(e):
                for f in self.prog["scalar"]: f(e)
            @block.gpsimd
            def _(e):
                for f in self.prog["gpsimd"]: f(e)
            @block.sync
            def _(e):
                for f in self.prog["sync"]: f(e)


class Buf:
    __slots__ = ("ap", "reg")
    def __init__(self, ap, reg=None):
        self.ap = ap; self.reg = reg if reg is not None else Reg()
    def __getitem__(self, idx):
        return Buf(self.ap[idx], self.reg)
    def v(self, ap):
        return Buf(ap, self.reg)


class Arena:
    def __init__(self, tf, tb, cap_f32):
        self.tf = tf; self.tb = tb; self.cap = cap_f32; self.off = 0; self.hw = 0
        self.ents = []
    def _reg(self, o, e):
        par = [r for (s_, e_, r) in self.ents if s_ < e and o < e_]
        self.ents = [(s_, e_, r) for (s_, e_, r) in self.ents if not (o <= s_ and e_ <= e)]
        reg = Reg(parents=par if par else None)
        self.ents.append((o, e, reg))
        return reg
    def f(self, n):
        o = self.off; self.off += n
        assert self.off <= self.cap, ("arena overflow", self.off, self.cap)
        self.hw = max(self.hw, self.off)
        return Buf(self.tf[:, o:o + n], self._reg(o, o + n))
    def b(self, n):
        o = self.off; self.off += (n + 1) // 2
        assert self.off <= self.cap, ("arena overflow", self.off, self.cap)
        self.hw = max(self.hw, self.off)
        return Buf(self.tb[:, 2 * o:2 * o + n], self._reg(o, self.off))
    def mark(self):
        return self.off
    def release(self, m):
        self.off = m


import os
OPT = set(os.environ.get("KOPT", "qact,mmk,mmb,uact").split(","))
def build(T, NL, SEG=256, ARENA_F32=17408):
    NT = T // 512; NCH = T // C; NSEG = T // SEG; CPS = SEG // C; R64 = T // 64; NSQ = T // 256
    assert SEG == 256
    nc = bass.Bass("TRN2", target_bir_lowering=False)
    dt_in = lambda n, s: nc.dram_tensor(n, s, F32, kind="ExternalInput").ap()
    dt_out = lambda n, s: nc.dram_tensor(n, s, F32, kind="ExternalOutput").ap()
    xT = dt_in("xT", [8, 128, T]); cv = dt_in("cv", [128, 8]); pp_d = dt_in("pp", [NL, 128, NPC])
    A0_d = dt_in("A0", [NL, 4, 128, 128]); h0_d = dt_in("h0", [NL, 128, 8]); fl_d = dt_in("fl", [128, 2 + 2 * R64])
    cst_d = dt_in("cst", [128, 7 * 512]); cstf_d = dt_in("cstf", [128, 256])
    w_mod = dt_in("w_mod", [NL, D, 6 * D]); w_in = dt_in("w_in", [NL, D, PIN]); w_out = dt_in("w_out", [NL, D, D])
    w_up = dt_in("w_up", [NL, D, 2 * DFF]); w_down = dt_in("w_down", [NL, DFF, D])
    rw_wup = dt_in("rw_w_up", [NL, 2, 64, WA]); rw_aup = dt_in("rw_a_up", [NL, 2, 64, WA]); rw_gup = dt_in("rw_g_up", [NL, 128, WA])
    ga_w = dt_in("lru_ga_w", [NL, 2, 8, 64, 64]); gx_w = dt_in("lru_gx_w", [NL, 2, 8, 64, 64])
    yT = dt_out("yT", [8, 128, T]); So_d = dt_out("So", [NL, 4, NSQ, 128, 128]); ho_d = dt_out("ho", [NL, 4, 128, 2 * NSQ])

    es = ExitStack()
    S = Sched(nc, es)
    sbt = lambda n, s, d: es.enter_context(nc.sbuf_tensor("sb_" + n, s, d))
    xres_t = sbt("xres", [128, 8 * T], F32); xres = [Buf(xres_t[:, c * T:(c + 1) * T]) for c in range(8)]
    hbf_t = sbt("hbf", [128, 8 * T], BF16); hbf = [Buf(hbf_t[:, c * T:(c + 1) * T]) for c in range(8)]
    ar_t = sbt("arena", [128, ARENA_F32], F32)
    AR = Arena(ar_t, ar_t.bitcast(BF16), ARENA_F32)
    cst_t = sbt("cst", [128, 7 * 512], BF16); cstB = Buf(cst_t[:, :])
    ML, MU, MLI, MUI, IDT, MRF, MRB = [cstB[:, i * 512:(i + 1) * 512] for i in range(7)]
    cstf_t = sbt("cstf", [128, 256], F32); cstF = Buf(cstf_t[:, :]); BD64 = cstF[:, 0:128]; BD1 = cstF[:, 128:256]
    onesm = Buf(sbt("onesm", [128, 128], BF16)[:, :])
    ppB = Buf(sbt("ppt", [128, NL * NPC], F32)[:, :])
    der_t = sbt("der", [128, NL * 48], F32); derL = [Buf(der_t[:, l * 48:(l + 1) * 48]) for l in range(NL)]
    mod_t = sbt("modt", [128, NL * 48], F32); modL = [Buf(mod_t[:, l * 48:(l + 1) * 48]) for l in range(NL)]
    flB = Buf(sbt("flt", [128, 2 + 2 * R64], F32)[:, :])
    cvB = Buf(sbt("cvt", [128, 8], F32)[:, :]); cvbf = Buf(sbt("cvbf", [128, 8], BF16)[:, :])
    wsm_t = sbt("wsm", [128, 5 * 512], BF16); wsm = Buf(wsm_t[:, :])
    bdw = Buf(sbt("bdw", [128, 16 * 128], BF16)[:, :])
    wst = [Buf(sbt("wst%d" % i, [128, 8 * 128], BF16)[:, :]) for i in range(4)]
    woutb = [Buf(sbt("wout%d" % i, [128, 1024], BF16)[:, :]) for i in range(2)]
    wst = wst + woutb
    A32 = Buf(sbt("A32", [128, 128], F32)[:, :]); Abf = Buf(sbt("Abf", [128, 128], BF16)[:, :])
    tmpA = Buf(sbt("tmpA", [128, 128], F32)[:, :])
    Xbf = Buf(sbt("Xbf", [128, 128], BF16)[:, :]); Ubf = Buf(sbt("Ubf", [128, 128], BF16)[:, :])
    stg = [Buf(sbt("stg%d" % i, [128, 128], F32)[:, :]) for i in range(2)]
    h0t = Buf(sbt("h0t", [128, 8], F32)[:, :])
    hfin_d = [Buf(sbt("hfin%d" % d, [128, NSQ], F32)[:, :]) for d in range(2)]
    hini_d = [Buf(sbt("hini%d" % d, [128, NSQ], F32)[:, :]) for d in range(2)]
    tbs = [Buf(sbt("tb%d" % i, [128, R64], F32)[:, :]) for i in range(4)]
    wc_t = sbt("wct", [128, 6 * 4], F32)
    WC = [[Buf(wc_t[:, (s3 * 2 + d) * 4:(s3 * 2 + d) * 4 + 4]) for d in range(2)] for s3 in range(3)]
    ps_t = es.enter_context(nc.psum_tensor("ps", [128, 4096], F32))
    bank = [Buf(ps_t[:, b * 512:(b + 1) * 512]) for b in range(8)]

    V, G, A_ = "vector", "gpsimd", "scalar"
    def fd(buf):
        n = 1
        for st_, c_ in list(buf.ap.ap)[1:]:
            n *= int(c_)
        return n
    def ecost(eng, out):
        n = fd(out)
        if eng == V: return 0.25 + n * 0.00105
        if eng == G: return 0.35 + n * 0.0017
        return 0.25 + n * 0.00085
    def tt(eng, out, a, b, op):
        S.op(eng, lambda e: e.tensor_tensor(out=out.ap, in0=a.ap, in1=b.ap, op=op), reads=[a.reg, b.reg], writes=[out.reg], cost=ecost(eng, out))
    def ts(eng, out, a, s1, s2=None, op0=ALU.mult, op1=None):
        rd = [a.reg] + [x.reg for x in (s1, s2) if isinstance(x, Buf)]
        a1 = s1.ap if isinstance(s1, Buf) else s1
        a2 = s2.ap if isinstance(s2, Buf) else s2
        if op1 is None and eng == G and op0 == ALU.mult:
            S.op(eng, lambda e: e.tensor_scalar(out=out.ap, in0=a.ap, scalar1=a1, scalar2=0.0, op0=ALU.mult, op1=ALU.add), reads=rd, writes=[out.reg], cost=ecost(eng, out))
        elif op1 is None:
            S.op(eng, lambda e: e.tensor_scalar(out=out.ap, in0=a.ap, scalar1=a1, scalar2=None, op0=op0), reads=rd, writes=[out.reg], cost=ecost(eng, out))
        else:
            S.op(eng, lambda e: e.tensor_scalar(out=out.ap, in0=a.ap, scalar1=a1, scalar2=a2, op0=op0, op1=op1), reads=rd, writes=[out.reg], cost=ecost(eng, out))
    def stt(out, a, s, b, op0, op1):
        rd = [a.reg, b.reg] + ([s.reg] if isinstance(s, Buf) else [])
        sa = s.ap if isinstance(s, Buf) else s
        S.op(V, lambda e: e.scalar_tensor_tensor(out=out.ap, in0=a.ap, scalar=sa, in1=b.ap, op0=op0, op1=op1), reads=rd, writes=[out.reg], cost=ecost(V, out))
    def act(out, a, func, scale=1.0, bias=0.0):
        rd = [a.reg] + [x.reg for x in (scale, bias) if isinstance(x, Buf)]
        sc = scale.ap if isinstance(scale, Buf) else scale
        bi = bias.ap if isinstance(bias, Buf) else bias
        S.op(A_, lambda e: e.activation(out=out.ap, in_=a.ap, func=func, bias=bi, scale=sc), reads=rd, writes=[out.reg], cost=ecost(A_, out))
    def cp(eng, out, a):
        if eng == A_:
            act(out, a, AF.Copy)
        else:
            S.op(eng, lambda e: e.tensor_copy(out=out.ap, in_=a.ap), reads=[a.reg], writes=[out.reg], cost=ecost(eng, out))
    def recip(out, a):
        S.op(V, lambda e: e.reciprocal(out=out.ap, in_=a.ap), reads=[a.reg], writes=[out.reg])
    def scan(out, d0, d1, init):
        ia = init.ap if isinstance(init, Buf) else init
        rd = [d0.reg, d1.reg] + ([init.reg] if isinstance(init, Buf) else [])
        S.op(V, lambda e: e.tensor_tensor_scan(out=out.ap, data0=d0.ap, data1=d1.ap, initial=ia, op0=ALU.mult, op1=ALU.add), reads=rd, writes=[out.reg], cost=0.3 + fd(out) * 0.0021)
    def mms(items, out_reg):
        rd = []
        fns = []
        cost = 0.0
        for (o, l_, r_, st, sp) in items:
            n_ = fd(r_)
            cost += max(0.045, n_ * 0.00052) * (4.0 if r_.ap.tensor.dtype == F32 else 1.0)
            rd.append(l_.reg); rd.append(r_.reg)
            fns.append(lambda e, o=o, l_=l_, r_=r_, st=st, sp=sp: e.matmul(o, lhsT=l_.ap, rhs=r_.ap, start=st, stop=sp))
        rdd = {}
        for r in _flat(rd): rdd[id(r)] = r
        S.mm_group(fns, reads=list(rdd.values()), writes=[out_reg], cost=cost)
    def ld(out, src, eng="sync"):
        S.dma(eng, out.ap, src, writes=[out.reg])
    def memset(eng, out, val):
        S.op(eng, lambda e: e.memset(out.ap, val), writes=[out.reg])
    def rev(b):
        return Buf(b.ap[:, ::-1], b.reg)
    def v3(b, w=64):
        return b.ap.rearrange("p (r w) -> p r w", w=w)

    ld(cstB, cst_d[:, :], "gpsimd"); ld(cstF, cstf_d[:, :]); ld(flB, fl_d[:, :]); ld(cvB, cv[:, :])
    for l in range(NL):
        ld(ppB[:, l * NPC:(l + 1) * NPC], pp_d[l])
    for c in range(8):
        ld(xres[c], xT[c])
    memset(V, onesm, 1.0 / D)
    keepf = flB[:, 0:1]
    cfl_prev = flB[:, 2:2 + R64]
    cfl_next = flB[:, 2 + R64:2 + 2 * R64]
    def P(l, name, j=0, n=1):
        o = l * NPC + PC[name] + j
        return ppB[:, o:o + n]
    def DR(l, j, n=1):
        return derL[l][:, j:j + n]
    def MOD(l, j, n=1):
        return modL[l][:, j:j + n]
    act(cvbf, cvB, AF.Silu)

    wst_i = {}
    def load_w_cols(wd, l, col0, row0=0, nk=8, pool=(0, 1, 2, 3)):
        i = wst_i.get(pool, 0); wst_i[pool] = i + 1
        wb = wst[pool[i % len(pool)]]
        S.dma("gpsimd", wb.ap[:, 0:nk * 128].rearrange("p (k n) -> p k n", k=nk),
              wd[l][row0:row0 + nk * 128, :].rearrange("(k p) n -> p k n", p=128)[:, :, col0:col0 + 128], writes=[wb.reg])
        return wb

    modps = bank[7][:, 256:512]
    mod_pool = [(0, 1, 2, 3)]
    def mod_cols(l, c0, c1):
        for col in range(c0, c1):
            wb = load_w_cols(w_mod, l, col * 128, pool=mod_pool[0])
            mms([(modps.ap[:, col:col + 1], wb[:, k * 128:(k + 1) * 128], cvbf[:, k:k + 1], k == 0, k == 7) for k in range(8)], modps.reg)
    def mod_fin(l):
        tt(V, MOD(l, 0, 48), modps[:, 0:48], P(l, "bmod", 0, 48), ALU.add)
        stt(DR(l, 0, 8), MOD(l, 8, 8), 1.0, P(l, "ln1g", 0, 8), ALU.add, ALU.mult)
        stt(DR(l, 8, 8), MOD(l, 32, 8), 1.0, P(l, "ln2g", 0, 8), ALU.add, ALU.mult)
        ts(V, DR(l, 16, 4), P(l, "k_a", 0, 4), -1.0, 1.0, ALU.mult, ALU.add)
        tt(V, DR(l, 20, 4), P(l, "k_a", 0, 4), P(l, "r_k", 0, 4), ALU.mult)
        ts(V, DR(l, 24, 4), P(l, "k_a", 0, 4), -2.0, 2.0, ALU.mult, ALU.add)
        tt(V, DR(l, 24, 4), DR(l, 24, 4), P(l, "r_k", 0, 4), ALU.mult)
        act(DR(l, 36, 8), P(l, "lam", 0, 8), AF.Exp, scale=-1.0)
        act(DR(l, 36, 8), DR(l, 36, 8), AF.Ln, bias=1.0)
        ts(V, DR(l, 28, 8), DR(l, 36, 8), -8.0, None, ALU.mult)
    mod_cols(0, 0, 48); mod_fin(0)

    def rms_rstd(tq, sqb, rs):
        tsl = slice(tq * 512, (tq + 1) * 512)
        for c in range(8):
            act(sqb[:, c * 512:(c + 1) * 512], xres[c][:, tsl], AF.Square)
        mms([(bank[0].ap, onesm, sqb[:, c * 512:(c + 1) * 512], c == 0, c == 7) for c in range(8)], bank[0].reg)
        act(rs, bank[0], AF.Ln, bias=EPS)
        act(rs, rs, AF.Exp, scale=-0.5)

    def rmsnorm_to_hbf(l, ggoff, shoff):
        m = AR.mark()
        sqw = AR.b(8 * T); rsw = AR.f(T); tmp = [AR.f(T) for _ in range(3)]
        for c in range(8):
            act(sqw[:, c * T:(c + 1) * T], xres[c], AF.Square)
        for tq in range(NT):
            tsl = slice(tq * 512, (tq + 1) * 512)
            bk = bank[tq % 4]
            mms([(bk.ap, onesm, sqw[:, c * T + tq * 512:c * T + (tq + 1) * 512], c == 0, c == 7) for c in range(8)], bk.reg)
            act(rsw[:, tsl], bk, AF.Ln, bias=EPS)
        act(rsw, rsw, AF.Exp, scale=-0.5)
        for c in range(8):
            tb = tmp[c % 3]
            stt(tb, xres[c], DR(l, ggoff + c), rsw, ALU.mult, ALU.mult)
            if c % 4 != 3:
                act(hbf[c], tb, AF.Identity, bias=MOD(l, shoff + c))
            else:
                ts(V, hbf[c], tb, MOD(l, shoff + c), None, ALU.add)
        AR.release(m)

    def proj_fm(wb, bk, tq):
        mms([(bk.ap, wb[:, k * 128:(k + 1) * 128], hbf[k][:, tq * 512:(tq + 1) * 512], k == 0, k == 7) for k in range(8)], bk.reg)

    wout_i = [0]
    def xres_evac(l, fo, tq, bk, bi, tmps):
        xs = xres[fo][:, tq * 512:(tq + 1) * 512]
        if tmps is not None and bi % 3 == 2:
            tm = tmps[(bi // 3) % len(tmps)]
            act(tm, bk, AF.Identity, scale=MOD(l, 16 + fo))
            tt(G, xs, xs, tm, ALU.add)
        else:
            stt(xs, bk, MOD(l, 16 + fo), xs, ALU.mult, ALU.add)

    def wout_accum(l, kc, ycur, banks, tmps=None):
        wb = woutb[wout_i[0] % 2]; wout_i[0] += 1
        S.dma("gpsimd", wb.ap, w_out[l][kc * 128:(kc + 1) * 128, :], writes=[wb.reg])
        bi = 0
        for fo in range(8):
            for tq in range(NT):
                bk = banks[bi % len(banks)]
                mms([(bk.ap, wb[:, fo * 128:(fo + 1) * 128], ycur[:, tq * 512:(tq + 1) * 512], True, True)], bk.reg)
                xres_evac(l, fo, tq, bk, bi, tmps)
                bi += 1

    def hblk(hh):
        return slice(hh * 64, (hh + 1) * 64)

    for l in range(NL):
        S.mark("L%d start" % l)
        rmsnorm_to_hbf(l, 0, 0)
        S.mark("L%d norm1" % l)
        for d in range(2):
            S.dma("gpsimd", wsm.ap[0:64, d * 512:(d + 1) * 512], rw_wup[l, d], writes=[wsm.reg])
            S.dma("gpsimd", wsm.ap[64:128, (2 + d) * 512:(3 + d) * 512], rw_aup[l, d], writes=[wsm.reg])
        S.dma("gpsimd", wsm.ap[:, 4 * 512:5 * 512], rw_gup[l], writes=[wsm.reg])
        memset(G, bdw, 0.0)
        for gt, gw in enumerate((ga_w, gx_w)):
            for d in range(2):
                for c in range(4):
                    o = ((gt * 2 + d) * 4 + c) * 128
                    for hh in range(2):
                        S.dma("gpsimd", bdw.ap[hblk(hh), o + hh * 64:o + (hh + 1) * 64], gw[l, d, 2 * c + hh], writes=[bdw.reg])
        WUP = lambda d, c: wsm[0:64, d * 512 + c * 128:d * 512 + (c + 1) * 128]
        AUP = lambda d, c: wsm[64:128, (2 + d) * 512 + c * 128:(2 + d) * 512 + (c + 1) * 128]
        GUP = lambda c: wsm[:, 4 * 512 + c * 128:4 * 512 + (c + 1) * 128]

        m_layer = AR.mark()
        txw = AR.b(T)
        wb = load_w_cols(w_in, l, 3 * WA)
        for tq in range(NT):
            bk = bank[6 + tq % 2]; tsl = slice(tq * 512, (tq + 1) * 512)
            proj_fm(wb, bk, tq)
            act(txw[0:64, tsl], bk[0:64, :], AF.Tanh)
            act(txw[64:128, tsl], bk[64:128, :], AF.Copy)

        m_wkv = AR.mark()
        pending = []
        for c in range(4):
            AR.release(m_wkv)
            rbf = AR.b(T); kbf = AR.b(T); oacc = AR.f(T); Vtm = AR.b(T)
            def raws():
                for (dst, col0) in ((rbf, c * 128), (kbf, WA + c * 128)):
                    wb = load_w_cols(w_in, l, col0, pool=(0, 1))
                    for tq in range(NT):
                        bk = bank[tq % 2]
                        proj_fm(wb, bk, tq)
                        cp(A_ if tq % 2 == 0 else V, dst[:, tq * 512:(tq + 1) * 512], bk)
                wv = load_w_cols(w_in, l, 2 * WA + c * 128, pool=(0, 1))
                for n0 in range(0, NCH, 8):
                    bk = bank[2 + (n0 // 8) % 2]
                    items = []
                    for n in range(n0, n0 + 8):
                        for hh in range(2):
                            for k in range(8):
                                items.append((bk.ap[hblk(hh), (n - n0) * 64:(n - n0 + 1) * 64],
                                              hbf[k][:, n * 64:(n + 1) * 64], wv[:, k * 128 + hh * 64:k * 128 + (hh + 1) * 64], k == 0, k == 7))
                    mms(items, bk.reg)
                    cp(A_, Vtm[:, n0 * 64:(n0 + 8) * 64], bk)
                ld(A32, A0_d[l, c])
                cp(A_, Abf, A32)
            S.emit_merged(pending + [S.record(raws)])
            pending = []
            m_seg = AR.mark()
            OPD3 = [[[AR.b(SEG) for _ in range(7)] for _ in range(2)] for _ in range(3)]
            TTD = [[AR.b(SEG) for _ in range(2)] for _ in range(2)]
            PQA = [[[AR.b(SEG) for _ in range(2)] for _ in range(2)] for _ in range(2)]
            PQB = [[AR.b(SEG) for _ in range(3)] for _ in range(2)]
            KB = [[AR.b(SEG) for _ in range(2)] for _ in range(2)]
            TMPS = [[AR.f(SEG) for _ in range(6)] for _ in range(2)]

            def blockmm(lt, rh, bk, rhs_fixed=None):
                items = []
                for n in range(CPS):
                    for hh in range(2):
                        hs = hblk(hh); ns = slice(n * 64, (n + 1) * 64)
                        r_ = rh[hs, ns] if rhs_fixed is None else rhs_fixed[hs, 0:64]
                        items.append((bk.ap[hs, ns], lt[hs, ns], r_, True, True))
                mms(items, bk.reg)

            def prepA(d, sg):
                seg = sg if d == 0 else NSEG - 1 - sg
                t0 = seg * SEG
                tsl = slice(t0, t0 + SEG)
                RT, QT, Ktm, Btm, LkT, MkT, MbT = OPD3[sg % 3][d]
                KT, BT = KB[d]
                Pa, Qa = PQA[sg % 2][d]
                sgm, cum, aa, t3, t4, Ep = TMPS[d]
                pk = bank[d][:, 0:SEG]
                mms([(pk.ap, WUP(d, c), txw[0:64, tsl], True, True)], pk.reg)
                act(sgm, pk, AF.Sigmoid, bias=P(l, "w0", d * 4 + c))
                mms([(pk.ap, AUP(d, c), txw[64:128, tsl], True, True)], pk.reg)
                act(aa, pk, AF.Sigmoid, bias=P(l, "a0", d * 4 + c))
                if d == 0:
                    scan(cum, MRF[:, 0:SEG], sgm, 0.0)
                else:
                    scan(rev(cum), rev(MRB[:, 0:SEG]), rev(sgm), 0.0)
                tt(G, sgm, cum, sgm, ALU.subtract)
                act(Ep, cum, AF.Exp, scale=-DSC)
                act(cum, cum, AF.Exp, scale=DSC)
                act(sgm, sgm, AF.Exp, scale=-DSC)
                Em = cum; Ex = sgm
                c0 = 63 if d == 0 else 0
                cp(G, WC[sg % 3][d], Buf(v3(Ep)[:, :, c0], Ep.reg))
                tt(G if "rtp" in OPT else V, RT, rbf[:, tsl], Ep, ALU.mult)
                ts(V, t3, aa, P(l, "k_a", c), DR(l, 16 + c), ALU.mult, ALU.add)
                tt(G, t3, t3, kbf[:, tsl], ALU.mult)
                tt(V, KT, t3, Em, ALU.mult)
                ts(G if "t4p" in OPT else V, t4, kbf[:, tsl], P(l, "k_k", c), None, ALU.mult)
                act(t3, t4, AF.Square)
                mms([(pk.ap, BD1, t3, True, True)], pk.reg)
                act(t3, pk, AF.Ln, bias=KK_EPS)
                act(t3, t3, AF.Exp, scale=-0.5)
                tt(V, t4, t4, t3, ALU.mult)
                tt(G, QT, t4, Ex, ALU.mult)
                tt(G if "bp" in OPT else V, t4, t4, aa, ALU.mult)
                stt(BT, t4, -1.0, Em, ALU.mult, ALU.mult)
                mN, mNT, mMT = (ML, MU, MUI) if d == 0 else (MU, ML, MLI)
                def score(lt, rh, mask, dst, alt=False):
                    blockmm(lt, rh, pk)
                    if alt:
                        cp(A_, dst, pk)
                        tt(G, dst, dst, mask[:, 0:SEG], ALU.mult)
                    else:
                        tt(V, dst, pk, mask[:, 0:SEG], ALU.mult)
                score(QT, BT, mN, Pa, "mpq" in OPT)
                score(BT, QT, mNT, Qa, "mpq" in OPT)
                score(KT, QT, mNT, LkT, "mlk" in OPT)
                score(KT, RT, mMT, MkT, "mmk" in OPT)
                score(BT, RT, mMT, MbT, "mmb" in OPT)
                wc = WC[sg % 3][d]
                wbc = Buf(bass.AP(tensor=wc.ap.tensor, offset=wc.ap.offset, ap=[list(wc.ap.ap[0]), [1, CPS], [0, 64]]), wc.reg)
                for src in (KT, BT):
                    s3 = Buf(v3(src), src.reg)
                    tt(V, s3, s3, wbc, ALU.mult)
                blockmm(KT, None, pk, rhs_fixed=IDT); cp(A_, Ktm, pk)
                blockmm(BT, None, pk, rhs_fixed=IDT); cp(A_, Btm, pk)

            def prepB(d, sg):
                Pa, Qa = PQA[sg % 2][d]
                Pb, Qb, Talt = PQB[d]
                TT = TTD[sg % 2][d]
                bb = [bank[2][:, 0:SEG], bank[3][:, 0:SEG]] if d == 0 else [bank[6][:, 0:SEG], bank[7][:, 0:SEG]]
                Tc, Tn = Talt, TT
                tt(V, Tc, Qa, IDT[:, 0:SEG], ALU.add)
                Pc, Pn, Qc, Qn = Pa, Pb, Qa, Qb
                for rd_ in range(1, 6):
                    b0 = bb[rd_ % 2]; b1 = bb[(rd_ + 1) % 2]
                    blockmm(Qc, Pc, b0); cp(A_, Pn, b0)
                    if rd_ < 5:
                        blockmm(Pc, Qc, b1); cp(A_ if "qact" in OPT else V, Qn, b1)
                    blockmm(Pn, Tc, b0); tt(V, Tn, b0, Tc, ALU.add)
                    Pc, Pn = Pn, Pc; Qc, Qn = Qn, Qc; Tc, Tn = Tn, Tc
                assert Tc is TT

            XS = bank[4][:, 0:256]; UU = bank[4][:, 256:512]; Yb = [bank[5][:, 0:256], bank[5][:, 256:512]]
            def seq(sg):
                segs = [sg, NSEG - 1 - sg]
                OPS = [OPD3[sg % 3][d] + [TTD[sg % 2][d]] for d in range(2)]
                for st in range(CPS):
                    nloc = [st, CPS - 1 - st]
                    for d in range(2):
                        ds = slice(d * 64, (d + 1) * 64)
                        ts(G, tmpA[:, ds], A32[:, ds], WC[sg % 3][d][:, nloc[d]:nloc[d] + 1], None, ALU.mult)
                    items = []
                    for d in range(2):
                        RT, QT, Ktm, Btm, LkT, MkT, MbT, TT = OPS[d]
                        nl = nloc[d]; ng = segs[d] * CPS + nl
                        for hh in range(2):
                            hs = hblk(hh); ns = slice(nl * 64, (nl + 1) * 64)
                            o = XS.ap[hs, d * 64:(d + 1) * 64]
                            items.append((o, QT[hs, ns], Abf[hs, d * 64:(d + 1) * 64], True, False))
                            items.append((o, LkT[hs, ns], Vtm[hs, ng * 64:(ng + 1) * 64], False, True))
                    mms(items, XS.reg)
                    cp(A_ if "xact" in OPT else V, Xbf, XS[:, 0:128])
                    items = []
                    for d in range(2):
                        TT = OPS[d][7]; nl = nloc[d]
                        for hh in range(2):
                            hs = hblk(hh); ns = slice(nl * 64, (nl + 1) * 64)
                            items.append((UU.ap[hs, d * 64:(d + 1) * 64], TT[hs, ns], Xbf[hs, d * 64:(d + 1) * 64], True, True))
                    mms(items, UU.reg)
                    cp(A_ if "uact" in OPT else V, Ubf, UU[:, 0:128])
                    items = []
                    for d in range(2):
                        RT, QT, Ktm, Btm, LkT, MkT, MbT, TT = OPS[d]
                        nl = nloc[d]; ng = segs[d] * CPS + nl
                        for hh in range(2):
                            hs = hblk(hh); ns = slice(nl * 64, (nl + 1) * 64)
                            o = XS.ap[hs, 128 + d * 64:128 + (d + 1) * 64]
                            items.append((o, Ktm[hs, ns], Vtm[hs, ng * 64:(ng + 1) * 64], True, False))
                            items.append((o, Btm[hs, ns], Ubf[hs, d * 64:(d + 1) * 64], False, True))
                    mms(items, XS.reg)
                    for d in range(2):
                        RT, QT, Ktm, Btm, LkT, MkT, MbT, TT = OPS[d]
                        nl = nloc[d]; ng = segs[d] * CPS + nl
                        items = []
                        for hh in range(2):
                            hs = hblk(hh); ns = slice(nl * 64, (nl + 1) * 64)
                            o = Yb[d].ap[hs, ns]
                            items.append((o, Abf[hs, d * 64:(d + 1) * 64], RT[hs, ns], True, False))
                            items.append((o, Vtm[hs, ng * 64:(ng + 1) * 64], MkT[hs, ns], False, False))
                            items.append((o, Ubf[hs, d * 64:(d + 1) * 64], MbT[hs, ns], False, True))
                        mms(items, Yb[d].reg)
                    gstep = sg * CPS + st
                    bnd = (gstep + 1) % 4 == 0
                    if not bnd:
                        tt(V, Abf, XS[:, 128:256], tmpA, ALU.add)
                    tt(V, A32, XS[:, 128:256], tmpA, ALU.add)
                    if bnd:
                        q = (gstep + 1) // 4 - 1
                        sb_ = stg[q % 2]
                        cp(V, sb_, A32)
                        S.dma("sync", So_d[l, c, q], sb_.ap, reads=[sb_.reg], is_output=True)
                        ts(V, A32, A32, keepf, None, ALU.mult)
                        cp(A_, Abf, A32)
                for d in range(2):
                    t0 = segs[d] * SEG
                    dst = oacc[:, t0:t0 + SEG]
                    first = (segs[d] < NSEG - 1 - segs[d]) if d == 0 else (segs[d] > NSEG - 1 - segs[d])
                    if d == 0 and segs[d] == NSEG - 1 - segs[d]:
                        first = True
                    if first:
                        cp(A_, dst, Yb[d])
                    else:
                        tt(V, dst, Yb[d], dst, ALU.add)

            S.emit_merged([S.record(lambda: prepA(0, 0)), S.record(lambda: prepA(1, 0))])
            recs = [S.record(lambda: prepB(0, 0)), S.record(lambda: prepB(1, 0))]
            if NSEG > 1:
                recs += [S.record(lambda: prepA(0, 1)), S.record(lambda: prepA(1, 1))]
            S.emit_merged(recs)
            for sg in range(NSEG):
                recs = [S.record(lambda: seq(sg))]
                if sg + 1 < NSEG:
                    recs += [S.record(lambda: prepB(0, sg + 1)), S.record(lambda: prepB(1, sg + 1))]
                if sg + 2 < NSEG:
                    recs += [S.record(lambda: prepA(0, sg + 2)), S.record(lambda: prepA(1, sg + 2))]
                S.emit_merged(recs)
            AR.release(m_seg)
            ycur = AR.b(T)
            NPS = min(NT, 4)
            f = [[AR.f(512) for _ in range(4)] for _ in range(NPS)]
            sxgt = [AR.b(512) for _ in range(NPS)]
            wtmp = [AR.f(512), AR.f(512)]
            wv = load_w_cols(w_in, l, 2 * WA + c * 128, pool=(2, 3))
            wg = load_w_cols(w_in, l, 3 * WA + 128, pool=(2, 3))
            def post(tq):
                tsl = slice(tq * 512, (tq + 1) * 512)
                ff = f[tq % NPS]; pbk = [bank[(tq % NPS) * 2], bank[(tq % NPS) * 2 + 1]]
                o = oacc[:, tsl]
                mms([(pbk[0].ap, BD64, o, True, True)], pbk[0].reg)
                tt(V, ff[0], o, pbk[0], ALU.subtract)
                act(ff[1], ff[0], AF.Square)
                mms([(pbk[1].ap, BD64, ff[1], True, True)], pbk[1].reg)
                act(ff[1], pbk[1], AF.Ln, bias=LNX_EPS)
                act(ff[1], ff[1], AF.Exp, scale=-0.5)
                tt(V, ff[0], ff[0], ff[1], ALU.mult)
                ts(V, ff[0], ff[0], P(l, "lnxg", c), P(l, "lnxb", c), ALU.mult, ALU.add)
                for d in range(2):
                    mms([(pbk[d].ap, AUP(d, c), txw[64:128, tsl], True, True)], pbk[d].reg)
                    act(ff[1 + d], pbk[d], AF.Sigmoid, bias=P(l, "a0", d * 4 + c))
                tt(G, ff[1], ff[1], ff[2], ALU.add)
                ts(V, ff[1], ff[1], DR(l, 20 + c), DR(l, 24 + c), ALU.mult, ALU.add)
                tt(G, ff[2], rbf[:, tsl], kbf[:, tsl], ALU.mult)
                tt(V, ff[1], ff[1], ff[2], ALU.mult)
                mms([(pbk[0].ap, BD1, ff[1], True, True)], pbk[0].reg)
                cp(A_, ff[2], pbk[0])
                proj_fm(wv, pbk[1], tq)
                tt(V, ff[1], pbk[1], ff[2], ALU.mult)
                tt(G, ff[0], ff[0], ff[1], ALU.add)
                proj_fm(wg, pbk[0], tq)
                act(sxgt[tq % NPS], pbk[0], AF.Sigmoid)
                mms([(pbk[1].ap, GUP(c), sxgt[tq % NPS], True, True)], pbk[1].reg)
                tt(V, ycur[:, tsl], pbk[1], ff[0], ALU.mult)
            recs = [S.record(lambda tq=tq: post(tq)) for tq in range(NT)]
            for i in range(0, NT, NPS):
                S.emit_merged(recs[i:i + NPS])
            pending = [S.record(lambda c=c, ycur=ycur, wtmp=wtmp: wout_accum(l, c, ycur, [bank[4], bank[5], bank[6], bank[7]], wtmp))]
        S.emit_merged(pending)
        S.mark("L%d wkv" % l)

        AR.release(m_layer)
        lru_pending = []
        ycs = [AR.b(T), AR.b(T)]
        m_lru = AR.mark()
        def wout_pair(kcA, kcB, ycA, ycB, banks, tmps=None):
            wA = woutb[0]; wB = woutb[1]
            S.dma("gpsimd", wA.ap, w_out[l][kcA * 128:(kcA + 1) * 128, :], writes=[wA.reg])
            S.dma("gpsimd", wB.ap, w_out[l][kcB * 128:(kcB + 1) * 128, :], writes=[wB.reg])
            bi = 0
            for fo in range(8):
                for tq in range(NT):
                    bk = banks[bi % len(banks)]; bi += 1
                    tsl = slice(tq * 512, (tq + 1) * 512)
                    mms([(bk.ap, wA[:, fo * 128:(fo + 1) * 128], ycA[:, tsl], True, False),
                         (bk.ap, wB[:, fo * 128:(fo + 1) * 128], ycB[:, tsl], False, True)], bk.reg)
                    xres_evac(l, fo, tq, bk, bi - 1, tmps)
        for c in range(4):
            AR.release(m_lru)
            xb32 = AR.f(T); xc = AR.f(T); gA0 = AR.f(T); gI0 = AR.f(T); gI1 = AR.f(T); hF = AR.f(T); hB = AR.f(T)
            xcb = AR.b(T); ycur = ycs[c % 2]
            gAs = [gA0, xb32]; gIs = [gI0, gI1]
            def lru_common():
                wb = load_w_cols(w_in, l, 3 * WA + 256 + c * 128, pool=(0, 1))
                for tq in range(NT):
                    bk = bank[tq % 4]
                    proj_fm(wb, bk, tq)
                    cp(A_, xb32[:, tq * 512:(tq + 1) * 512], bk)
                cw = lambda j: P(l, "cw", j * 4 + c)
                act(xc, xb32, AF.Identity, scale=cw(2), bias=P(l, "cb", c))
                x3 = v3(xb32); y3 = v3(xc)
                def shifted(j, off):
                    if off < 0:
                        o_ = Buf(y3[:, :, -off:64], xc.reg); i_ = Buf(x3[:, :, 0:64 + off], xb32.reg)
                    else:
                        o_ = Buf(y3[:, :, 0:64 - off], xc.reg); i_ = Buf(x3[:, :, off:64], xb32.reg)
                    stt(o_, i_, cw(j), o_, ALU.mult, ALU.add)
                shifted(1, -1); shifted(0, -2); shifted(3, 1)
                if R64 > 1:
                    def fix(j, ocol, icol, nxt, tb):
                        if nxt:
                            o_ = Buf(y3[:, 0:R64 - 1, ocol], xc.reg); i_ = Buf(x3[:, 1:R64, icol], xb32.reg); fl_ = cfl_next[:, 0:R64 - 1]
                        else:
                            o_ = Buf(y3[:, 1:R64, ocol], xc.reg); i_ = Buf(x3[:, 0:R64 - 1, icol], xb32.reg); fl_ = cfl_prev[:, 0:R64 - 1]
                        tt(G, tb[:, 0:R64 - 1], i_, fl_, ALU.mult)
                        stt(o_, tb[:, 0:R64 - 1], cw(j), o_, ALU.mult, ALU.add)
                    fix(1, 0, 63, False, tbs[0]); fix(0, 0, 62, False, tbs[1]); fix(0, 1, 63, False, tbs[2]); fix(3, 63, 0, True, tbs[3])
                cp(G, xcb, xc)
                ld(h0t, h0_d[l])
            def lru_dir(d):
                gA = gAs[d]; gI = gIs[d]
                for gt, dst in ((0, gA), (1, gI)):
                    o = ((gt * 2 + d) * 4 + c) * 128
                    bnm = "gab" if gt == 0 else "gxb"
                    for tq in range(NT):
                        bk = bank[2 * d + tq % 2]
                        mms([(bk.ap, bdw[:, o:o + 128], xcb[:, tq * 512:(tq + 1) * 512], True, True)], bk.reg)
                        act(dst[:, tq * 512:(tq + 1) * 512], bk, AF.Sigmoid, bias=P(l, bnm, d * 4 + c))
                hD = hF if d == 0 else hB
                act(gA, gA, AF.Exp, scale=DR(l, 28 + d * 4 + c))
                tt(G, gI, gI, xc, ALU.mult)
                tt(V, hD, gA, gA, ALU.mult)
                act(hD, hD, AF.Sqrt, scale=-1.0, bias=1.0)
                tt(V if d == 0 else G, gI, gI, hD, ALU.mult)
                ini = h0t[:, d * 4 + c:d * 4 + c + 1]
                for q in (range(NSQ) if d == 0 else range(NSQ - 1, -1, -1)):
                    qs = slice(q * 256, (q + 1) * 256)
                    if d == 0:
                        scan(hD[:, qs], gA[:, qs], gI[:, qs], ini)
                        last = hD[:, q * 256 + 255:q * 256 + 256]
                    else:
                        scan(rev(hD[:, qs]), rev(gA[:, qs]), rev(gI[:, qs]), ini)
                        last = hD[:, q * 256:q * 256 + 1]
                    cp(G, hfin_d[d][:, q:q + 1], last)
                    ini = hini_d[d][:, q:q + 1]
                    ts(G, ini, last, keepf, None, ALU.mult)
                S.dma("sync", ho_d[l, c][:, d * NSQ:(d + 1) * NSQ], hfin_d[d].ap, reads=[hfin_d[d].reg], is_output=True)
            def lru_final():
                tt(V, hF, hF, hB, ALU.add)
                wb2 = load_w_cols(w_in, l, 3 * WA + 256 + WA + c * 128, pool=(0, 1))
                for tq in range(NT):
                    bk = bank[4 + tq % 3]
                    proj_fm(wb2, bk, tq)
                    act(gA0[:, tq * 512:(tq + 1) * 512], bk, AF.Gelu_apprx_tanh)
                tt(V, ycur, hF, gA0, ALU.mult)
                if c % 2 == 1:
                    wout_pair(4 + c - 1, 4 + c, ycs[0], ycs[1], [bank[4], bank[5], bank[6]], [hB[:, 0:512]] + ([hB[:, 512:1024]] if T >= 1024 else []))
            mod_pool[0] = (2, 3)
            S.emit_merged(lru_pending + [S.record(lru_common)])
            recs = [S.record(lambda: lru_dir(0)), S.record(lambda: lru_dir(1))]
            wts = [3, 3]
            if l + 1 < NL:
                recs.append(S.record(lambda: mod_cols(l + 1, c * 12, (c + 1) * 12)))
                wts.append(1)
            S.emit_merged(recs, weights=wts)
            lru_pending = [S.record(lru_final)]
        S.emit_merged(lru_pending)
        if l + 1 < NL:
            mod_fin(l + 1)

        AR.release(m_layer)
        S.mark("L%d lru" % l)
        rmsnorm_to_hbf(l, 8, 24)
        S.mark("L%d norm2" % l)
        TH = T // 2 if T >= 1024 else T
        NH = T // TH; NTH = TH // 512
        mj = [AR.b(TH) for _ in range(22)]
        accs = [[AR.f(TH), AR.f(TH)] for _ in range(2)]
        ftmp = [AR.f(TH) for _ in range(2)]
        for hf in range(NH):
            tb0 = hf * TH
            def ffn_j(j, s, wbs, nxt):
                accA, accB = accs[s]
                tmpc = ftmp[s]
                R = TH // 64
                r0 = tb0 // 64
                for ab, acc in ((0, accA), (1, accB)):
                    colc = ab * 22 + j
                    wb = wbs.pop(0)
                    if nxt:
                        wbs.append(load_w_cols(w_up, l, nxt.pop(0) * 128, pool=(2 * s, 2 * s + 1, 4 + s)))
                    fw = lambda jj, colc=colc: P(l, "fw", jj * 44 + colc)
                    b0 = s * 4 + ab * 2
                    for i in range(NTH):
                        proj_fm(wb, bank[b0 + i], tb0 // 512 + i)
                    pu2 = Buf(ps_t[:, b0 * 512:(b0 + NTH) * 512], [bank[b0 + i].reg for i in range(NTH)])
                    act(acc, pu2, AF.Identity, scale=fw(1), bias=P(l, "fb", colc))
                    p3 = v3(pu2); a3 = v3(acc); t3_ = v3(tmpc)
                    act(Buf(t3_[:, :, 0:63], tmpc.reg), Buf(p3[:, :, 0:63], pu2.reg), AF.Identity, scale=fw(0))
                    o_ = Buf(a3[:, :, 1:64], acc.reg)
                    tt(G, o_, o_, Buf(t3_[:, :, 0:63], tmpc.reg), ALU.add)
                    o_ = Buf(a3[:, :, 0:63], acc.reg); i_ = Buf(p3[:, :, 1:64], pu2.reg)
                    stt(o_, i_, fw(2), o_, ALU.mult, ALU.add)
                    tA = tbs[s * 2]; tB = tbs[s * 2 + 1]
                    tt(V, tA[:, 0:R - 1], Buf(p3[:, 0:R - 1, 63], pu2.reg), cfl_prev[:, r0:r0 + R - 1], ALU.mult)
                    o_ = Buf(a3[:, 1:R, 0], acc.reg)
                    stt(o_, tA[:, 0:R - 1], fw(0), o_, ALU.mult, ALU.add)
                    tt(V, tB[:, 0:R - 1], Buf(p3[:, 1:R, 0], pu2.reg), cfl_next[:, r0:r0 + R - 1], ALU.mult)
                    o_ = Buf(a3[:, 0:R - 1, 63], acc.reg)
                    stt(o_, tB[:, 0:R - 1], fw(2), o_, ALU.mult, ALU.add)
                act(accA, accA, AF.Gelu_apprx_tanh)
                tt(V, mj[j], accA, accB, ALU.mult)
            def ffn_stream(s):
                cols = [ab * 22 + j for j in range(s, 22, 2) for ab in (0, 1)]
                wbs = [load_w_cols(w_up, l, cols.pop(0) * 128, pool=(2 * s, 2 * s + 1, 4 + s))]
                wbs.append(load_w_cols(w_up, l, cols.pop(0) * 128, pool=(2 * s, 2 * s + 1, 4 + s)))
                for j in range(s, 22, 2):
                    ffn_j(j, s, wbs, cols)
            S.emit_merged([S.record(lambda: ffn_stream(0)), S.record(lambda: ffn_stream(1))])
            def down(fo, s):
                wks = []
                for k0 in range(0, 22, 8):
                    kn = min(8, 22 - k0)
                    wks.append((k0, kn, load_w_cols(w_down, l, fo * 128, row0=k0 * 128, nk=kn, pool=(2 * s, 2 * s + 1, 4 + s))))
                for tq in range(NTH):
                    bk = bank[s * 4 + tq % 4]
                    items = []
                    for (k0, kn, wbk) in wks:
                        for kk_ in range(kn):
                            j = k0 + kk_
                            items.append((bk.ap, wbk[:, kk_ * 128:(kk_ + 1) * 128], mj[j][:, tq * 512:(tq + 1) * 512], j == 0, j == 21))
                    mms(items, bk.reg)
                    xs = xres[fo][:, tb0 + tq * 512:tb0 + (tq + 1) * 512]
                    stt(xs, bk, MOD(l, 40 + fo), xs, ALU.mult, ALU.add)
            def down_stream(s):
                for fo in range(s, 8, 2):
                    down(fo, s)
            S.emit_merged([S.record(lambda: down_stream(0)), S.record(lambda: down_stream(1))])
        AR.release(m_layer)

    S.mark("end layers")
    LNF = ppB[:, PC["lnf"]:PC["lnf"] + 8]
    sqb = [AR.b(8 * 512) for _ in range(2)]; rs = [AR.f(512) for _ in range(2)]; obs = [AR.f(512) for _ in range(8)]
    for tq in range(NT):
        tsl = slice(tq * 512, (tq + 1) * 512)
        rms_rstd(tq, sqb[tq % 2], rs[tq % 2])
        for c in range(8):
            ob = obs[c]
            stt(ob, xres[c][:, tsl], LNF[:, c:c + 1], rs[tq % 2], ALU.mult, ALU.mult)
            S.dma("sync", yT[c][:, tsl], ob.ap, reads=[ob.reg], is_output=True)
    S.mark("final")
    build.last_sched = S
    S.finish()
    es.close()
    return nc, AR.hw


def _colmat(v, n):
    return np.ascontiguousarray(np.swapaxes(v.reshape(v.shape[:-1] + (n, 128)), -1, -2))


def _consts():
    p = np.arange(128)[:, None] % 64; f = np.arange(512)[None, :] % 64
    ml = (f < p); mu = (f > p); mli = (f <= p); mui = (f >= p); idt = (f == p)
    mrf = np.broadcast_to(f != 0, (128, 512)); mrb = np.broadcast_to(f != 63, (128, 512))
    cst = np.concatenate([ml, mu, mli, mui, idt, mrf, mrb], axis=1).astype(np.float32)
    blk = (np.arange(128)[:, None] // 64) == (np.arange(128)[None, :] // 64)
    cstf = np.concatenate([blk / 64.0, blk * 1.0], axis=1).astype(np.float32)
    return cst, cstf


def make_in_maps(inp, T, NL, roles):
    R64 = T // 64
    cst, cstf = _consts()
    L = NL
    pp = np.zeros((L, 128, NPC), np.float32)
    def put(name, arr):
        pp[:, :, PC[name]:PC[name] + arr.shape[-1]] = arr
    put("ln1g", _colmat(inp["ln1_g"][:L], 8)); put("ln2g", _colmat(inp["ln2_g"][:L], 8)); put("bmod", _colmat(inp["b_mod"][:L], 48))
    dcat = lambda a: np.concatenate([_colmat(a[:L, 0], 4), _colmat(a[:L, 1], 4)], axis=-1)
    put("w0", dcat(inp["rw_w0"])); put("a0", dcat(inp["rw_a0"]))
    put("k_k", _colmat(inp["rw_k_k"][:L], 4)); put("k_a", _colmat(inp["rw_k_a"][:L], 4))
    put("r_k", _colmat(inp["rw_r_k"][:L].reshape(L, 512), 4))
    put("lnxg", _colmat(inp["rw_lnx_g"][:L], 4)); put("lnxb", _colmat(inp["rw_lnx_b"][:L], 4))
    put("cw", np.concatenate([_colmat(inp["lru_conv_w"][:L, j], 4) for j in range(4)], axis=-1))
    put("cb", _colmat(inp["lru_conv_b"][:L], 4))
    put("gab", dcat(inp["lru_ga_b"])); put("gxb", dcat(inp["lru_gx_b"])); put("lam", dcat(inp["lru_lam"]))
    put("fw", np.concatenate([_colmat(inp["ffn_conv_w"][:L, j], 44) for j in range(3)], axis=-1))
    put("fb", _colmat(inp["ffn_conv_b"][:L], 44))
    put("lnf", np.broadcast_to(_colmat(inp["lnf_g"], 8), (L, 128, 8)))
    shared = {"pp": pp, "cst": cst, "cstf": cstf}
    for k in ("w_mod", "w_in", "w_out", "w_up", "w_down", "rw_w_up", "rw_a_up", "rw_g_up", "lru_ga_w", "lru_gx_w"):
        shared[k] = np.ascontiguousarray(inp[k][:L])
    maps = []
    for role in roles:
        m = dict(shared)
        fl = np.zeros((128, 2 + 2 * R64), np.float32)
        if role[0] == "P":
            x = np.concatenate([inp["x_prompt"][s] for s in role[1]], axis=0)
            cvec = inp["c_ctx"]
            A0 = np.zeros((L, 4, 128, 128), np.float32); h0 = np.zeros((L, 128, 8), np.float32)
            r = np.arange(R64)
            fl[:, 2:2 + R64 - 1] = ((r[1:] % 4) != 0).astype(np.float32)[None, :]
            fl[:, 2 + R64:2 + 2 * R64 - 1] = (((r[:-1] + 1) % 4) != 0).astype(np.float32)[None, :]
        else:
            b = role[1]
            x = inp["x_sample"][b]
            cvec = inp["c"][b]
            st = inp["state_rwkv"][b][:L]
            A0 = np.ascontiguousarray(st.reshape(L, 2, 4, 2, 64, 64).transpose(0, 2, 3, 5, 1, 4).reshape(L, 4, 128, 128))
            h0 = dcat(inp["state_lru"][b][None].transpose(1, 0, 2, 3).reshape(L, 1, 2, 512)[:, 0][:, None].repeat(1, 1).reshape(L, 2, 512)[:, :, :].reshape(L, 2, 512)) if False else \
                np.concatenate([_colmat(inp["state_lru"][b][:L, 0], 4), _colmat(inp["state_lru"][b][:L, 1], 4)], axis=-1)
            fl[:, 0] = 1.0
        m["xT"] = np.ascontiguousarray(x.T.reshape(8, 128, T))
        m["cv"] = _colmat(cvec, 8)
        m["A0"] = A0.astype(np.float32); m["h0"] = np.ascontiguousarray(h0.astype(np.float32)); m["fl"] = fl
        maps.append(m)
    return maps


def assemble(results, roles, T, NL, n_prompt, n_sample):
    NSQ = T // 256
    yp = [None] * n_prompt; ys = [None] * n_sample
    nr = np.zeros((n_prompt, NL, 2, 8, 64, 64), np.float32); nl_ = np.zeros((n_prompt, NL, 2, 512), np.float32)
    for res, role in zip(results, roles):
        y = np.asarray(res["yT"]).reshape(1024, T).T
        if role[0] == "P":
            So = np.asarray(res["So"]).reshape(NL, 4, NSQ, 2, 64, 2, 64)
            ho = np.asarray(res["ho"]).reshape(NL, 4, 128, 2, NSQ)
            for qi, s in enumerate(role[1]):
                yp[s] = y[qi * 256:(qi + 1) * 256]
                for d in range(2):
                    qb = qi if d == 0 else NSQ - 1 - qi
                    blk = So[:, :, qb, :, :, d, :]
                    nr[s, :, d] = blk.transpose(0, 1, 2, 4, 3).reshape(NL, 8, 64, 64)
                    nl_[s, :, d] = ho[:, :, :, d, qi].reshape(NL, 512)
        else:
            ys[role[1]] = y
    return np.stack(yp), np.stack(ys), nr, nl_


def kernel(**inputs):
    inp = {k: np.asarray(v, dtype=np.float32) for k, v in inputs.items()}
    T, NL = 2048, 4
    roles = [("P", list(range(8 * i, 8 * i + 8))) for i in range(4)] + [("S", b) for b in range(4)]
    nc, _ = build(T, NL)
    maps = make_in_maps(inp, T, NL, roles)
    res = run_bass_kernel_spmd(nc, maps, core_ids=list(range(8)))
    yp, ys, nr, nl_ = assemble(res.results, roles, T, NL, 32, 4)
    return (yp.astype(np.float32), ys.astype(np.float32), nr.astype(np.float32), nl_.astype(np.float32))
```

```python
import numpy as np
from contextlib import ExitStack
import concourse.bass as bass
import concourse.mybir as mybir
from concourse.bass_utils import run_bass_kernel_spmd

F32 = mybir.dt.float32
BF16 = mybir.dt.bfloat16
AF = mybir.ActivationFunctionType
ALU = mybir.AluOpType
ENGS = ["tensor", "vector", "scalar", "gpsimd", "sync"]

D = 1024; WA = 512; PIN = 2816; DFF = 2816; C = 64
EPS = 1e-6; LNX_EPS = 64e-5; KK_EPS = 1e-12; DSC = 0.6065306597126334

def _cols():
    o = {}; n = 0
    def add(name, k):
        nonlocal n
        o[name] = n; n += k
    add("ln1g", 8); add("ln2g", 8); add("bmod", 48)
    add("w0", 8); add("a0", 8); add("k_k", 4); add("k_a", 4); add("r_k", 4); add("lnxg", 4); add("lnxb", 4)
    add("cw", 16); add("cb", 4); add("gab", 8); add("gxb", 8); add("lam", 8)
    add("fw", 132); add("fb", 44); add("lnf", 8)
    return o, n
PC, NPC = _cols()


class Reg:
    __slots__ = ("w", "r", "parents")
    def __init__(self, parents=None):
        self.w = None; self.r = {}; self.parents = parents

    def resolve(self):
        if self.parents:
            ps = self.parents; self.parents = None
            for p in ps:
                p.resolve()
                if p.w is not None and self.r.get(p.w[0], 0) < p.w[1]: self.r[p.w[0]] = p.w[1]
                for k, v in p.r.items():
                    if self.r.get(k, 0) < v: self.r[k] = v


def _flat(regs):
    out = []
    for r in regs:
        if isinstance(r, (list, tuple)): out.extend(_flat(r))
        else: out.append(r)
    return out


class Sched:
    ROT = 12000
    def __init__(self, nc, es, n_dma_sems=16):
        self.nc = nc; self.es = es
        self.prog = {e: [] for e in ENGS}
        self.cnt = {e: 0 for e in ENGS}
        self.epoch = {e: 0 for e in ENGS}
        self.waited = {e: {} for e in ENGS}
        self.sems = {}
        self.n_dma = n_dma_sems
        self.dma_cnt = {}
        self.dma_next = {"sync": 0, "gpsimd": 0, "scalar": 0}
        for e in ENGS:
            self.sems[(e, 0)] = es.enter_context(nc.semaphore("s_%s0" % e))
        for q in ("sync", "gpsimd"):
            for i in range(n_dma_sems):
                self.sems[("dma", q, i)] = es.enter_context(nc.semaphore("s_dma_%s%d" % (q, i)))
                self.dma_cnt[("dma", q, i)] = 0
        self.out_tokens = []
        self.last_tok = {}
        self.rec = None
        self.t_eng = {e: 0.0 for e in ENGS}
        self.t_w = {}; self.t_r = {}
        self.marks = []; self.busy = {e: 0.0 for e in ENGS}

    def _deps(self, reads, writes):
        for r in reads: r.resolve()
        for w in writes: w.resolve()
        deps = {}
        def add(tok):
            if tok is None: return
            k, v = tok
            if deps.get(k, 0) < v: deps[k] = v
        for r in reads: add(r.w)
        for w in writes:
            add(w.w)
            for t in w.r.items(): add(t)
        return deps

    def _emit_waits(self, eng, deps):
        for k, v in deps.items():
            if self.waited[eng].get(k, 0) >= v: continue
            self.waited[eng][k] = v
            sem = self.sems[k]
            self.prog[eng].append(lambda e, sem=sem, v=v: e.wait_ge(sem, v))

    def _mark(self, tok, reads, writes):
        for r in reads:
            if r.r.get(tok[0], 0) < tok[1]: r.r[tok[0]] = tok[1]
        for w in writes:
            w.w = tok; w.r = {}

    def _next_tok(self, eng):
        if self.cnt[eng] >= self.ROT:
            self.epoch[eng] += 1; self.cnt[eng] = 0
            self.sems[(eng, self.epoch[eng])] = self.es.enter_context(self.nc.semaphore("s_%s%d" % (eng, self.epoch[eng])))
        self.cnt[eng] += 1
        key = (eng, self.epoch[eng])
        self.last_tok[eng] = (key, self.cnt[eng])
        return (key, self.cnt[eng])

    def mark(self, name):
        self.marks.append((name, max(self.t_eng.values()), dict(self.busy)))

    def record(self, body):
        assert self.rec is None
        self.rec = []
        body()
        r = self.rec; self.rec = None
        return r

    def _est(self, kind, args):
        if kind == "op":
            eng, fn, reads, writes, cost = args
            return eng, reads, writes, cost, cost
        if kind == "mm_group":
            fns, reads, writes, cost = args
            return "tensor", reads, writes, cost, cost
        eng, o, a, reads, writes, is_out, cost = args
        return eng, reads, writes, cost, 0.15

    def _start_time(self, est):
        eng, reads, writes, cost, occ = est
        t = 0.0
        for r in reads:
            t = max(t, self.t_w.get(id(r), 0.0))
        for w in writes:
            t = max(t, self.t_w.get(id(w), 0.0), self.t_r.get(id(w), 0.0))
        return max(t + 0.12, self.t_eng[eng]), t

    def _account(self, est):
        eng, reads, writes, cost, occ = est
        st, _ = self._start_time(est)
        end = st + cost
        self.busy[eng] += occ
        self.t_eng[eng] = st + occ
        for r in reads:
            if self.t_r.get(id(r), 0.0) < end: self.t_r[id(r)] = end
        for w in writes:
            self.t_w[id(w)] = end

    def emit_merged(self, recs, weights=None):
        import os
        eps = float(os.environ.get("KEPS", "0.1"))
        idx = [0] * len(recs)
        rem = [sum(self._est(k, a)[3] for k, a in rl) for rl in recs]
        while True:
            cands = []
            for si, rl in enumerate(recs):
                if idx[si] >= len(rl): continue
                kind, args = rl[idx[si]]
                st, rdy = self._start_time(self._est(kind, args))
                cands.append((st, rdy, si))
            if not cands: break
            mn = min(c[0] for c in cands)
            if eps > 0:
                pool = [c for c in cands if c[0] <= mn + eps]
                si = max(pool, key=lambda c: (rem[c[2]], -c[2]))[2]
            else:
                si = min(cands)[2]
            kind, args = recs[si][idx[si]]; idx[si] += 1
            rem[si] -= self._est(kind, args)[3]
            getattr(self, kind)(*args)

    def op(self, eng, fn, reads=(), writes=(), cost=0.5):
        reads = _flat(reads); writes = _flat(writes)
        if self.rec is not None:
            self.rec.append(("op", (eng, fn, reads, writes, cost))); return
        self._account((eng, reads, writes, cost, cost))
        deps = self._deps(reads, writes)
        self._emit_waits(eng, deps)
        tok = self._next_tok(eng)
        sem = self.sems[tok[0]]
        self.prog[eng].append(lambda e, fn=fn, sem=sem: fn(e).then_inc(sem, 1))
        self._mark(tok, reads, writes)

    def mm_group(self, fns, reads=(), writes=(), cost=0.5):
        reads = _flat(reads); writes = _flat(writes)
        if self.rec is not None:
            self.rec.append(("mm_group", (fns, reads, writes, cost))); return
        self._account(("tensor", reads, writes, cost, cost))
        deps = self._deps(reads, writes)
        self._emit_waits("tensor", deps)
        tok = self._next_tok("tensor")
        sem = self.sems[tok[0]]
        for fn in fns[:-1]:
            self.prog["tensor"].append(lambda e, fn=fn: fn(e))
        fn = fns[-1]
        self.prog["tensor"].append(lambda e, fn=fn, sem=sem: fn(e).then_inc(sem, 1))
        self._mark(tok, reads, writes)

    def dma(self, eng, out_ap, in_ap, reads=(), writes=(), is_output=False, cost=3.0):
        reads = _flat(reads); writes = _flat(writes)
        if self.rec is not None:
            self.rec.append(("dma", (eng, out_ap, in_ap, reads, writes, is_output, cost))); return
        self._account((eng, reads, writes, cost, 0.15))
        i = self.dma_next[eng]
        self.dma_next[eng] = (i + 1) % self.n_dma
        key = ("dma", eng, i)
        deps = self._deps(reads, writes)
        if self.dma_cnt[key] > 0 and deps.get(key, 0) < self.dma_cnt[key]:
            deps[key] = self.dma_cnt[key]
        self._emit_waits(eng, deps)
        self.dma_cnt[key] += 16
        tok = (key, self.dma_cnt[key])
        sem = self.sems[key]
        self.prog[eng].append(lambda e, o=out_ap, a=in_ap, sem=sem: e.dma_start(out=o, in_=a).then_inc(sem, 16))
        self._mark(tok, reads, writes)
        if is_output: self.out_tokens.append(tok)

    def barrier(self):
        deps = {}
        for e in ENGS:
            if e in self.last_tok:
                k, v = self.last_tok[e]; deps[k] = v
        for key, v in self.dma_cnt.items():
            if v > 0: deps[key] = v
        for e in ENGS:
            self._emit_waits(e, dict(deps))

    def finish(self):
        deps = {}
        for k, v in self.out_tokens:
            deps[k] = max(deps.get(k, 0), v)
        for e in ENGS:
            if e in self.last_tok:
                k, v = self.last_tok[e]; deps[k] = max(deps.get(k, 0), v)
        self._emit_waits("sync", deps)
        with self.nc.Block() as block:
            @block.tensor
            def _(e):
                for f in self.prog["tensor"]: f(e)
            @block.vector
            def _(e):
                for f in self.prog["vector"]: f(e)
            @block.scalar
            def _(e):
                for f in self.prog["scalar"]: f(e)
            @block.gpsimd
            def _(e):
                for f in self.prog["gpsimd"]: f(e)
            @block.sync
            def _(e):
                for f in self.prog["sync"]: f(e)


class Buf:
    __slots__ = ("ap", "reg")
    def __init__(self, ap, reg=None):
        self.ap = ap; self.reg = reg if reg is not None else Reg()
    def __getitem__(self, idx):
        return Buf(self.ap[idx], self.reg)
    def v(self, ap):
        return Buf(ap, self.reg)


class Arena:
    def __init__(self, tf, tb, cap_f32):
        self.tf = tf; self.tb = tb; self.cap = cap_f32; self.off = 0; self.hw = 0
        self.ents = []
    def _reg(self, o, e):
        par = [r for (s_, e_, r) in self.ents if s_ < e and o < e_]
        self.ents = [(s_, e_, r) for (s_, e_, r) in self.ents if not (o <= s_ and e_ <= e)]
        reg = Reg(parents=par if par else None)
        self.ents.append((o, e, reg))
        return reg
    def f(self, n):
        o = self.off; self.off += n
        assert self.off <= self.cap, ("arena overflow", self.off, self.cap)
        self.hw = max(self.hw, self.off)
        return Buf(self.tf[:, o:o + n], self._reg(o, o + n))
    def b(self, n):
        o = self.off; self.off += (n + 1) // 2
        assert self.off <= self.cap, ("arena overflow", self.off, self.cap)
        self.hw = max(self.hw, self.off)
        return Buf(self.tb[:, 2 * o:2 * o + n], self._reg(o, self.off))
    def mark(self):
        return self.off
    def release(self, m):
        self.off = m


import os
OPT = set(os.environ.get("KOPT", "qact,mmk,mmb").split(","))
def build(T, NL, SEG=256, ARENA_F32=17408):
    NT = T // 512; NCH = T // C; NSEG = T // SEG; CPS = SEG // C; R64 = T // 64; NSQ = T // 256
    assert SEG == 256
    nc = bass.Bass("TRN2", target_bir_lowering=False)
    dt_in = lambda n, s: nc.dram_tensor(n, s, F32, kind="ExternalInput").ap()
    dt_out = lambda n, s: nc.dram_tensor(n, s, F32, kind="ExternalOutput").ap()
    xT = dt_in("xT", [8, 128, T]); cv = dt_in("cv", [128, 8]); pp_d = dt_in("pp", [NL, 128, NPC])
    A0_d = dt_in("A0", [NL, 4, 128, 128]); h0_d = dt_in("h0", [NL, 128, 8]); fl_d = dt_in("fl", [128, 2 + 2 * R64])
    cst_d = dt_in("cst", [128, 7 * 512]); cstf_d = dt_in("cstf", [128, 256])
    w_mod = dt_in("w_mod", [NL, D, 6 * D]); w_in = dt_in("w_in", [NL, D, PIN]); w_out = dt_in("w_out", [NL, D, D])
    w_up = dt_in("w_up", [NL, D, 2 * DFF]); w_down = dt_in("w_down", [NL, DFF, D])
    rw_wup = dt_in("rw_w_up", [NL, 2, 64, WA]); rw_aup = dt_in("rw_a_up", [NL, 2, 64, WA]); rw_gup = dt_in("rw_g_up", [NL, 128, WA])
    ga_w = dt_in("lru_ga_w", [NL, 2, 8, 64, 64]); gx_w = dt_in("lru_gx_w", [NL, 2, 8, 64, 64])
    yT = dt_out("yT", [8, 128, T]); So_d = dt_out("So", [NL, 4, NSQ, 128, 128]); ho_d = dt_out("ho", [NL, 4, 128, 2 * NSQ])

    es = ExitStack()
    S = Sched(nc, es)
    sbt = lambda n, s, d: es.enter_context(nc.sbuf_tensor("sb_" + n, s, d))
    xres_t = sbt("xres", [128, 8 * T], F32); xres = [Buf(xres_t[:, c * T:(c + 1) * T]) for c in range(8)]
    hbf_t = sbt("hbf", [128, 8 * T], BF16); hbf = [Buf(hbf_t[:, c * T:(c + 1) * T]) for c in range(8)]
    ar_t = sbt("arena", [128, ARENA_F32], F32)
    AR = Arena(ar_t, ar_t.bitcast(BF16), ARENA_F32)
    cst_t = sbt("cst", [128, 7 * 512], BF16); cstB = Buf(cst_t[:, :])
    ML, MU, MLI, MUI, IDT, MRF, MRB = [cstB[:, i * 512:(i + 1) * 512] for i in range(7)]
    cstf_t = sbt("cstf", [128, 256], F32); cstF = Buf(cstf_t[:, :]); BD64 = cstF[:, 0:128]; BD1 = cstF[:, 128:256]
    onesm = Buf(sbt("onesm", [128, 128], BF16)[:, :])
    ppB = Buf(sbt("ppt", [128, NL * NPC], F32)[:, :])
    der_t = sbt("der", [128, NL * 48], F32); derL = [Buf(der_t[:, l * 48:(l + 1) * 48]) for l in range(NL)]
    mod_t = sbt("modt", [128, NL * 48], F32); modL = [Buf(mod_t[:, l * 48:(l + 1) * 48]) for l in range(NL)]
    flB = Buf(sbt("flt", [128, 2 + 2 * R64], F32)[:, :])
    cvB = Buf(sbt("cvt", [128, 8], F32)[:, :]); cvbf = Buf(sbt("cvbf", [128, 8], BF16)[:, :])
    wsm_t = sbt("wsm", [128, 5 * 512], BF16); wsm = Buf(wsm_t[:, :])
    bdw = Buf(sbt("bdw", [128, 16 * 128], BF16)[:, :])
    wst = [Buf(sbt("wst%d" % i, [128, 8 * 128], BF16)[:, :]) for i in range(4)]
    woutb = [Buf(sbt("wout%d" % i, [128, 1024], BF16)[:, :]) for i in range(2)]
    wst = wst + woutb
    A32 = Buf(sbt("A32", [128, 128], F32)[:, :]); Abf = Buf(sbt("Abf", [128, 128], BF16)[:, :])
    tmpA = Buf(sbt("tmpA", [128, 128], F32)[:, :])
    Xbf = Buf(sbt("Xbf", [128, 128], BF16)[:, :]); Ubf = Buf(sbt("Ubf", [128, 128], BF16)[:, :])
    stg = [Buf(sbt("stg%d" % i, [128, 128], F32)[:, :]) for i in range(2)]
    h0t = Buf(sbt("h0t", [128, 8], F32)[:, :])
    hfin_d = [Buf(sbt("hfin%d" % d, [128, NSQ], F32)[:, :]) for d in range(2)]
    hini_d = [Buf(sbt("hini%d" % d, [128, NSQ], F32)[:, :]) for d in range(2)]
    tbs = [Buf(sbt("tb%d" % i, [128, R64], F32)[:, :]) for i in range(4)]
    wc_t = sbt("wct", [128, 6 * 4], F32)
    WC = [[Buf(wc_t[:, (s3 * 2 + d) * 4:(s3 * 2 + d) * 4 + 4]) for d in range(2)] for s3 in range(3)]
    ps_t = es.enter_context(nc.psum_tensor("ps", [128, 4096], F32))
    bank = [Buf(ps_t[:, b * 512:(b + 1) * 512]) for b in range(8)]

    V, G, A_ = "vector", "gpsimd", "scalar"
    def fd(buf):
        n = 1
        for st_, c_ in list(buf.ap.ap)[1:]:
            n *= int(c_)
        return n
    def ecost(eng, out):
        n = fd(out)
        if eng == V: return 0.25 + n * 0.00105
        if eng == G: return 0.35 + n * 0.0017
        return 0.25 + n * 0.00085
    def tt(eng, out, a, b, op):
        S.op(eng, lambda e: e.tensor_tensor(out=out.ap, in0=a.ap, in1=b.ap, op=op), reads=[a.reg, b.reg], writes=[out.reg], cost=ecost(eng, out))
    def ts(eng, out, a, s1, s2=None, op0=ALU.mult, op1=None):
        rd = [a.reg] + [x.reg for x in (s1, s2) if isinstance(x, Buf)]
        a1 = s1.ap if isinstance(s1, Buf) else s1
        a2 = s2.ap if isinstance(s2, Buf) else s2
        if op1 is None and eng == G and op0 == ALU.mult:
            S.op(eng, lambda e: e.tensor_scalar(out=out.ap, in0=a.ap, scalar1=a1, scalar2=0.0, op0=ALU.mult, op1=ALU.add), reads=rd, writes=[out.reg], cost=ecost(eng, out))
        elif op1 is None:
            S.op(eng, lambda e: e.tensor_scalar(out=out.ap, in0=a.ap, scalar1=a1, scalar2=None, op0=op0), reads=rd, writes=[out.reg], cost=ecost(eng, out))
        else:
            S.op(eng, lambda e: e.tensor_scalar(out=out.ap, in0=a.ap, scalar1=a1, scalar2=a2, op0=op0, op1=op1), reads=rd, writes=[out.reg], cost=ecost(eng, out))
    def stt(out, a, s, b, op0, op1):
        rd = [a.reg, b.reg] + ([s.reg] if isinstance(s, Buf) else [])
        sa = s.ap if isinstance(s, Buf) else s
        S.op(V, lambda e: e.scalar_tensor_tensor(out=out.ap, in0=a.ap, scalar=sa, in1=b.ap, op0=op0, op1=op1), reads=rd, writes=[out.reg], cost=ecost(V, out))
    def act(out, a, func, scale=1.0, bias=0.0):
        rd = [a.reg] + [x.reg for x in (scale, bias) if isinstance(x, Buf)]
        sc = scale.ap if isinstance(scale, Buf) else scale
        bi = bias.ap if isinstance(bias, Buf) else bias
        S.op(A_, lambda e: e.activation(out=out.ap, in_=a.ap, func=func, bias=bi, scale=sc), reads=rd, writes=[out.reg], cost=ecost(A_, out))
    def cp(eng, out, a):
        if eng == A_:
            act(out, a, AF.Copy)
        else:
            S.op(eng, lambda e: e.tensor_copy(out=out.ap, in_=a.ap), reads=[a.reg], writes=[out.reg], cost=ecost(eng, out))
    def recip(out, a):
        S.op(V, lambda e: e.reciprocal(out=out.ap, in_=a.ap), reads=[a.reg], writes=[out.reg])
    def scan(out, d0, d1, init):
        ia = init.ap if isinstance(init, Buf) else init
        rd = [d0.reg, d1.reg] + ([init.reg] if isinstance(init, Buf) else [])
        S.op(V, lambda e: e.tensor_tensor_scan(out=out.ap, data0=d0.ap, data1=d1.ap, initial=ia, op0=ALU.mult, op1=ALU.add), reads=rd, writes=[out.reg], cost=0.3 + fd(out) * 0.0021)
    def mms(items, out_reg):
        rd = []
        fns = []
        cost = 0.0
        for (o, l_, r_, st, sp) in items:
            n_ = fd(r_)
            cost += max(0.045, n_ * 0.00052) * (4.0 if r_.ap.tensor.dtype == F32 else 1.0)
            rd.append(l_.reg); rd.append(r_.reg)
            fns.append(lambda e, o=o, l_=l_, r_=r_, st=st, sp=sp: e.matmul(o, lhsT=l_.ap, rhs=r_.ap, start=st, stop=sp))
        rdd = {}
        for r in _flat(rd): rdd[id(r)] = r
        S.mm_group(fns, reads=list(rdd.values()), writes=[out_reg], cost=cost)
    def ld(out, src, eng="sync"):
        S.dma(eng, out.ap, src, writes=[out.reg])
    def memset(eng, out, val):
        S.op(eng, lambda e: e.memset(out.ap, val), writes=[out.reg])
    def rev(b):
        return Buf(b.ap[:, ::-1], b.reg)
    def v3(b, w=64):
        return b.ap.rearrange("p (r w) -> p r w", w=w)

    ld(cstB, cst_d[:, :], "gpsimd"); ld(cstF, cstf_d[:, :]); ld(flB, fl_d[:, :]); ld(cvB, cv[:, :])
    for l in range(NL):
        ld(ppB[:, l * NPC:(l + 1) * NPC], pp_d[l])
    for c in range(8):
        ld(xres[c], xT[c])
    memset(V, onesm, 1.0 / D)
    keepf = flB[:, 0:1]
    cfl_prev = flB[:, 2:2 + R64]
    cfl_next = flB[:, 2 + R64:2 + 2 * R64]
    def P(l, name, j=0, n=1):
        o = l * NPC + PC[name] + j
        return ppB[:, o:o + n]
    def DR(l, j, n=1):
        return derL[l][:, j:j + n]
    def MOD(l, j, n=1):
        return modL[l][:, j:j + n]
    act(cvbf, cvB, AF.Silu)

    wst_i = {}
    def load_w_cols(wd, l, col0, row0=0, nk=8, pool=(0, 1, 2, 3)):
        i = wst_i.get(pool, 0); wst_i[pool] = i + 1
        wb = wst[pool[i % len(pool)]]
        S.dma("gpsimd", wb.ap[:, 0:nk * 128].rearrange("p (k n) -> p k n", k=nk),
              wd[l][row0:row0 + nk * 128, :].rearrange("(k p) n -> p k n", p=128)[:, :, col0:col0 + 128], writes=[wb.reg])
        return wb

    modps = bank[7][:, 256:512]
    mod_pool = [(0, 1, 2, 3)]
    def mod_cols(l, c0, c1):
        for col in range(c0, c1):
            wb = load_w_cols(w_mod, l, col * 128, pool=mod_pool[0])
            mms([(modps.ap[:, col:col + 1], wb[:, k * 128:(k + 1) * 128], cvbf[:, k:k + 1], k == 0, k == 7) for k in range(8)], modps.reg)
    def mod_fin(l):
        tt(V, MOD(l, 0, 48), modps[:, 0:48], P(l, "bmod", 0, 48), ALU.add)
        stt(DR(l, 0, 8), MOD(l, 8, 8), 1.0, P(l, "ln1g", 0, 8), ALU.add, ALU.mult)
        stt(DR(l, 8, 8), MOD(l, 32, 8), 1.0, P(l, "ln2g", 0, 8), ALU.add, ALU.mult)
        ts(V, DR(l, 16, 4), P(l, "k_a", 0, 4), -1.0, 1.0, ALU.mult, ALU.add)
        tt(V, DR(l, 20, 4), P(l, "k_a", 0, 4), P(l, "r_k", 0, 4), ALU.mult)
        ts(V, DR(l, 24, 4), P(l, "k_a", 0, 4), -2.0, 2.0, ALU.mult, ALU.add)
        tt(V, DR(l, 24, 4), DR(l, 24, 4), P(l, "r_k", 0, 4), ALU.mult)
        act(DR(l, 36, 8), P(l, "lam", 0, 8), AF.Exp, scale=-1.0)
        act(DR(l, 36, 8), DR(l, 36, 8), AF.Ln, bias=1.0)
        ts(V, DR(l, 28, 8), DR(l, 36, 8), -8.0, None, ALU.mult)
    mod_cols(0, 0, 48); mod_fin(0)

    def rms_rstd(tq, sqb, rs):
        tsl = slice(tq * 512, (tq + 1) * 512)
        for c in range(8):
            act(sqb[:, c * 512:(c + 1) * 512], xres[c][:, tsl], AF.Square)
        mms([(bank[0].ap, onesm, sqb[:, c * 512:(c + 1) * 512], c == 0, c == 7) for c in range(8)], bank[0].reg)
        act(rs, bank[0], AF.Ln, bias=EPS)
        act(rs, rs, AF.Exp, scale=-0.5)

    def rmsnorm_to_hbf(l, ggoff, shoff):
        m = AR.mark()
        sqw = AR.b(8 * T); rsw = AR.f(T); tmp = [AR.f(T) for _ in range(3)]
        for c in range(8):
            act(sqw[:, c * T:(c + 1) * T], xres[c], AF.Square)
        for tq in range(NT):
            tsl = slice(tq * 512, (tq + 1) * 512)
            bk = bank[tq % 4]
            mms([(bk.ap, onesm, sqw[:, c * T + tq * 512:c * T + (tq + 1) * 512], c == 0, c == 7) for c in range(8)], bk.reg)
            act(rsw[:, tsl], bk, AF.Ln, bias=EPS)
        act(rsw, rsw, AF.Exp, scale=-0.5)
        for c in range(8):
            tb = tmp[c % 3]
            stt(tb, xres[c], DR(l, ggoff + c), rsw, ALU.mult, ALU.mult)
            if c % 4 != 3:
                act(hbf[c], tb, AF.Identity, bias=MOD(l, shoff + c))
            else:
                ts(V, hbf[c], tb, MOD(l, shoff + c), None, ALU.add)
        AR.release(m)

    def proj_fm(wb, bk, tq):
        mms([(bk.ap, wb[:, k * 128:(k + 1) * 128], hbf[k][:, tq * 512:(tq + 1) * 512], k == 0, k == 7) for k in range(8)], bk.reg)

    wout_i = [0]
    def xres_evac(l, fo, tq, bk, bi, tmps):
        xs = xres[fo][:, tq * 512:(tq + 1) * 512]
        if tmps is not None and bi % 3 == 2:
            tm = tmps[(bi // 3) % len(tmps)]
            act(tm, bk, AF.Identity, scale=MOD(l, 16 + fo))
            tt(G, xs, xs, tm, ALU.add)
        else:
            stt(xs, bk, MOD(l, 16 + fo), xs, ALU.mult, ALU.add)

    def wout_accum(l, kc, ycur, banks, tmps=None):
        wb = woutb[wout_i[0] % 2]; wout_i[0] += 1
        S.dma("gpsimd", wb.ap, w_out[l][kc * 128:(kc + 1) * 128, :], writes=[wb.reg])
        bi = 0
        for fo in range(8):
            for tq in range(NT):
                bk = banks[bi % len(banks)]
                mms([(bk.ap, wb[:, fo * 128:(fo + 1) * 128], ycur[:, tq * 512:(tq + 1) * 512], True, True)], bk.reg)
                xres_evac(l, fo, tq, bk, bi, tmps)
                bi += 1

    def hblk(hh):
        return slice(hh * 64, (hh + 1) * 64)

    for l in range(NL):
        S.mark("L%d start" % l)
        rmsnorm_to_hbf(l, 0, 0)
        S.mark("L%d norm1" % l)
        for d in range(2):
            S.dma("gpsimd", wsm.ap[0:64, d * 512:(d + 1) * 512], rw_wup[l, d], writes=[wsm.reg])
            S.dma("gpsimd", wsm.ap[64:128, (2 + d) * 512:(3 + d) * 512], rw_aup[l, d], writes=[wsm.reg])
        S.dma("gpsimd", wsm.ap[:, 4 * 512:5 * 512], rw_gup[l], writes=[wsm.reg])
        memset(G, bdw, 0.0)
        for gt, gw in enumerate((ga_w, gx_w)):
            for d in range(2):
                for c in range(4):
                    o = ((gt * 2 + d) * 4 + c) * 128
                    for hh in range(2):
                        S.dma("gpsimd", bdw.ap[hblk(hh), o + hh * 64:o + (hh + 1) * 64], gw[l, d, 2 * c + hh], writes=[bdw.reg])
        WUP = lambda d, c: wsm[0:64, d * 512 + c * 128:d * 512 + (c + 1) * 128]
        AUP = lambda d, c: wsm[64:128, (2 + d) * 512 + c * 128:(2 + d) * 512 + (c + 1) * 128]
        GUP = lambda c: wsm[:, 4 * 512 + c * 128:4 * 512 + (c + 1) * 128]

        m_layer = AR.mark()
        txw = AR.b(T)
        wb = load_w_cols(w_in, l, 3 * WA)
        for tq in range(NT):
            bk = bank[6 + tq % 2]; tsl = slice(tq * 512, (tq + 1) * 512)
            proj_fm(wb, bk, tq)
            act(txw[0:64, tsl], bk[0:64, :], AF.Tanh)
            act(txw[64:128, tsl], bk[64:128, :], AF.Copy)

        m_wkv = AR.mark()
        pending = []
        for c in range(4):
            AR.release(m_wkv)
            rbf = AR.b(T); kbf = AR.b(T); oacc = AR.f(T); Vtm = AR.b(T)
            def raws():
                for (dst, col0) in ((rbf, c * 128), (kbf, WA + c * 128)):
                    wb = load_w_cols(w_in, l, col0, pool=(0, 1))
                    for tq in range(NT):
                        bk = bank[tq % 2]
                        proj_fm(wb, bk, tq)
                        cp(A_ if tq % 2 == 0 else V, dst[:, tq * 512:(tq + 1) * 512], bk)
                wv = load_w_cols(w_in, l, 2 * WA + c * 128, pool=(0, 1))
                for n0 in range(0, NCH, 8):
                    bk = bank[2 + (n0 // 8) % 2]
                    items = []
                    for n in range(n0, n0 + 8):
                        for hh in range(2):
                            for k in range(8):
                                items.append((bk.ap[hblk(hh), (n - n0) * 64:(n - n0 + 1) * 64],
                                              hbf[k][:, n * 64:(n + 1) * 64], wv[:, k * 128 + hh * 64:k * 128 + (hh + 1) * 64], k == 0, k == 7))
                    mms(items, bk.reg)
                    cp(A_, Vtm[:, n0 * 64:(n0 + 8) * 64], bk)
                ld(A32, A0_d[l, c])
                cp(A_, Abf, A32)
            S.emit_merged(pending + [S.record(raws)])
            pending = []
            m_seg = AR.mark()
            OPD3 = [[[AR.b(SEG) for _ in range(7)] for _ in range(2)] for _ in range(3)]
            TTD = [[AR.b(SEG) for _ in range(2)] for _ in range(2)]
            PQA = [[[AR.b(SEG) for _ in range(2)] for _ in range(2)] for _ in range(2)]
            PQB = [[AR.b(SEG) for _ in range(3)] for _ in range(2)]
            KB = [[AR.b(SEG) for _ in range(2)] for _ in range(2)]
            TMPS = [[AR.f(SEG) for _ in range(6)] for _ in range(2)]

            def blockmm(lt, rh, bk, rhs_fixed=None):
                items = []
                for n in range(CPS):
                    for hh in range(2):
                        hs = hblk(hh); ns = slice(n * 64, (n + 1) * 64)
                        r_ = rh[hs, ns] if rhs_fixed is None else rhs_fixed[hs, 0:64]
                        items.append((bk.ap[hs, ns], lt[hs, ns], r_, True, True))
                mms(items, bk.reg)

            def prepA(d, sg):
                seg = sg if d == 0 else NSEG - 1 - sg
                t0 = seg * SEG
                tsl = slice(t0, t0 + SEG)
                RT, QT, Ktm, Btm, LkT, MkT, MbT = OPD3[sg % 3][d]
                KT, BT = KB[d]
                Pa, Qa = PQA[sg % 2][d]
                sgm, cum, aa, t3, t4, Ep = TMPS[d]
                pk = bank[d][:, 0:SEG]
                mms([(pk.ap, WUP(d, c), txw[0:64, tsl], True, True)], pk.reg)
                act(sgm, pk, AF.Sigmoid, bias=P(l, "w0", d * 4 + c))
                mms([(pk.ap, AUP(d, c), txw[64:128, tsl], True, True)], pk.reg)
                act(aa, pk, AF.Sigmoid, bias=P(l, "a0", d * 4 + c))
                if d == 0:
                    scan(cum, MRF[:, 0:SEG], sgm, 0.0)
                else:
                    scan(rev(cum), rev(MRB[:, 0:SEG]), rev(sgm), 0.0)
                tt(G, sgm, cum, sgm, ALU.subtract)
                act(Ep, cum, AF.Exp, scale=-DSC)
                act(cum, cum, AF.Exp, scale=DSC)
                act(sgm, sgm, AF.Exp, scale=-DSC)
                Em = cum; Ex = sgm
                c0 = 63 if d == 0 else 0
                cp(G, WC[sg % 3][d], Buf(v3(Ep)[:, :, c0], Ep.reg))
                tt(G if "rtp" in OPT else V, RT, rbf[:, tsl], Ep, ALU.mult)
                ts(V, t3, aa, P(l, "k_a", c), DR(l, 16 + c), ALU.mult, ALU.add)
                tt(G, t3, t3, kbf[:, tsl], ALU.mult)
                tt(V, KT, t3, Em, ALU.mult)
                ts(G if "t4p" in OPT else V, t4, kbf[:, tsl], P(l, "k_k", c), None, ALU.mult)
                act(t3, t4, AF.Square)
                mms([(pk.ap, BD1, t3, True, True)], pk.reg)
                act(t3, pk, AF.Ln, bias=KK_EPS)
                act(t3, t3, AF.Exp, scale=-0.5)
                tt(V, t4, t4, t3, ALU.mult)
                tt(G, QT, t4, Ex, ALU.mult)
                tt(G if "bp" in OPT else V, t4, t4, aa, ALU.mult)
                stt(BT, t4, -1.0, Em, ALU.mult, ALU.mult)
                mN, mNT, mMT = (ML, MU, MUI) if d == 0 else (MU, ML, MLI)
                def score(lt, rh, mask, dst, alt=False):
                    blockmm(lt, rh, pk)
                    if alt:
                        cp(A_, dst, pk)
                        tt(G, dst, dst, mask[:, 0:SEG], ALU.mult)
                    else:
                        tt(V, dst, pk, mask[:, 0:SEG], ALU.mult)
                score(QT, BT, mN, Pa, "mpq" in OPT)
                score(BT, QT, mNT, Qa, "mpq" in OPT)
                score(KT, QT, mNT, LkT, "mlk" in OPT)
                score(KT, RT, mMT, MkT, "mmk" in OPT)
                score(BT, RT, mMT, MbT, "mmb" in OPT)
                wc = WC[sg % 3][d]
                wbc = Buf(bass.AP(tensor=wc.ap.tensor, offset=wc.ap.offset, ap=[list(wc.ap.ap[0]), [1, CPS], [0, 64]]), wc.reg)
                for src in (KT, BT):
                    s3 = Buf(v3(src), src.reg)
                    tt(V, s3, s3, wbc, ALU.mult)
                blockmm(KT, None, pk, rhs_fixed=IDT); cp(A_, Ktm, pk)
                blockmm(BT, None, pk, rhs_fixed=IDT); cp(A_, Btm, pk)

            def prepB(d, sg):
                Pa, Qa = PQA[sg % 2][d]
                Pb, Qb, Talt = PQB[d]
                TT = TTD[sg % 2][d]
                bb = [bank[2][:, 0:SEG], bank[3][:, 0:SEG]] if d == 0 else [bank[6][:, 0:SEG], bank[7][:, 0:SEG]]
                Tc, Tn = Talt, TT
                tt(V, Tc, Qa, IDT[:, 0:SEG], ALU.add)
                Pc, Pn, Qc, Qn = Pa, Pb, Qa, Qb
                for rd_ in range(1, 6):
                    b0 = bb[rd_ % 2]; b1 = bb[(rd_ + 1) % 2]
                    blockmm(Qc, Pc, b0); cp(A_, Pn, b0)
                    if rd_ < 5:
                        blockmm(Pc, Qc, b1); cp(A_ if "qact" in OPT else V, Qn, b1)
                    blockmm(Pn, Tc, b0); tt(V, Tn, b0, Tc, ALU.add)
                    Pc, Pn = Pn, Pc; Qc, Qn = Qn, Qc; Tc, Tn = Tn, Tc
                assert Tc is TT

            XS = bank[4][:, 0:256]; UU = bank[4][:, 256:512]; Yb = [bank[5][:, 0:256], bank[5][:, 256:512]]
            def seq(sg):
                segs = [sg, NSEG - 1 - sg]
                OPS = [OPD3[sg % 3][d] + [TTD[sg % 2][d]] for d in range(2)]
                for st in range(CPS):
                    nloc = [st, CPS - 1 - st]
                    for d in range(2):
                        ds = slice(d * 64, (d + 1) * 64)
                        ts(G, tmpA[:, ds], A32[:, ds], WC[sg % 3][d][:, nloc[d]:nloc[d] + 1], None, ALU.mult)
                    items = []
                    for d in range(2):
                        RT, QT, Ktm, Btm, LkT, MkT, MbT, TT = OPS[d]
                        nl = nloc[d]; ng = segs[d] * CPS + nl
                        for hh in range(2):
                            hs = hblk(hh); ns = slice(nl * 64, (nl + 1) * 64)
                            o = XS.ap[hs, d * 64:(d + 1) * 64]
                            items.append((o, QT[hs, ns], Abf[hs, d * 64:(d + 1) * 64], True, False))
                            items.append((o, LkT[hs, ns], Vtm[hs, ng * 64:(ng + 1) * 64], False, True))
                    mms(items, XS.reg)
                    cp(A_ if "xact" in OPT else V, Xbf, XS[:, 0:128])
                    items = []
                    for d in range(2):
                        TT = OPS[d][7]; nl = nloc[d]
                        for hh in range(2):
                            hs = hblk(hh); ns = slice(nl * 64, (nl + 1) * 64)
                            items.append((UU.ap[hs, d * 64:(d + 1) * 64], TT[hs, ns], Xbf[hs, d * 64:(d + 1) * 64], True, True))
                    mms(items, UU.reg)
                    cp(A_ if "uact" in OPT else V, Ubf, UU[:, 0:128])
                    items = []
                    for d in range(2):
                        RT, QT, Ktm, Btm, LkT, MkT, MbT, TT = OPS[d]
                        nl = nloc[d]; ng = segs[d] * CPS + nl
                        for hh in range(2):
                            hs = hblk(hh); ns = slice(nl * 64, (nl + 1) * 64)
                            o = XS.ap[hs, 128 + d * 64:128 + (d + 1) * 64]
                            items.append((o, Ktm[hs, ns], Vtm[hs, ng * 64:(ng + 1) * 64], True, False))
                            items.append((o, Btm[hs, ns], Ubf[hs, d * 64:(d + 1) * 64], False, True))
                    mms(items, XS.reg)
                    for d in range(2):
                        RT, QT, Ktm, Btm, LkT, MkT, MbT, TT = OPS[d]
                        nl = nloc[d]; ng = segs[d] * CPS + nl
                        items = []
                        for hh in range(2):
                            hs = hblk(hh); ns = slice(nl * 64, (nl + 1) * 64)
                            o = Yb[d].ap[hs, ns]
                            items.append((o, Abf[hs, d * 64:(d + 1) * 64], RT[hs, ns], True, False))
                            items.append((o, Vtm[hs, ng * 64:(ng + 1) * 64], MkT[hs, ns], False, False))
                            items.append((o, Ubf[hs, d * 64:(d + 1) * 64], MbT[hs, ns], False, True))
                        mms(items, Yb[d].reg)
                    gstep = sg * CPS + st
                    bnd = (gstep + 1) % 4 == 0
                    if not bnd:
                        tt(V, Abf, XS[:, 128:256], tmpA, ALU.add)
                    tt(V, A32, XS[:, 128:256], tmpA, ALU.add)
                    if bnd:
                        q = (gstep + 1) // 4 - 1
                        sb_ = stg[q % 2]
                        cp(V, sb_, A32)
                        S.dma("sync", So_d[l, c, q], sb_.ap, reads=[sb_.reg], is_output=True)
                        ts(V, A32, A32, keepf, None, ALU.mult)
                        cp(A_, Abf, A32)
                for d in range(2):
                    t0 = segs[d] * SEG
                    dst = oacc[:, t0:t0 + SEG]
                    first = (segs[d] < NSEG - 1 - segs[d]) if d == 0 else (segs[d] > NSEG - 1 - segs[d])
                    if d == 0 and segs[d] == NSEG - 1 - segs[d]:
                        first = True
                    if first:
                        cp(A_, dst, Yb[d])
                    else:
                        tt(V, dst, Yb[d], dst, ALU.add)

            S.emit_merged([S.record(lambda: prepA(0, 0)), S.record(lambda: prepA(1, 0))])
            recs = [S.record(lambda: prepB(0, 0)), S.record(lambda: prepB(1, 0))]
            if NSEG > 1:
                recs += [S.record(lambda: prepA(0, 1)), S.record(lambda: prepA(1, 1))]
            S.emit_merged(recs)
            for sg in range(NSEG):
                recs = [S.record(lambda: seq(sg))]
                if sg + 1 < NSEG:
                    recs += [S.record(lambda: prepB(0, sg + 1)), S.record(lambda: prepB(1, sg + 1))]
                if sg + 2 < NSEG:
                    recs += [S.record(lambda: prepA(0, sg + 2)), S.record(lambda: prepA(1, sg + 2))]
                S.emit_merged(recs)
            AR.release(m_seg)
            ycur = AR.b(T)
            NPS = min(NT, 4)
            f = [[AR.f(512) for _ in range(4)] for _ in range(NPS)]
            sxgt = [AR.b(512) for _ in range(NPS)]
            wtmp = [AR.f(512), AR.f(512)]
            wv = load_w_cols(w_in, l, 2 * WA + c * 128, pool=(2, 3))
            wg = load_w_cols(w_in, l, 3 * WA + 128, pool=(2, 3))
            def post(tq):
                tsl = slice(tq * 512, (tq + 1) * 512)
                ff = f[tq % NPS]; pbk = [bank[(tq % NPS) * 2], bank[(tq % NPS) * 2 + 1]]
                o = oacc[:, tsl]
                mms([(pbk[0].ap, BD64, o, True, True)], pbk[0].reg)
                tt(V, ff[0], o, pbk[0], ALU.subtract)
                act(ff[1], ff[0], AF.Square)
                mms([(pbk[1].ap, BD64, ff[1], True, True)], pbk[1].reg)
                act(ff[1], pbk[1], AF.Ln, bias=LNX_EPS)
                act(ff[1], ff[1], AF.Exp, scale=-0.5)
                tt(V, ff[0], ff[0], ff[1], ALU.mult)
                ts(V, ff[0], ff[0], P(l, "lnxg", c), P(l, "lnxb", c), ALU.mult, ALU.add)
                for d in range(2):
                    mms([(pbk[d].ap, AUP(d, c), txw[64:128, tsl], True, True)], pbk[d].reg)
                    act(ff[1 + d], pbk[d], AF.Sigmoid, bias=P(l, "a0", d * 4 + c))
                tt(G, ff[1], ff[1], ff[2], ALU.add)
                ts(V, ff[1], ff[1], DR(l, 20 + c), DR(l, 24 + c), ALU.mult, ALU.add)
                tt(G, ff[2], rbf[:, tsl], kbf[:, tsl], ALU.mult)
                tt(V, ff[1], ff[1], ff[2], ALU.mult)
                mms([(pbk[0].ap, BD1, ff[1], True, True)], pbk[0].reg)
                cp(A_, ff[2], pbk[0])
                proj_fm(wv, pbk[1], tq)
                tt(V, ff[1], pbk[1], ff[2], ALU.mult)
                tt(G, ff[0], ff[0], ff[1], ALU.add)
                proj_fm(wg, pbk[0], tq)
                act(sxgt[tq % NPS], pbk[0], AF.Sigmoid)
                mms([(pbk[1].ap, GUP(c), sxgt[tq % NPS], True, True)], pbk[1].reg)
                tt(V, ycur[:, tsl], pbk[1], ff[0], ALU.mult)
            recs = [S.record(lambda tq=tq: post(tq)) for tq in range(NT)]
            for i in range(0, NT, NPS):
                S.emit_merged(recs[i:i + NPS])
            pending = [S.record(lambda c=c, ycur=ycur, wtmp=wtmp: wout_accum(l, c, ycur, [bank[4], bank[5], bank[6], bank[7]], wtmp))]
        S.emit_merged(pending)
        S.mark("L%d wkv" % l)

        AR.release(m_layer)
        lru_pending = []
        ycs = [AR.b(T), AR.b(T)]
        m_lru = AR.mark()
        def wout_pair(kcA, kcB, ycA, ycB, banks, tmps=None):
            wA = woutb[0]; wB = woutb[1]
            S.dma("gpsimd", wA.ap, w_out[l][kcA * 128:(kcA + 1) * 128, :], writes=[wA.reg])
            S.dma("gpsimd", wB.ap, w_out[l][kcB * 128:(kcB + 1) * 128, :], writes=[wB.reg])
            bi = 0
            for fo in range(8):
                for tq in range(NT):
                    bk = banks[bi % len(banks)]; bi += 1
                    tsl = slice(tq * 512, (tq + 1) * 512)
                    mms([(bk.ap, wA[:, fo * 128:(fo + 1) * 128], ycA[:, tsl], True, False),
                         (bk.ap, wB[:, fo * 128:(fo + 1) * 128], ycB[:, tsl], False, True)], bk.reg)
                    xres_evac(l, fo, tq, bk, bi - 1, tmps)
        for c in range(4):
            AR.release(m_lru)
            xb32 = AR.f(T); xc = AR.f(T); gA0 = AR.f(T); gI0 = AR.f(T); gI1 = AR.f(T); hF = AR.f(T); hB = AR.f(T)
            xcb = AR.b(T); ycur = ycs[c % 2]
            gAs = [gA0, xb32]; gIs = [gI0, gI1]
            def lru_common():
                wb = load_w_cols(w_in, l, 3 * WA + 256 + c * 128, pool=(0, 1))
                for tq in range(NT):
                    bk = bank[tq % 4]
                    proj_fm(wb, bk, tq)
                    cp(A_, xb32[:, tq * 512:(tq + 1) * 512], bk)
                cw = lambda j: P(l, "cw", j * 4 + c)
                act(xc, xb32, AF.Identity, scale=cw(2), bias=P(l, "cb", c))
                x3 = v3(xb32); y3 = v3(xc)
                def shifted(j, off):
                    if off < 0:
                        o_ = Buf(y3[:, :, -off:64], xc.reg); i_ = Buf(x3[:, :, 0:64 + off], xb32.reg)
                    else:
                        o_ = Buf(y3[:, :, 0:64 - off], xc.reg); i_ = Buf(x3[:, :, off:64], xb32.reg)
                    stt(o_, i_, cw(j), o_, ALU.mult, ALU.add)
                shifted(1, -1); shifted(0, -2); shifted(3, 1)
                if R64 > 1:
                    def fix(j, ocol, icol, nxt, tb):
                        if nxt:
                            o_ = Buf(y3[:, 0:R64 - 1, ocol], xc.reg); i_ = Buf(x3[:, 1:R64, icol], xb32.reg); fl_ = cfl_next[:, 0:R64 - 1]
                        else:
                            o_ = Buf(y3[:, 1:R64, ocol], xc.reg); i_ = Buf(x3[:, 0:R64 - 1, icol], xb32.reg); fl_ = cfl_prev[:, 0:R64 - 1]
                        tt(G, tb[:, 0:R64 - 1], i_, fl_, ALU.mult)
                        stt(o_, tb[:, 0:R64 - 1], cw(j), o_, ALU.mult, ALU.add)
                    fix(1, 0, 63, False, tbs[0]); fix(0, 0, 62, False, tbs[1]); fix(0, 1, 63, False, tbs[2]); fix(3, 63, 0, True, tbs[3])
                cp(G, xcb, xc)
                ld(h0t, h0_d[l])
            def lru_dir(d):
                gA = gAs[d]; gI = gIs[d]
                for gt, dst in ((0, gA), (1, gI)):
                    o = ((gt * 2 + d) * 4 + c) * 128
                    bnm = "gab" if gt == 0 else "gxb"
                    for tq in range(NT):
                        bk = bank[2 * d + tq % 2]
                        mms([(bk.ap, bdw[:, o:o + 128], xcb[:, tq * 512:(tq + 1) * 512], True, True)], bk.reg)
                        act(dst[:, tq * 512:(tq + 1) * 512], bk, AF.Sigmoid, bias=P(l, bnm, d * 4 + c))
                hD = hF if d == 0 else hB
                act(gA, gA, AF.Exp, scale=DR(l, 28 + d * 4 + c))
                tt(G, gI, gI, xc, ALU.mult)
                tt(V, hD, gA, gA, ALU.mult)
                act(hD, hD, AF.Sqrt, scale=-1.0, bias=1.0)
                tt(V if d == 0 else G, gI, gI, hD, ALU.mult)
                ini = h0t[:, d * 4 + c:d * 4 + c + 1]
                for q in (range(NSQ) if d == 0 else range(NSQ - 1, -1, -1)):
                    qs = slice(q * 256, (q + 1) * 256)
                    if d == 0:
                        scan(hD[:, qs], gA[:, qs], gI[:, qs], ini)
                        last = hD[:, q * 256 + 255:q * 256 + 256]
                    else:
                        scan(rev(hD[:, qs]), rev(gA[:, qs]), rev(gI[:, qs]), ini)
                        last = hD[:, q * 256:q * 256 + 1]
                    cp(G, hfin_d[d][:, q:q + 1], last)
                    ini = hini_d[d][:, q:q + 1]
                    ts(G, ini, last, keepf, None, ALU.mult)
                S.dma("sync", ho_d[l, c][:, d * NSQ:(d + 1) * NSQ], hfin_d[d].ap, reads=[hfin_d[d].reg], is_output=True)
            def lru_final():
                tt(V, hF, hF, hB, ALU.add)
                wb2 = load_w_cols(w_in, l, 3 * WA + 256 + WA + c * 128, pool=(0, 1))
                for tq in range(NT):
                    bk = bank[4 + tq % 3]
                    proj_fm(wb2, bk, tq)
                    act(gA0[:, tq * 512:(tq + 1) * 512], bk, AF.Gelu_apprx_tanh)
                tt(V, ycur, hF, gA0, ALU.mult)
                if c % 2 == 1:
                    wout_pair(4 + c - 1, 4 + c, ycs[0], ycs[1], [bank[4], bank[5], bank[6]], [hB[:, 0:512]] + ([hB[:, 512:1024]] if T >= 1024 else []))
            mod_pool[0] = (2, 3)
            S.emit_merged(lru_pending + [S.record(lru_common)])
            recs = [S.record(lambda: lru_dir(0)), S.record(lambda: lru_dir(1))]
            wts = [3, 3]
            if l + 1 < NL:
                recs.append(S.record(lambda: mod_cols(l + 1, c * 12, (c + 1) * 12)))
                wts.append(1)
            S.emit_merged(recs, weights=wts)
            lru_pending = [S.record(lru_final)]
        S.emit_merged(lru_pending)
        if l + 1 < NL:
            mod_fin(l + 1)

        AR.release(m_layer)
        S.mark("L%d lru" % l)
        rmsnorm_to_hbf(l, 8, 24)
        S.mark("L%d norm2" % l)
        TH = T // 2 if T >= 1024 else T
        NH = T // TH; NTH = TH // 512
        mj = [AR.b(TH) for _ in range(22)]
        accs = [[AR.f(TH), AR.f(TH)] for _ in range(2)]
        ftmp = [AR.f(TH) for _ in range(2)]
        for hf in range(NH):
            tb0 = hf * TH
            def ffn_j(j, s, wbs, nxt):
                accA, accB = accs[s]
                tmpc = ftmp[s]
                R = TH // 64
                r0 = tb0 // 64
                for ab, acc in ((0, accA), (1, accB)):
                    colc = ab * 22 + j
                    wb = wbs.pop(0)
                    if nxt:
                        wbs.append(load_w_cols(w_up, l, nxt.pop(0) * 128, pool=(2 * s, 2 * s + 1, 4 + s)))
                    fw = lambda jj, colc=colc: P(l, "fw", jj * 44 + colc)
                    b0 = s * 4 + ab * 2
                    for i in range(NTH):
                        proj_fm(wb, bank[b0 + i], tb0 // 512 + i)
                    pu2 = Buf(ps_t[:, b0 * 512:(b0 + NTH) * 512], [bank[b0 + i].reg for i in range(NTH)])
                    act(acc, pu2, AF.Identity, scale=fw(1), bias=P(l, "fb", colc))
                    p3 = v3(pu2); a3 = v3(acc); t3_ = v3(tmpc)
                    act(Buf(t3_[:, :, 0:63], tmpc.reg), Buf(p3[:, :, 0:63], pu2.reg), AF.Identity, scale=fw(0))
                    o_ = Buf(a3[:, :, 1:64], acc.reg)
                    tt(G, o_, o_, Buf(t3_[:, :, 0:63], tmpc.reg), ALU.add)
                    o_ = Buf(a3[:, :, 0:63], acc.reg); i_ = Buf(p3[:, :, 1:64], pu2.reg)
                    stt(o_, i_, fw(2), o_, ALU.mult, ALU.add)
                    tA = tbs[s * 2]; tB = tbs[s * 2 + 1]
                    tt(V, tA[:, 0:R - 1], Buf(p3[:, 0:R - 1, 63], pu2.reg), cfl_prev[:, r0:r0 + R - 1], ALU.mult)
                    o_ = Buf(a3[:, 1:R, 0], acc.reg)
                    stt(o_, tA[:, 0:R - 1], fw(0), o_, ALU.mult, ALU.add)
                    tt(V, tB[:, 0:R - 1], Buf(p3[:, 1:R, 0], pu2.reg), cfl_next[:, r0:r0 + R - 1], ALU.mult)
                    o_ = Buf(a3[:, 0:R - 1, 63], acc.reg)
                    stt(o_, tB[:, 0:R - 1], fw(2), o_, ALU.mult, ALU.add)
                act(accA, accA, AF.Gelu_apprx_tanh)
                tt(V, mj[j], accA, accB, ALU.mult)
            def ffn_stream(s):
                cols = [ab * 22 + j for j in range(s, 22, 2) for ab in (0, 1)]
                wbs = [load_w_cols(w_up, l, cols.pop(0) * 128, pool=(2 * s, 2 * s + 1, 4 + s))]
                wbs.append(load_w_cols(w_up, l, cols.pop(0) * 128, pool=(2 * s, 2 * s + 1, 4 + s)))
                for j in range(s, 22, 2):
                    ffn_j(j, s, wbs, cols)
            S.emit_merged([S.record(lambda: ffn_stream(0)), S.record(lambda: ffn_stream(1))])
            def down(fo, s):
                wks = []
                for k0 in range(0, 22, 8):
                    kn = min(8, 22 - k0)
                    wks.append((k0, kn, load_w_cols(w_down, l, fo * 128, row0=k0 * 128, nk=kn, pool=(2 * s, 2 * s + 1, 4 + s))))
                for tq in range(NTH):
                    bk = bank[s * 4 + tq % 4]
                    items = []
                    for (k0, kn, wbk) in wks:
                        for kk_ in range(kn):
                            j = k0 + kk_
                            items.append((bk.ap, wbk[:, kk_ * 128:(kk_ + 1) * 128], mj[j][:, tq * 512:(tq + 1) * 512], j == 0, j == 21))
                    mms(items, bk.reg)
                    xs = xres[fo][:, tb0 + tq * 512:tb0 + (tq + 1) * 512]
                    stt(xs, bk, MOD(l, 40 + fo), xs, ALU.mult, ALU.add)
            def down_stream(s):
                for fo in range(s, 8, 2):
                    down(fo, s)
            S.emit_merged([S.record(lambda: down_stream(0)), S.record(lambda: down_stream(1))])
        AR.release(m_layer)

    S.mark("end layers")
    LNF = ppB[:, PC["lnf"]:PC["lnf"] + 8]
    sqb = [AR.b(8 * 512) for _ in range(2)]; rs = [AR.f(512) for _ in range(2)]; obs = [AR.f(512) for _ in range(8)]
    for tq in range(NT):
        tsl = slice(tq * 512, (tq + 1) * 512)
        rms_rstd(tq, sqb[tq % 2], rs[tq % 2])
        for c in range(8):
            ob = obs[c]
            stt(ob, xres[c][:, tsl], LNF[:, c:c + 1], rs[tq % 2], ALU.mult, ALU.mult)
            S.dma("sync", yT[c][:, tsl], ob.ap, reads=[ob.reg], is_output=True)
    S.mark("final")
    build.last_sched = S
    S.finish()
    es.close()
    return nc, AR.hw


def _colmat(v, n):
    return np.ascontiguousarray(np.swapaxes(v.reshape(v.shape[:-1] + (n, 128)), -1, -2))


def _consts():
    p = np.arange(128)[:, None] % 64; f = np.arange(512)[None, :] % 64
    ml = (f < p); mu = (f > p); mli = (f <= p); mui = (f >= p); idt = (f == p)
    mrf = np.broadcast_to(f != 0, (128, 512)); mrb = np.broadcast_to(f != 63, (128, 512))
    cst = np.concatenate([ml, mu, mli, mui, idt, mrf, mrb], axis=1).astype(np.float32)
    blk = (np.arange(128)[:, None] // 64) == (np.arange(128)[None, :] // 64)
    cstf = np.concatenate([blk / 64.0, blk * 1.0], axis=1).astype(np.float32)
    return cst, cstf


def make_in_maps(inp, T, NL, roles):
    R64 = T // 64
    cst, cstf = _consts()
    L = NL
    pp = np.zeros((L, 128, NPC), np.float32)
    def put(name, arr):
        pp[:, :, PC[name]:PC[name] + arr.shape[-1]] = arr
    put("ln1g", _colmat(inp["ln1_g"][:L], 8)); put("ln2g", _colmat(inp["ln2_g"][:L], 8)); put("bmod", _colmat(inp["b_mod"][:L], 48))
    dcat = lambda a: np.concatenate([_colmat(a[:L, 0], 4), _colmat(a[:L, 1], 4)], axis=-1)
    put("w0", dcat(inp["rw_w0"])); put("a0", dcat(inp["rw_a0"]))
    put("k_k", _colmat(inp["rw_k_k"][:L], 4)); put("k_a", _colmat(inp["rw_k_a"][:L], 4))
    put("r_k", _colmat(inp["rw_r_k"][:L].reshape(L, 512), 4))
    put("lnxg", _colmat(inp["rw_lnx_g"][:L], 4)); put("lnxb", _colmat(inp["rw_lnx_b"][:L], 4))
    put("cw", np.concatenate([_colmat(inp["lru_conv_w"][:L, j], 4) for j in range(4)], axis=-1))
    put("cb", _colmat(inp["lru_conv_b"][:L], 4))
    put("gab", dcat(inp["lru_ga_b"])); put("gxb", dcat(inp["lru_gx_b"])); put("lam", dcat(inp["lru_lam"]))
    put("fw", np.concatenate([_colmat(inp["ffn_conv_w"][:L, j], 44) for j in range(3)], axis=-1))
    put("fb", _colmat(inp["ffn_conv_b"][:L], 44))
    put("lnf", np.broadcast_to(_colmat(inp["lnf_g"], 8), (L, 128, 8)))
    shared = {"pp": pp, "cst": cst, "cstf": cstf}
    for k in ("w_mod", "w_in", "w_out", "w_up", "w_down", "rw_w_up", "rw_a_up", "rw_g_up", "lru_ga_w", "lru_gx_w"):
        shared[k] = np.ascontiguousarray(inp[k][:L])
    maps = []
    for role in roles:
        m = dict(shared)
        fl = np.zeros((128, 2 + 2 * R64), np.float32)
        if role[0] == "P":
            x = np.concatenate([inp["x_prompt"][s] for s in role[1]], axis=0)
            cvec = inp["c_ctx"]
            A0 = np.zeros((L, 4, 128, 128), np.float32); h0 = np.zeros((L, 128, 8), np.float32)
            r = np.arange(R64)
            fl[:, 2:2 + R64 - 1] = ((r[1:] % 4) != 0).astype(np.float32)[None, :]
            fl[:, 2 + R64:2 + 2 * R64 - 1] = (((r[:-1] + 1) % 4) != 0).astype(np.float32)[None, :]
        else:
            b = role[1]
            x = inp["x_sample"][b]
            cvec = inp["c"][b]
            st = inp["state_rwkv"][b][:L]
            A0 = np.ascontiguousarray(st.reshape(L, 2, 4, 2, 64, 64).transpose(0, 2, 3, 5, 1, 4).reshape(L, 4, 128, 128))
            h0 = dcat(inp["state_lru"][b][None].transpose(1, 0, 2, 3).reshape(L, 1, 2, 512)[:, 0][:, None].repeat(1, 1).reshape(L, 2, 512)[:, :, :].reshape(L, 2, 512)) if False else \
                np.concatenate([_colmat(inp["state_lru"][b][:L, 0], 4), _colmat(inp["state_lru"][b][:L, 1], 4)], axis=-1)
            fl[:, 0] = 1.0
        m["xT"] = np.ascontiguousarray(x.T.reshape(8, 128, T))
        m["cv"] = _colmat(cvec, 8)
        m["A0"] = A0.astype(np.float32); m["h0"] = np.ascontiguousarray(h0.astype(np.float32)); m["fl"] = fl
        maps.append(m)
    return maps


def assemble(results, roles, T, NL, n_prompt, n_sample):
    NSQ = T // 256
    yp = [None] * n_prompt; ys = [None] * n_sample
    nr = np.zeros((n_prompt, NL, 2, 8, 64, 64), np.float32); nl_ = np.zeros((n_prompt, NL, 2, 512), np.float32)
    for res, role in zip(results, roles):
        y = np.asarray(res["yT"]).reshape(1024, T).T
        if role[0] == "P":
            So = np.asarray(res["So"]).reshape(NL, 4, NSQ, 2, 64, 2, 64)
            ho = np.asarray(res["ho"]).reshape(NL, 4, 128, 2, NSQ)
            for qi, s in enumerate(role[1]):
                yp[s] = y[qi * 256:(qi + 1) * 256]
                for d in range(2):
                    qb = qi if d == 0 else NSQ - 1 - qi
                    blk = So[:, :, qb, :, :, d, :]
                    nr[s, :, d] = blk.transpose(0, 1, 2, 4, 3).reshape(NL, 8, 64, 64)
                    nl_[s, :, d] = ho[:, :, :, d, qi].reshape(NL, 512)
        else:
            ys[role[1]] = y
    return np.stack(yp), np.stack(ys), nr, nl_


def kernel(**inputs):
    inp = {k: np.asarray(v, dtype=np.float32) for k, v in inputs.items()}
    T, NL = 2048, 4
    roles = [("P", list(range(8 * i, 8 * i + 8))) for i in range(4)] + [("S", b) for b in range(4)]
    nc, _ = build(T, NL)
    maps = make_in_maps(inp, T, NL, roles)
    res = run_bass_kernel_spmd(nc, maps, core_ids=list(range(8)))
    yp, ys, nr, nl_ = assemble(res.results, roles, T, NL, 32, 4)
    return (yp.astype(np.float32), ys.astype(np.float32), nr.astype(np.float32), nl_.astype(np.float32))
```
